# Optimizing a Trainium2 kernel written in Bass

```python
import jax, jax.numpy as jnp
from jax import lax
import numpy as np


D_MODEL = 1024
BATCH = 8
SEQ = 2048
DEPTH = 2

GRID_W = 64
HEAD_DIM = 64
N_HEADS_TOTAL = D_MODEL // HEAD_DIM
A_HEADS = N_HEADS_TOTAL // 4
B_Q_HEADS = N_HEADS_TOTAL // 2
B_KV_HEADS = B_Q_HEADS // 4
C_HEADS = N_HEADS_TOTAL // 4
A_WIDTH = A_HEADS * HEAD_DIM
B_Q_WIDTH = B_Q_HEADS * HEAD_DIM
B_KV_WIDTH = B_KV_HEADS * HEAD_DIM
C_WIDTH = C_HEADS * HEAD_DIM
N_BRANCH = 3
IN_SPLIT_SIZES = (A_WIDTH, A_WIDTH, A_WIDTH, B_Q_WIDTH, B_KV_WIDTH, B_KV_WIDTH,
                  C_WIDTH, C_WIDTH, C_WIDTH, N_BRANCH * D_MODEL)
IN_WIDTH = 3 * A_WIDTH + B_Q_WIDTH + 2 * B_KV_WIDTH + 3 * C_WIDTH + N_BRANCH * D_MODEL
A_PATTERNS = ((128, 1), (512, 4), (2048, 16))
C_WIN_ROWS = 8
C_WIN_COLS = 16
Q_BLOCK = 128
ROPE_THETA = 10000.0
D_FF = 3 * D_MODEL
CONV_WIDTH = 3
EPS = 1e-6
NEG_INF = -1e30

kernel_name = 'hybrid_gated_dilated_axial_neighborhood_encoder'


def rms_norm(x, g):
    xf = x.astype(jnp.float32)
    y = xf * lax.rsqrt(jnp.mean(xf * xf, axis=-1, keepdims=True) + EPS)
    return (y * g.astype(jnp.float32)).astype(x.dtype)


def rope_angles(pos, dim):
    inv = ROPE_THETA ** (-jnp.arange(0, dim, 2, dtype=jnp.float32) / dim)
    return pos[:, None] * inv[None, :]


def apply_rope(x, ang):
    d2 = x.shape[-1] // 2
    cos = jnp.cos(ang)[None, :, None, :]
    sin = jnp.sin(ang)[None, :, None, :]
    xf = x.astype(jnp.float32)
    x1, x2 = xf[..., :d2], xf[..., d2:]
    return jnp.concatenate([x1 * cos - x2 * sin, x1 * sin + x2 * cos], axis=-1).astype(x.dtype)


def _split_last(x, sizes):
    out, start = [], 0
    for n in sizes:
        out.append(x[..., start:start + n])
        start += n
    return out


def _dilated_band_attention(q, k, v, dil, radius):
    B, S, H, d = q.shape
    L = S // dil
    N = B * dil

    def to_sub(t):
        return t.reshape(B, L, dil, H, d).transpose(0, 2, 1, 3, 4).reshape(N, L, H, d)

    def from_sub(t):
        tail = t.shape[2:]
        t = t.reshape((B, dil, L) + tail)
        t = t.transpose((0, 2, 1) + tuple(range(3, 3 + len(tail))))
        return t.reshape((B, S) + tail)

    qs, ks, vs = to_sub(q), to_sub(k), to_sub(v)
    qb_len = min(Q_BLOCK, L)
    nblk = -(-L // qb_len)
    Lp = nblk * qb_len
    kb_len = qb_len + 2 * radius
    qs = jnp.pad(qs, ((0, 0), (0, Lp - L), (0, 0), (0, 0))).reshape(N, nblk, qb_len, H, d)
    pad_kv = ((0, 0), (radius, Lp - L + radius), (0, 0), (0, 0))
    kidx = jnp.arange(nblk)[:, None] * qb_len + jnp.arange(kb_len)[None, :]
    kg = jnp.pad(ks, pad_kv)[:, kidx]
    vg = jnp.pad(vs, pad_kv)[:, kidx]
    s = jnp.einsum('nbqhd,nbkhd->nbhqk', qs, kg, preferred_element_type=jnp.float32) * (d ** -0.5)
    key_pos = kidx - radius
    q_pos = jnp.arange(nblk)[:, None] * qb_len + jnp.arange(qb_len)[None, :]
    rel = key_pos[:, None, :] - q_pos[:, :, None]
    valid = (jnp.abs(rel) <= radius) & (key_pos[:, None, :] >= 0) & (key_pos[:, None, :] < L)
    s = jnp.where(valid[None, :, None, :, :], s, NEG_INF)
    m = jnp.max(s, axis=-1, keepdims=True)
    p = jnp.exp(s - m)
    den = jnp.sum(p, axis=-1, keepdims=True)
    lse = (m + jnp.log(den))[..., 0]
    o = jnp.einsum('nbhqk,nbkhd->nbqhd', p / den, vg)
    o = o.reshape(N, Lp, H, d)[:, :L]
    lse = lse.transpose(0, 1, 3, 2).reshape(N, Lp, H)[:, :L]
    return from_sub(o), from_sub(lse)


def dilated_mixture_attention(q, k, v):
    outs, lses = [], []
    for window, dil in A_PATTERNS:
        o, lse = _dilated_band_attention(q, k, v, dil, window // (2 * dil))
        outs.append(o)
        lses.append(lse)
    wts = jax.nn.softmax(jnp.stack(lses, axis=0), axis=0)
    out = jnp.sum(wts[..., None] * jnp.stack(outs, axis=0), axis=0)
    return out.astype(q.dtype)


def gqa_block_attention(q, k, v):
    B, S, Hq, d = q.shape
    Hkv = k.shape[2]
    G = Hq // Hkv
    nblk = S // Q_BLOCK
    qb = q.reshape(B, nblk, Q_BLOCK, Hkv, G, d).transpose(1, 0, 2, 3, 4, 5)

    def one_block(qi):
        s = jnp.einsum('bqhgd,bkhd->bhgqk', qi, k, preferred_element_type=jnp.float32) * (d ** -0.5)
        p = jax.nn.softmax(s, axis=-1)
        return jnp.einsum('bhgqk,bkhd->bqhgd', p, v).astype(q.dtype)

    o = lax.map(one_block, qb)
    return o.transpose(1, 0, 2, 3, 4, 5).reshape(B, S, Hq, d)


def neighborhood_attention_2d(q, k, v, rpb):
    B, S, H, d = q.shape
    R = S // GRID_W
    wr = min(C_WIN_ROWS, R)
    ncb = GRID_W // C_WIN_COLS
    kc_len = 2 * C_WIN_COLS
    rows = jnp.arange(R)
    cols = jnp.arange(GRID_W)
    key_rows = jnp.clip(rows - wr // 2, 0, R - wr)[:, None] + jnp.arange(wr)[None, :]
    col_start = jnp.clip(cols - C_WIN_COLS // 2, 0, GRID_W - C_WIN_COLS)
    blk_start = jnp.clip(jnp.arange(ncb) * C_WIN_COLS - C_WIN_COLS // 2, 0, GRID_W - kc_len)
    key_cols = blk_start[:, None] + jnp.arange(kc_len)[None, :]
    q_cols = cols.reshape(ncb, C_WIN_COLS)
    qg = q.reshape(B, R, ncb, C_WIN_COLS, H, d)
    kgrid = k.reshape(B, R, GRID_W, H, d)
    vgrid = v.reshape(B, R, GRID_W, H, d)
    ridx = key_rows[:, :, None, None]
    cidx = key_cols[None, None, :, :]
    kg = kgrid[:, ridx, cidx]
    vg = vgrid[:, ridx, cidx]
    s = jnp.einsum('brcqhd,bricjhd->bhrcqij', qg, kg, preferred_element_type=jnp.float32) * (d ** -0.5)
    kcol = key_cols[:, None, :]
    qstart = col_start[q_cols][:, :, None]
    col_ok = (kcol >= qstart) & (kcol < qstart + C_WIN_COLS)
    dc_idx = jnp.clip(kcol - q_cols[:, :, None], -(C_WIN_COLS - 1), C_WIN_COLS - 1) + (C_WIN_COLS - 1)
    dr_idx = key_rows - rows[:, None] + (C_WIN_ROWS - 1)
    bias = rpb[:, dr_idx[:, None, None, :, None], dc_idx[None, :, :, None, :]]
    s = s + bias[None].astype(jnp.float32)
    s = jnp.where(col_ok[None, None, None, :, :, None, :], s, NEG_INF)
    p = jax.nn.softmax(s.reshape(s.shape[:5] + (wr * kc_len,)), axis=-1).reshape(s.shape)
    o = jnp.einsum('bhrcqij,bricjhd->brcqhd', p, vg)
    return o.reshape(B, S, H, d).astype(q.dtype)


def centred_depthwise_conv(u, w, b):
    half = CONV_WIDTH // 2
    S = u.shape[1]
    up = jnp.pad(u, ((0, 0), (half, half), (0, 0)))
    out = b + up[:, 0:S] * w[0]
    for j in range(1, CONV_WIDTH):
        out = out + up[:, j:j + S] * w[j]
    return out


def hybrid_layer(x, w_in, b_gate, qk_gain, rpb, w_branch, w_out, norm_mix, norm_ffn,
                 w_up, conv_w, conv_b, w_down, ang_1d, ang_2d):
    B, S, D = x.shape
    h = rms_norm(x, norm_mix)
    qa, ka, va, qb, kb, vb, qc, kc, vc, gate_logits = _split_last(h @ w_in, IN_SPLIT_SIZES)

    def heads(t, n):
        return t.reshape(B, S, n, HEAD_DIM)

    qa = apply_rope(rms_norm(heads(qa, A_HEADS), qk_gain[0, 0]), ang_1d)
    ka = apply_rope(rms_norm(heads(ka, A_HEADS), qk_gain[0, 1]), ang_1d)
    oa = dilated_mixture_attention(qa, ka, heads(va, A_HEADS)).reshape(B, S, A_WIDTH)
    qb = apply_rope(rms_norm(heads(qb, B_Q_HEADS), qk_gain[1, 0]), ang_2d)
    kb = apply_rope(rms_norm(heads(kb, B_KV_HEADS), qk_gain[1, 1]), ang_2d)
    ob = gqa_block_attention(qb, kb, heads(vb, B_KV_HEADS)).reshape(B, S, B_Q_WIDTH)
    qc = rms_norm(heads(qc, C_HEADS), qk_gain[2, 0])
    kc = rms_norm(heads(kc, C_HEADS), qk_gain[2, 1])
    oc = neighborhood_attention_2d(qc, kc, heads(vc, C_HEADS), rpb).reshape(B, S, C_WIDTH)
    ya = oa @ w_branch[:A_WIDTH]
    yb = ob @ w_branch[A_WIDTH:A_WIDTH + B_Q_WIDTH]
    yc = oc @ w_branch[A_WIDTH + B_Q_WIDTH:]
    gates = jax.nn.sigmoid(gate_logits + b_gate).reshape(B, S, N_BRANCH, D)
    merged = gates[:, :, 0] * ya + gates[:, :, 1] * yb + gates[:, :, 2] * yc
    x = x + merged @ w_out
    h = rms_norm(x, norm_ffn)
    u = centred_depthwise_conv(h @ w_up, conv_w, conv_b)
    gate, val = u[..., :D_FF], u[..., D_FF:]
    return x + (jax.nn.gelu(gate, approximate=True) * val) @ w_down


def setup_inputs(seed: int = 0) -> dict:
    key = jax.random.key(seed)
    ks = jax.random.split(key, 16)
    f32 = jnp.float32

    def nrm(k, shape, scale):
        return jax.random.normal(k, shape, f32) * scale

    kb = jax.random.split(ks[5], 3)
    w_branch = jnp.concatenate([
        nrm(kb[0], (DEPTH, A_WIDTH, D_MODEL), A_WIDTH ** -0.5),
        nrm(kb[1], (DEPTH, B_Q_WIDTH, D_MODEL), B_Q_WIDTH ** -0.5),
        nrm(kb[2], (DEPTH, C_WIDTH, D_MODEL), C_WIDTH ** -0.5)], axis=1)
    return {
        'x': nrm(ks[0], (BATCH, SEQ, D_MODEL), 1.0),
        'w_in': nrm(ks[1], (DEPTH, D_MODEL, IN_WIDTH), D_MODEL ** -0.5),
        'b_gate': nrm(ks[2], (DEPTH, N_BRANCH * D_MODEL), 0.1),
        'qk_gain': 1.0 + nrm(ks[3], (DEPTH, N_BRANCH, 2, HEAD_DIM), 0.05),
        'rel_pos_bias': nrm(ks[4], (DEPTH, C_HEADS, 2 * C_WIN_ROWS - 1, 2 * C_WIN_COLS - 1), 0.5),
        'w_branch': w_branch,
        'w_out': nrm(ks[6], (DEPTH, D_MODEL, D_MODEL), D_MODEL ** -0.5),
        'norm_mix': 1.0 + nrm(ks[7], (DEPTH, D_MODEL), 0.05),
        'norm_ffn': 1.0 + nrm(ks[8], (DEPTH, D_MODEL), 0.05),
        'w_up': nrm(ks[9], (DEPTH, D_MODEL, 2 * D_FF), D_MODEL ** -0.5),
        'conv_w': nrm(ks[10], (DEPTH, CONV_WIDTH, 2 * D_FF), CONV_WIDTH ** -0.5),
        'conv_b': nrm(ks[11], (DEPTH, 2 * D_FF), 0.02),
        'w_down': nrm(ks[12], (DEPTH, D_FF, D_MODEL), D_FF ** -0.5),
    }


def reference(x, w_in, b_gate, qk_gain, rel_pos_bias, w_branch, w_out, norm_mix, norm_ffn,
              w_up, conv_w, conv_b, w_down):
    S = x.shape[1]
    t = jnp.arange(S)
    ang_1d = rope_angles(t.astype(jnp.float32), HEAD_DIM)
    ang_2d = jnp.concatenate([
        rope_angles((t // GRID_W).astype(jnp.float32), HEAD_DIM // 2),
        rope_angles((t % GRID_W).astype(jnp.float32), HEAD_DIM // 2)],
        axis=-1)
    for l in range(DEPTH):
        x = hybrid_layer(x, w_in[l], b_gate[l], qk_gain[l], rel_pos_bias[l], w_branch[l],
                         w_out[l], norm_mix[l], norm_ffn[l], w_up[l], conv_w[l], conv_b[l],
                         w_down[l], ang_1d, ang_2d)
    return x
```

```python
import numpy as np
import concourse.bass as bass
import concourse.mybir as mybir
from concourse.bass_utils import run_bass_kernel_spmd

F32 = mybir.dt.float32
BF16 = mybir.dt.bfloat16
ALU = mybir.AluOpType
AF = mybir.ActivationFunctionType
DT_SIZE = {F32: 4, BF16: 2}


def _dsize(dt):
    return DT_SIZE[dt]


class Sched:
    DMA_ROT = 6

    def __init__(self, nc):
        self.nc = nc
        self.ops = []
        self.acc = {}
        self.eng_ops = {e: [] for e in ('pe', 'act', 'dve', 'pool', 'sp')}

    @staticmethod
    def region(ap, whole=False):
        t = ap.tensor
        name = t.name
        space = str(ap.space)
        if 'DRAM' in space.upper() or 'HBM' in space.upper() or type(t).__name__.startswith('DRam'):
            return (name, 0, 1 << 30, 0, 1 << 40, 'dram')
        pat = ap.ap
        pstep, pcnt = pat[0]
        off = int(ap.offset)
        es = _dsize(ap.dtype)
        if pstep == 0:
            p0, fo = 0, off
            pcnt = 1
        else:
            p0, fo = divmod(off, pstep)
        ext = 0
        for st, cnt in pat[1:]:
            ext += abs(st) * (cnt - 1)
        b0 = fo * es
        b1 = (fo + ext + 1) * es
        kind = 'psum' if 'PSUM' in space.upper() or type(t).__name__.startswith('PSum') else 'sbuf'
        if kind == 'psum':
            return (name, 0, 128, (b0 // 2048) * 2048, ((b1 + 2047) // 2048) * 2048, kind)
        return (name, p0, p0 + pcnt, b0, b1, kind)

    def op(self, eng, fn, reads=(), writes=(), dma=False):
        opid = len(self.ops)
        deps = set()
        regs = [(self.region(a), False) for a in reads] + [(self.region(a), True) for a in writes]
        for (name, p0, p1, b0, b1, kind), is_w in regs:
            if kind == 'dram':
                continue
            lst = self.acc.setdefault(name, [])
            w = is_w or kind == 'psum'
            for e in lst:
                if e[0] < p1 and p0 < e[1] and e[2] < b1 and b0 < e[3] and (w or e[5]):
                    if e[4] != opid:
                        deps.add(e[4])
        for (name, p0, p1, b0, b1, kind), is_w in regs:
            if kind == 'dram':
                continue
            lst = self.acc[name]
            w = is_w or kind == 'psum'
            if w:
                lst[:] = [e for e in lst if not (p0 <= e[0] and e[1] <= p1 and b0 <= e[2] and e[3] <= b1)]
                lst.append([p0, p1, b0, b1, opid, True, eng])
            else:
                rep = False
                if not dma:
                    for e in lst:
                        if (not e[5]) and e[6] == eng and e[0] == p0 and e[1] == p1 and e[2] == b0 and e[3] == b1 \
                                and not self.ops[e[4]]['dma']:
                            e[4] = opid
                            rep = True
                            break
                if not rep:
                    lst.append([p0, p1, b0, b1, opid, False, eng])
        if eng == 'pe':
            deps = {d for d in deps if not (self.ops[d]['eng'] == 'pe' and not self.ops[d]['dma'])}
        self.ops.append(dict(eng=eng, fn=fn, deps=deps, dma=dma))
        self.eng_ops[eng].append(opid)
        return opid

    def emit(self, block_engs, sems, dma_sems):
        ops = self.ops
        dma_idx = {}
        cnt = {e: 0 for e in self.eng_ops}
        for i, o in enumerate(ops):
            if o['dma']:
                dma_idx[i] = cnt[o['eng']]
                cnt[o['eng']] += 1
        R = self.DMA_ROT
        needed = set()
        for i, o in enumerate(ops):
            for d in o['deps']:
                if not ops[d]['dma']:
                    needed.add(d)
        signo = {}
        c = {e: 0 for e in self.eng_ops}
        for i, o in enumerate(ops):
            if (not o['dma']) and i in needed:
                c[o['eng']] += 1
                signo[i] = c[o['eng']]
        self.signo = signo
        self.dma_idx = dma_idx
        self.n_waits = 0

    def emit_engine(self, eng, engobj, sems, dma_sems):
        ops = self.ops
        R = self.DMA_ROT
        waited = {}

        def wait(key, sem, val):
            if waited.get(key, 0) >= val:
                return
            waited[key] = val
            engobj.wait_ge(sem, val)
            self.n_waits += 1

        for i in self.eng_ops[eng]:
            o = ops[i]
            for d in sorted(o['deps']):
                od = ops[d]
                if od['dma']:
                    k = self.dma_idx[d]
                    wait(('d', od['eng'], k % R), dma_sems[od['eng']][k % R], 16 * (k // R + 1))
                else:
                    wait(('c', od['eng']), sems[od['eng']], self.signo[d])
            if o['dma']:
                k = self.dma_idx[i]
                if k >= R:
                    wait(('d', eng, k % R), dma_sems[eng][k % R], 16 * (k // R))
                ins = o['fn'](engobj)
                ins.then_inc(dma_sems[eng][k % R], 16)
            else:
                ins = o['fn'](engobj)
                if i in self.signo:
                    ins.then_inc(sems[eng], 1)

    def final_waits(self, eng, engobj, sems, dma_sems, opids):
        R = self.DMA_ROT
        for d in opids:
            od = self.ops[d]
            if od['dma']:
                k = self.dma_idx[d]
                engobj.wait_ge(dma_sems[od['eng']][k % R], 16 * (k // R + 1))
            else:
                engobj.wait_ge(sems[od['eng']], self.signo[d])


D = 1024
SEQ = 2048
NL = 2
IN_W = 5376
A_Q, A_K, A_V = 0, 256, 512
B_Q, B_K, B_V = 768, 1280, 1408
C_Q, C_K, C_V = 1536, 1792, 2048
GATE0 = 2304
DFF = 3072
EPS = 1e-6
NEG = -30000.0
VW = 64
ARENA_BYTES = 104 * 1024
STG_OFF = 96 * 1024
NU_INT, NU_FULL = 22, 14


def _param_layout():
    off = {}
    n = 0
    for name, cnt in (('normg', NL * 2 * 8), ('bgate', NL * 24), ('convw', NL * 3 * 48), ('convb', NL * 48),
                      ('qkg', NL * 6), ('eps', 1)):
        off[name] = n
        n += cnt
    return off, n


POFF, NPAR = _param_layout()


def _pack_params(b_gate, qk_gain, norm_mix, norm_ffn, conv_w, conv_b):
    P = np.zeros((128, NPAR), np.float32)
    for l in range(NL):
        P[:, POFF['normg'] + (l * 2 + 0) * 8:POFF['normg'] + (l * 2 + 0) * 8 + 8] = norm_mix[l].reshape(8, 128).T
        P[:, POFF['normg'] + (l * 2 + 1) * 8:POFF['normg'] + (l * 2 + 1) * 8 + 8] = norm_ffn[l].reshape(8, 128).T
        P[:, POFF['bgate'] + l * 24:POFF['bgate'] + l * 24 + 24] = b_gate[l].reshape(24, 128).T
        for j in range(3):
            o = POFF['convw'] + (l * 3 + j) * 48
            P[:, o:o + 48] = conv_w[l, j].reshape(48, 128).T
        o = POFF['convb'] + l * 48
        P[:, o:o + 48] = conv_b[l].reshape(48, 128).T
        for br in range(3):
            for qk in range(2):
                P[:, POFF['qkg'] + l * 6 + br * 2 + qk] = np.tile(qk_gain[l, br, qk], 2)
    P[:, POFF['eps']] = EPS
    return P


def _rope_tables():
    t = np.arange(SEQ)

    def ang(pos, dim):
        inv = (np.float32(10000.0) ** (-np.arange(0, dim, 2, dtype=np.float32) / np.float32(dim))).astype(np.float32)
        return (pos.astype(np.float32)[:, None] * inv[None, :]).astype(np.float32)

    a1 = ang(t, 64)
    a2 = np.concatenate([ang(t // 64, 32), ang(t % 64, 32)], axis=-1)
    out = []
    for a in (a1, a2):
        idx = (np.arange(128) % 64) % 32
        out.append(np.ascontiguousarray(np.cos(a).astype(np.float32)[:, idx].T))
        out.append(np.ascontiguousarray(np.sin(a).astype(np.float32)[:, idx].T))
    return np.stack(out, 0)


def _const_mats():
    M = np.zeros((5, 128, 128), np.float32)
    M[4, 0, 0:64] = 1.0
    M[4, 32, 64:128] = 1.0
    M[0] = np.eye(128, dtype=np.float32)
    M[1] = 1.0
    M[2, :64, :64] = 1.0
    M[2, 64:, 64:] = 1.0
    for d in range(128):
        if d % 64 < 32:
            M[3, d + 32, d] = -1.0
        else:
            M[3, d - 32, d] = 1.0
    return M


def _band_mask():
    kk = np.arange(128)[:, None]
    qq = np.arange(256)[None, :]
    return np.where((kk <= qq) & (kk >= qq - 128), 0.0, 8.0 * NEG).astype(np.float32)


def _bias_tables(rpb):
    a = (np.arange(128) // 64)[:, None, None]
    cp = (np.arange(128) % 64)[:, None, None]
    c = np.arange(64)[None, None, :]
    cs = np.clip(c - 8, 0, 48)
    col_ok = (cp >= cs) & (cp < cs + 16)
    dc = np.clip(cp - c, -15, 15) + 15
    outs = []
    for (u_lo, nu, lo, hi) in ((-10, NU_INT, -4, 3), (-6, NU_FULL, -7, 7)):
        u = (u_lo + np.arange(nu))[None, :, None]
        dr = a - u
        ok = (dr >= lo) & (dr <= hi) & col_ok
        dri = np.clip(dr + 7, 0, 14)
        g = rpb[:, :, dri, dc]
        g = np.where(ok[None, None], g, np.float32(NEG)).astype(np.float32)
        outs.append(g.reshape(NL, 4, 128, nu * 64))
    return np.ascontiguousarray(np.concatenate(outs, axis=-1))


class Prog:
    def __init__(self, n_layers, stages=None):
        from contextlib import ExitStack
        self.nl = n_layers
        self.stages = stages
        nc = self.nc = bass.Bass("TRN2", target_bir_lowering=False)
        L = n_layers
        dr = lambda name, shape, kind="ExternalInput": nc.dram_tensor(name, shape, F32, kind=kind).ap()
        self.x = dr("x", [SEQ, D])
        self.w_in = dr("w_in", [L, 44, 128, 1024])
        self.w_br = dr("w_branch", [L, 8, 128, 1024])
        self.w_out = dr("w_out", [L, 8, 128, 1024])
        self.w_up = dr("w_up", [L, 48, 128, 1024])
        self.w_down = dr("w_down", [L, 8, 128, 24 * 128])
        self.params = dr("params", [128, NPAR])
        self.rope = dr("rope", [4, 128, SEQ])
        self.cmats = dr("cmats", [5, 128, 128])
        self.band = dr("band", [128, 256])
        self.gtab = dr("gtab", [L, 4, 128, (NU_INT + NU_FULL) * 64])
        self.out = dr("out", [SEQ, D], kind="ExternalOutput")
        self.st = ExitStack()
        E = self.st.enter_context
        self.xT = E(nc.sbuf_tensor("xT", [128, 8, SEQ], F32))
        self.hT = E(nc.sbuf_tensor("hT", [128, 8, SEQ], BF16))
        self.arena = E(nc.sbuf_tensor("arena", [128, ARENA_BYTES // 4], F32))
        self.par = E(nc.sbuf_tensor("par", [128, NPAR], F32))
        self.cm = E(nc.sbuf_tensor("cm", [128, 5, 128], F32))
        self.cmb = E(nc.sbuf_tensor("cmb", [128, 4, 128], BF16))
        self.bandm = E(nc.sbuf_tensor("bandm", [128, 256], BF16))
        self.pp = [E(nc.psum_tensor(f"pp{i}", [128, 1024], F32)) for i in range(4)]
        self.ps = [self.pp[i // 2][:, (i % 2) * 512:(i % 2) * 512 + 512] for i in range(8)]
        self.sems = {e: E(nc.semaphore(f"s_{e}")) for e in ('pe', 'act', 'dve', 'pool', 'sp')}
        self.dsems = {e: [E(nc.semaphore(f"d_{e}{i}")) for i in range(Sched.DMA_ROT)] for e in ('sp', 'pool')}
        self.stg = [self.arena[:, (STG_OFF + 4096 * i) // 4:(STG_OFF + 4096 * (i + 1)) // 4] for i in range(2)]
        self.nstg = 0
        self.ncast = 0
        self.lq = []
        self.inflight = []
        self.S = Sched(nc)
        self.apos = 0
        self.rr = 0

    def alloc(self, shape, dt):
        n = int(np.prod(shape)) * _dsize(dt)
        n = (n + 63) // 64 * 64
        o = self.apos
        assert o + n <= STG_OFF, (o, n)
        self.apos = o + n
        v = self.arena[:, o // 4:(o + n) // 4]
        if dt != F32:
            v = v.bitcast(dt)
        v = v[:, 0:int(np.prod(shape))]
        if len(shape) == 2:
            v = v.rearrange("p (a b) -> p a b", a=shape[0])
        elif len(shape) == 3:
            v = v.rearrange("p (a b c) -> p a b c", a=shape[0], b=shape[1])
        elif len(shape) == 4:
            v = v.rearrange("p (a b c d) -> p a b c d", a=shape[0], b=shape[1], c=shape[2])
        return v

    def mm(self, out, lhsT, rhs, start=True, stop=True, tp=None):
        kw = {} if tp is None else dict(tile_position=tp)
        return self.S.op('pe', lambda e: e.matmul(out, lhsT=lhsT, rhs=rhs, start=start, stop=stop, **kw),
                         reads=[lhsT, rhs], writes=[out])

    def tr(self, out, in_, ident):
        return self.S.op('pe', lambda e: e.transpose(out, in_, ident), reads=[in_, ident], writes=[out])

    def act(self, out, in_, func, bias=None, scale=None):
        kw = {}
        rd = [in_]
        if bias is not None:
            kw['bias'] = bias
            if not isinstance(bias, float):
                rd.append(bias)
        if scale is not None:
            kw['scale'] = scale
            if not isinstance(scale, float):
                rd.append(scale)
        return self.S.op('act', lambda e: e.activation(out=out, in_=in_, func=func, **kw), reads=rd, writes=[out])

    def tt(self, eng, out, in0, in1, op):
        return self.S.op(eng, lambda e: e.tensor_tensor(out=out, in0=in0, in1=in1, op=op), reads=[in0, in1], writes=[out])

    def stt(self, eng, out, in0, scalar, in1, op0, op1):
        rd = [in0, in1] + ([] if isinstance(scalar, float) else [scalar])
        return self.S.op(eng, lambda e: e.scalar_tensor_tensor(out=out, in0=in0, scalar=scalar, in1=in1, op0=op0, op1=op1),
                         reads=rd, writes=[out])

    def cp(self, eng, out, in_):
        if eng == 'act':
            return self.act(out, in_, AF.Copy)
        return self.S.op(eng, lambda e: e.tensor_copy(out=out, in_=in_), reads=[in_], writes=[out])

    def memset(self, eng, out, val):
        return self.S.op(eng, lambda e: e.memset(out, val), writes=[out])

    def recip(self, out, in_):
        return self.S.op('dve', lambda e: e.reciprocal(out=out, in_=in_), reads=[in_], writes=[out])

    def dma(self, eng, out, in_):
        return self.S.op(eng, lambda e: e.dma_start(out=out, in_=in_), reads=[in_], writes=[out], dma=True)

    def pcol(self, name, idx):
        o = POFF[name] + idx
        return self.par[:, o:o + 1]

    def alt(self):
        self.rr += 1
        return 'dve' if self.rr % 2 else 'pool'

    def load_consts(self):
        self.dma('sp', self.par[:], self.params[:, :])
        self.dma('sp', self.cm[:], self.cmats.rearrange("m p n -> p m n"))
        self.cp('dve', self.cmb[:], self.cm[:, 0:4, :])
        self.dma('sp', self.stg[0][:, 0:256], self.band[:, :])
        self.cp('dve', self.bandm[:], self.stg[0][:, 0:256])
        self.ident = self.cm[:, 0, :]
        self.ones_f = self.cm[:, 1, :]
        self.perm_f = self.cm[:, 3, :]
        self.sel_f = self.cm[:, 4, :]
        self.ident_b = self.cmb[:, 0, :]
        self.ones_b = self.cmb[:, 1, :]
        self.bones_b = self.cmb[:, 2, :]
        self.perm_b = self.cmb[:, 3, :]

    def load_x_blk(self, blk, xt):
        for t in range(4 * blk, 4 * blk + 4):
            b = xt[t % len(xt)]
            self.dma('sp', b, self.x[t * 128:(t + 1) * 128, :])
            for half in range(2):
                p = self.ps[4 + (2 * t + half) % 4]
                for j in range(4):
                    c = 4 * half + j
                    self.tr(p[:, j * 128:(j + 1) * 128], b[:, c * 128:(c + 1) * 128], self.ident)
                self.cp('act' if half else 'dve', self.xT[:, 4 * half:4 * half + 4, t * 128:(t + 1) * 128],
                        p[:, :].rearrange("p (j n) -> p j n", j=4))

    def store_tile(self, t, b, banks):
        for half in range(2):
            p = self.ps[banks[half]]
            for j in range(4):
                c = 4 * half + j
                self.tr(p[:, j * 128:(j + 1) * 128], self.xT[:, c, t * 128:(t + 1) * 128], self.ident)
            self.cp('act' if half else 'dve', b[:, 512 * half:512 * half + 512], p[:, :])
        return self.dma('sp', self.out[t * 128:(t + 1) * 128, :], b)

    def store_x(self, tiles):
        self.apos = 0
        ot = [self.alloc([D], F32) for _ in range(2)]
        last = []
        for t in tiles:
            last.append(self.store_tile(t, ot[t % 2], [(2 * t) % 4, (2 * t + 1) % 4]))
        return last

    def norm(self, l, which, work, blks=(0, 1, 2, 3)):
        sq, rstd = work
        for blk in blks:
            bs = slice(blk * 512, (blk + 1) * 512)
            pn = self.ps[blk % 2]
            for c in range(8):
                s = sq[c % 2]
                self.act(s, self.xT[:, c, bs], AF.Square)
                self.mm(pn[:, :], self.ones_b, s, start=(c == 0), stop=(c == 7))
            r = rstd[blk % 2]
            self.act(r, pn[:, :], AF.Ln, bias=self.pcol('eps', 0), scale=1.0 / D)
            self.act(r, r, AF.Exp, scale=-0.5)
            for c in range(8):
                self.stt('dve', self.hT[:, c, bs], self.xT[:, c, bs],
                         self.pcol('normg', (l * 2 + which) * 8 + c), r, ALU.mult, ALU.mult)

    def slab_load(self, dst, src, ceng=None):
        d2 = dst.rearrange("p k n -> p (k n)")
        n = d2.shape[1]
        for o in range(0, n, 1024):
            self.lq.append((d2[:, o:o + 1024], src[:, o:o + 1024], ceng))

    def pump(self):
        for (st, d, ceng) in self.inflight:
            self.ncast += 1
            eng = ceng if ceng is not None else ('act' if self.ncast % 2 else 'dve')
            self.cp(eng, d, st)
        self.inflight = []
        while self.lq and len(self.inflight) < 2:
            d, src, ceng = self.lq.pop(0)
            st = self.stg[self.nstg % 2]
            self.nstg += 1
            self.dma('sp', st, src)
            self.inflight.append((st, d, ceng))

    def drain(self):
        while self.lq or self.inflight:
            self.pump()

    def prep_qk(self, l, specs, tabs, work, slabs, gidx, vnext=None):
        w_in = self.w_in
        units = []
        for ci, (dst, cols, qk) in enumerate(specs):
            for blk in range(4):
                units.append((ci, dst, cols, qk, blk))

        def stage1(u):
            ci, dst, cols, qk, blk = units[u]
            sl = slabs[ci % 2]
            if blk == 0:
                if ci == 0:
                    self.slab_load(sl, w_in[l, cols])
                self.drain()
                if ci + 1 < len(specs):
                    self.slab_load(slabs[(ci + 1) % 2], w_in[l, specs[ci + 1][1]])
                elif vnext is not None:
                    self.slab_load(vnext[0], w_in[l, vnext[1]])
            self.pump()
            gain = self.pcol('qkg', l * 6 + gidx * 2 + qk)
            sqb, rstdb, qnb, t1b, t2b = work[u % 2]
            bs = slice(blk * 512, (blk + 1) * 512)
            qp = self.ps[u % 2]
            for kc in range(8):
                self.mm(qp[:, :], sl[:, kc, :], self.hT[:, kc, bs], start=(kc == 0), stop=(kc == 7))
            self.act(sqb, qp[:, :], AF.Square)
            sp_ = self.ps[2 + u % 2]
            self.mm(sp_[:, :], self.bones_b, sqb)
            self.act(rstdb, sp_[:, :], AF.Ln, bias=self.pcol('eps', 0), scale=1.0 / 64)
            self.act(rstdb, rstdb, AF.Exp, scale=-0.5)
            if tabs is None:
                self.stt('dve', dst[:, bs], qp[:, :], gain, rstdb, ALU.mult, ALU.mult)
            else:
                cos, sin = tabs
                self.stt('dve', qnb, qp[:, :], gain, rstdb, ALU.mult, ALU.mult)
                self.tt('dve', t1b, qnb, cos[:, bs], ALU.mult)
                self.tt('dve', t2b, qnb, sin[:, bs], ALU.mult)

        def stage2(u):
            ci, dst, cols, qk, blk = units[u]
            if tabs is None:
                return
            sqb, rstdb, qnb, t1b, t2b = work[u % 2]
            bs = slice(blk * 512, (blk + 1) * 512)
            rp = self.ps[4 + u % 2]
            self.mm(rp[:, :], self.ident_b, t1b, start=True, stop=False)
            self.mm(rp[:, :], self.perm_b, t2b, start=False, stop=True)
            self.cp('act', dst[:, bs], rp[:, :])

        stage1(0)
        for u in range(len(units)):
            if u + 1 < len(units):
                stage1(u + 1)
            stage2(u)

    def calc_vt(self, slab, VT):
        self.drain()
        for blk in range(4):
            bs = slice(blk * 512, (blk + 1) * 512)
            vp = self.ps[4 + blk % 2]
            for kc in range(8):
                self.mm(vp[:, :], slab[:, kc, :], self.hT[:, kc, bs], start=(kc == 0), stop=(kc == 7))
            self.cp('dve' if blk % 2 else 'act', VT[:, bs], vp[:, :])

    def v_tiles(self, VT, vdst4_fn, tok_fn, nkt=16, split=None):
        pbf = self.pp[3].bitcast(BF16)
        for k4 in range(nkt // 4):
            pb = pbf[:, (k4 % 2) * 1024:(k4 % 2) * 1024 + 512]
            for j in range(4):
                self.tr(pb[:, j * 128:(j + 1) * 128], VT[:, tok_fn(4 * k4 + j)], self.ident_b)
            src = pb.rearrange("p (j n) -> p j n", j=4) if split is None else \
                pb.rearrange("p (j g d) -> p j g d", j=4, g=split)
            self.cp('dve' if k4 % 2 else 'act', vdst4_fn(k4), src)

    def finalize(self, o_src_num, o_src_den, dst, osb_den_row, nq):
        rec = osb_den_row
        self.act(rec, o_src_den, AF.Ln)
        self.act(rec, rec, AF.Exp, scale=-1.0)
        bp = self.ps[7]
        self.mm(bp[0:64, 0:nq], self.ones_f[64:65, 0:64], rec)
        self.tt('dve', dst, o_src_num, bp[0:64, 0:nq], ALU.mult)

    def branch_b(self, l, oT):
        VB = 72
        self.apos = 32768
        V = self.alloc([16, 2, VB], BF16)
        VT = self.alloc([SEQ], BF16)
        cos = self.alloc([SEQ], F32)
        sin = self.alloc([SEQ], F32)
        self.dma('sp', cos, self.rope[2])
        self.dma('sp', sin, self.rope[3])
        base = self.apos
        for g in range(2):
            self.apos = base
            qT = self.alloc([2, SEQ], BF16)
            kT = self.alloc([SEQ], BF16)
            mark = self.apos
            work = [(self.alloc([512], BF16), self.alloc([512], F32), self.alloc([512], F32), self.alloc([512], BF16),
                     self.alloc([512], BF16)) for _ in range(2)]
            slabs = [self.alloc([8, 128], BF16) for _ in range(2)]
            specs = [(qT[:, 0, :], 6 + 2 * g, 0),
                     (qT[:, 1, :], 7 + 2 * g, 0),
                     (kT, 42 + g, 1)]
            self.prep_qk(l, specs, (cos, sin), work, slabs, 1, vnext=((slabs[1], 11) if g == 0 else None))
            if g == 0:
                self.calc_vt(slabs[1], VT)
                self.memset('dve', V[:, :, :, 64:65], 1.0)
                self.v_tiles(VT, lambda k4: V[:, 4 * k4:4 * k4 + 4, :, 0:64],
                             lambda kt: slice(kt * 128, (kt + 1) * 128), split=2)
            self.apos = mark
            NP = 3
            P = [self.alloc([1024], BF16) for _ in range(NP)]
            osb = [self.alloc([512], F32) for _ in range(2)]
            rec = [self.alloc([1024], F32) for _ in range(2)]
            tiles = [(hp2, qc, kt) for hp2 in range(2) for qc in range(4) for kt in range(16)]

            def s_mm(i):
                hp2, qc, kt = tiles[i]
                sp_ = self.pp[i % 2]
                for e in range(2):
                    self.mm(sp_[:, 512 * e:512 * e + 512], kT[64 * e:64 * e + 64, kt * 128:(kt + 1) * 128],
                            qT[64 * e:64 * e + 64, hp2, qc * 512:(qc + 1) * 512])

            pending = None
            s_mm(0)
            for i, (hp2, qc, kt) in enumerate(tiles):
                if i + 1 < len(tiles):
                    s_mm(i + 1)
                Pt = P[i % NP]
                self.act(Pt, self.pp[i % 2][:, :], AF.Exp, scale=0.125)
                j = hp2 * 4 + qc
                ob = self.pp[2 + j % 2]
                for e in range(2):
                    self.mm(ob[0:65, 512 * e:512 * e + 512], V[:, kt, g, 0:65], Pt[:, 512 * e:512 * e + 512],
                            start=(kt == 0), stop=(kt == 15))
                if pending is not None and i - pending[0] >= 3:
                    pending[1]()
                    pending = None
                if kt == 15:
                    def fin(ob=ob, k=j % 2, hp2=hp2, qc=qc):
                        r_ = rec[k]
                        for e in range(2):
                            self.act(r_[64:65, 512 * e:512 * e + 512], ob[64:65, 512 * e:512 * e + 512], AF.Ln)
                        self.act(r_[64:65, :], r_[64:65, :], AF.Exp, scale=-1.0)
                        for e in range(2):
                            self.cp('dve', osb[k][64 * e:64 * e + 64, :], ob[0:64, 512 * e:512 * e + 512])
                        for e in range(2):
                            self.mm(ob[64 * e:64 * e + 64, 0:512], self.ones_f[64:65, 0:64], r_[64:65, 512 * e:512 * e + 512],
                                    tp=(64, 64 * e))
                        self.tt('dve', oT[:, 2 + 2 * g + hp2, qc * 512:(qc + 1) * 512], osb[k], ob[:, 0:512], ALU.mult)
                    pending = (i, fin)
            if pending is not None:
                pending[1]()

    def branch_a(self, l, oT):
        self.apos = 32768
        cos = self.alloc([SEQ], F32)
        sin = self.alloc([SEQ], F32)
        self.dma('sp', cos, self.rope[0])
        self.dma('sp', sin, self.rope[1])
        base = self.apos
        for hp in range(2):
            self.apos = base
            qT = self.alloc([SEQ], BF16)
            kT = self.alloc([SEQ], BF16)
            V = self.alloc([3, 16, 2, VW], BF16)
            VT = self.alloc([SEQ], BF16)
            mark = self.apos
            slabs = [self.alloc([8, 128], BF16) for _ in range(2)]
            work = [(self.alloc([512], BF16), self.alloc([512], F32), self.alloc([512], F32), self.alloc([512], BF16),
                     self.alloc([512], BF16)) for _ in range(2)]
            specs = [(qT, hp, 0), (kT, 2 + hp, 1)]
            self.prep_qk(l, specs, (cos, sin), work, slabs, 0, vnext=(slabs[0], 4 + hp))
            self.calc_vt(slabs[0], VT)
            pats = [(1, 64), (4, 64), (16, 64)]
            for p, (dil, rad) in enumerate(pats):
                nts = 16 // dil

                def tok(kt, dil=dil, nts=nts):
                    r, j = kt // nts, kt % nts
                    s0 = r + dil * 128 * j
                    return slice(s0, s0 + dil * 127 + 1, dil)

                self.v_tiles(VT, lambda k4, p=p: V[:, p, 4 * k4:4 * k4 + 4, :, :].rearrange("p k e d -> p k (e d)"), tok)
            self.apos = mark
            oacc = self.alloc([SEQ], F32)
            dacc = self.alloc([SEQ], F32)
            NP = 3
            P = [self.alloc([2, 256], BF16) for _ in range(NP)]
            self.memset('dve', oacc, 0.0)
            self.memset('dve', dacc[0:33, :], 1.0)
            self.memset('dve', dacc[0:1, :], 0.0)
            self.memset('dve', dacc[32:33, :], 0.0)
            for k in range(2):
                self.memset('dve', self.pp[2 + k][0:33, 512:1024], 0.0)
            tl = []
            for p, (dil, rad) in enumerate(pats):
                Lp = SEQ // dil
                nts = 16 // dil
                for kt in range(16):
                    r, j = kt // nts, kt % nts
                    ql0 = max(0, 128 * j - 64)
                    ql1 = min(Lp, 128 * j + 192)
                    nq = ql1 - ql0
                    mo = ql0 - (128 * j - 64)
                    ks0 = r + dil * 128 * j
                    ksl = slice(ks0, ks0 + dil * 127 + 1, dil)
                    qs0 = r + dil * ql0
                    qsl = slice(qs0, qs0 + dil * (nq - 1) + 1, dil)
                    tl.append((p, kt, nq, mo, ksl, qsl))

            def s_stage(i):
                p, kt, nq, mo, ksl, qsl = tl[i]
                sb = self.pp[i % 2]
                for e in range(2):
                    self.mm(sb[:, 512 * e:512 * e + nq], kT[64 * e:64 * e + 64, ksl], qT[64 * e:64 * e + 64, qsl],
                            start=True, stop=False)
                for e in range(2):
                    self.mm(sb[:, 512 * e:512 * e + nq], self.ident_b, self.bandm[:, mo:mo + nq], start=False, stop=True)

            s_stage(0)
            for i, (p, kt, nq, mo, ksl, qsl) in enumerate(tl):
                if i + 1 < len(tl):
                    s_stage(i + 1)
                sb = self.pp[i % 2]
                Pt = P[i % NP]
                self.act(Pt[:, :, 0:nq], sb[:, :].rearrange("p (e n) -> p e n", e=2)[:, :, 0:nq], AF.Exp, scale=0.125)
                ob = self.pp[2 + i % 2]
                for e in range(2):
                    self.mm(ob[64 * e:64 * e + 64, 0:nq], V[:, p, kt, e, 0:64], Pt[:, e, 0:nq], tp=(0, 64 * e))
                for e in range(2):
                    self.mm(ob[32 * e:32 * e + 1, 512:512 + nq], self.ones_b[:, 0:1], Pt[:, e, 0:nq], tp=(0, 32 * e))
                self.tt('dve', oacc[:, qsl], oacc[:, qsl], ob[:, 0:nq], ALU.add)
                self.tt('dve', dacc[0:33, qsl], dacc[0:33, qsl], ob[0:33, 512:512 + nq], ALU.add)
            self.act(dacc[0:33, :], dacc[0:33, :], AF.Ln)
            self.act(dacc[0:33, :], dacc[0:33, :], AF.Exp, scale=-1.0)
            for blk in range(4):
                bs = slice(blk * 512, (blk + 1) * 512)
                bp = self.ps[blk % 2]
                self.mm(bp[:, :], self.sel_f[0:33, :], dacc[0:33, bs])
                self.tt('dve', oT[:, hp, bs], oacc[:, bs], bp[:, :], ALU.mult)

    def branch_c(self, l, oT):
        GW = (NU_INT + NU_FULL) * 64
        for hp in range(2):
            self.apos = 32768
            qT = self.alloc([SEQ], BF16)
            kT = self.alloc([SEQ], BF16)
            V = self.alloc([16, 2, VW], BF16)
            slabs = [self.alloc([8, 128], BF16) for _ in range(2)]
            G = [self.alloc([GW], F32) for _ in range(2)]
            work = [(self.alloc([512], BF16), self.alloc([512], F32), None, None, None) for _ in range(2)]
            NP = 3
            P = [self.alloc([2, 512], BF16) for _ in range(NP)]
            mark_s = self.apos
            VT = self.alloc([SEQ], BF16)
            self.apos = mark_s
            sbf = [self.alloc([2, 512], F32) for _ in range(2)]
            osb = [self.alloc([512], F32) for _ in range(2)]
            rec = [self.alloc([512], F32) for _ in range(2)]
            for e in range(2):
                self.dma('sp', G[e], self.gtab[l, 2 * hp + e])
            specs = [(qT, 12 + hp, 0), (kT, 14 + hp, 1)]
            self.prep_qk(l, specs, None, work, slabs, 2, vnext=(slabs[0], 16 + hp))
            self.calc_vt(slabs[0], VT)
            self.v_tiles(VT, lambda k4: V[:, 4 * k4:4 * k4 + 4, :, :].rearrange("p k e d -> p k (e d)"),
                         lambda kt: slice(kt * 128, (kt + 1) * 128))
            chunks = []
            chunks.append((0, 256, [(j, NU_INT * 64 + (6 - 2 * j) * 64) for j in range(4)]))
            for ii in range(3):
                R0 = 4 + 8 * ii
                chunks.append((64 * R0, 512, [(j, (10 - (2 * j - R0)) * 64) for j in range(4 * ii, 4 * ii + 8)]))
            chunks.append((64 * 28, 256, [(12 + jj, NU_INT * 64 + (10 - 2 * jj) * 64) for jj in range(4)]))
            for k in range(2):
                self.memset('dve', self.pp[2 + k][0:33, 512:1024], 1.0)
            flat = []
            for k, (q0, nq, tl) in enumerate(chunks):
                for ti, (j, goff) in enumerate(tl):
                    flat.append((k, q0, nq, ti, len(tl), j, goff))

            def s_stage(i):
                k, q0, nq, ti, nt, j, goff = flat[i]
                sb = self.pp[i % 2]
                for e in range(2):
                    self.mm(sb[:, 512 * e:512 * e + nq], kT[64 * e:64 * e + 64, j * 128:(j + 1) * 128],
                            qT[64 * e:64 * e + 64, q0:q0 + nq])

            pending = None
            s_stage(0)
            for i, (k, q0, nq, ti, nt, j, goff) in enumerate(flat):
                if i + 1 < len(flat):
                    s_stage(i + 1)
                sb = self.pp[i % 2]
                ob = self.pp[2 + k % 2]
                sf = sbf[i % 2]
                for e in range(2):
                    self.stt('dve', sf[:, e, 0:nq], sb[:, 512 * e:512 * e + nq], 0.125, G[e][:, goff:goff + nq], ALU.mult, ALU.add)
                Pt = P[i % NP]
                self.act(Pt[:, :, 0:nq], sf[:, :, 0:nq], AF.Exp)
                for e in range(2):
                    self.mm(ob[64 * e:64 * e + 64, 0:nq], V[:, j, e, 0:64], Pt[:, e, 0:nq],
                            start=(ti == 0), stop=(ti == nt - 1), tp=(0, 64 * e))
                for e in range(2):
                    self.mm(ob[32 * e:32 * e + 1, 512:512 + nq], self.ones_b[:, 0:1], Pt[:, e, 0:nq],
                            start=(ti == 0), stop=(ti == nt - 1), tp=(0, 32 * e))
                if pending is not None and i - pending[0] >= 2:
                    pending[1]()
                    pending = None
                if ti == nt - 1:
                    def fin(ob=ob, k=k, q0=q0, nq=nq):
                        r_ = rec[k % 2]
                        self.act(r_[0:33, 0:nq], ob[0:33, 512:512 + nq], AF.Ln)
                        self.act(r_[0:33, 0:nq], r_[0:33, 0:nq], AF.Exp, scale=-1.0)
                        self.cp('dve', osb[k % 2][:, 0:nq], ob[:, 0:nq])
                        self.mm(ob[:, 512:512 + nq], self.sel_f[0:33, :], r_[0:33, 0:nq])
                        self.tt('dve', oT[:, 6 + hp, q0:q0 + nq], osb[k % 2][:, 0:nq], ob[:, 512:512 + nq], ALU.mult)
                    pending = (i, fin)
            if pending is not None:
                pending[1]()

    def merge(self, l, oT):
        self.apos = 32768
        mT = self.alloc([8, SEQ], BF16)
        wg = [[self.alloc([8, 128], BF16) for _ in range(3)] for _ in range(2)]
        wb = [self.alloc([8, 128], BF16) for _ in range(2)]
        gsb = [self.alloc([512], F32) for _ in range(3)]
        acc = [self.alloc([512], F32) for _ in range(2)]
        tmp = [self.alloc([512], F32) for _ in range(2)]
        wo = [wg[0][0], wg[0][1]]
        brk = [(0, 2), (2, 6), (6, 8)]

        def load(m):
            for b in range(3):
                self.slab_load(wg[m % 2][b], self.w_in[l, 18 + 8 * b + m])
            self.slab_load(wb[m % 2], self.w_br[l, m])

        load(0)
        self.drain()
        n = 0
        for m in range(8):
            if m + 1 < 8:
                load(m + 1)
            else:
                self.slab_load(wo[0], self.w_out[l, 0])
            for blk in range(4):
                self.pump()
                bs = slice(blk * 512, (blk + 1) * 512)
                for b in range(3):
                    gp = self.ps[n % 3]
                    for kc in range(8):
                        self.mm(gp[:, :], wg[m % 2][b][:, kc, :], self.hT[:, kc, bs], start=(kc == 0), stop=(kc == 7))
                    self.act(gsb[b], gp[:, :], AF.Sigmoid, bias=self.pcol('bgate', l * 24 + b * 8 + m))
                    yp = self.ps[3 + n % 3]
                    k0, k1 = brk[b]
                    for kc in range(k0, k1):
                        self.mm(yp[:, :], wb[m % 2][:, kc, :], oT[:, kc, bs], start=(kc == k0), stop=(kc == k1 - 1))
                    a = acc[(m * 4 + blk) % 2]
                    if b == 0:
                        self.tt('dve', a, gsb[b], yp[:, :], ALU.mult)
                    elif b == 1:
                        self.tt('dve', tmp[0], gsb[b], yp[:, :], ALU.mult)
                        self.tt('dve', a, a, tmp[0], ALU.add)
                    else:
                        self.tt('dve', tmp[1], gsb[b], yp[:, :], ALU.mult)
                        self.tt('dve', mT[:, m, bs], a, tmp[1], ALU.add)
                    n += 1
        self.drain()
        for m in range(8):
            if m + 1 < 8:
                self.slab_load(wo[(m + 1) % 2], self.w_out[l, m + 1])
            for blk in range(4):
                self.pump()
                bs = slice(blk * 512, (blk + 1) * 512)
                op_ = self.ps[6 + (m * 4 + blk) % 2]
                for kc in range(8):
                    self.mm(op_[:, :], wo[m % 2][:, kc, :], mT[:, kc, bs], start=(kc == 0), stop=(kc == 7))
                self.tt('dve', self.xT[:, m, bs], self.xT[:, m, bs], op_[:, :], ALU.add)

    def ffn(self, l):
        self.apos = 0
        sq = [self.alloc([512], BF16) for _ in range(2)]
        rstd = [self.alloc([512], F32) for _ in range(2)]
        self.norm(l, 1, (sq, rstd))
        self.apos = 0
        gT = self.alloc([24, 1024], BF16)
        NU = 1026
        mark_r = self.apos
        rawS = [[self.alloc([NU], F32) for _ in range(2)] for _ in range(2)]
        accS = [[self.alloc([NU], F32) for _ in range(2)] for _ in range(2)]
        wu = [[self.alloc([8, 128], BF16) for _ in range(2)] for _ in range(2)]
        end_ = self.apos
        self.apos = mark_r
        wd = [self.alloc([24, 128], BF16) for _ in range(2)]
        self.apos = end_
        otb = self.alloc([D], F32)
        for half in range(2):
            T0 = 1024 * half
            ua = max(0, T0 - 1)
            ub = min(SEQ, T0 + 1025)
            nu = ub - ua
            o0 = T0 - ua
            blocks = [(0, 512), (512, 1024), (1024, nu)] if half == 0 else [(0, 1), (1, 513), (513, nu)]

            def load(fc):
                self.slab_load(wu[fc % 2][0], self.w_up[l, fc])
                self.slab_load(wu[fc % 2][1], self.w_up[l, 24 + fc])

            load(0)
            self.drain()
            for fc in range(24):
                if fc + 1 < 24:
                    load(fc + 1)
                else:
                    self.slab_load(wd[0], self.w_down[l, 0])
                raw, accb = rawS[fc % 2], accS[fc % 2]
                if l == self.nl - 1 and half == 1 and fc % 3 == 0:
                    self.early_last.append(self.store_tile(fc // 3, otb, [6, 7]))
                for gv in range(2):
                    self.pump()
                    f = fc + 24 * gv
                    r_, a_ = raw[gv], accb[gv]
                    for bi, (b0, b1) in enumerate(blocks):
                        up = self.ps[(gv * 3 + bi) % 6]
                        for kc in range(8):
                            self.mm(up[:, 0:b1 - b0], wu[fc % 2][gv][:, kc, :], self.hT[:, kc, ua + b0:ua + b1],
                                    start=(kc == 0), stop=(kc == 7))
                        self.cp('act', r_[:, b0:b1], up[:, 0:b1 - b0])
                        self.act(a_[:, b0:b1], up[:, 0:b1 - b0], AF.Identity, bias=self.pcol('convb', l * 48 + f),
                                 scale=self.pcol('convw', (l * 3 + 1) * 48 + f))
                    self.stt('dve', a_[:, 1:nu], r_[:, 0:nu - 1], self.pcol('convw', (l * 3 + 0) * 48 + f), a_[:, 1:nu],
                             ALU.mult, ALU.add)
                    self.stt('dve', a_[:, 0:nu - 1], r_[:, 1:nu], self.pcol('convw', (l * 3 + 2) * 48 + f),
                             a_[:, 0:nu - 1], ALU.mult, ALU.add)
                self.act(accb[0][:, o0:o0 + 1024], accb[0][:, o0:o0 + 1024], AF.Gelu_apprx_tanh)
                self.tt('dve', gT[:, fc, :], accb[0][:, o0:o0 + 1024], accb[1][:, o0:o0 + 1024], ALU.mult)
            self.drain()
            for m in range(8):
                if m + 1 < 8:
                    self.slab_load(wd[(m + 1) % 2], self.w_down[l, m + 1])
                for blk in range(2):
                    self.pump()
                    dp = self.ps[6 + (m * 2 + blk) % 2]
                    for fc in range(24):
                        self.mm(dp[:, :], wd[m % 2][:, fc, :], gT[:, fc, blk * 512:(blk + 1) * 512], start=(fc == 0), stop=(fc == 23))
                    ts = slice(T0 + blk * 512, T0 + (blk + 1) * 512)
                    self.tt('dve', self.xT[:, m, ts], self.xT[:, m, ts], dp[:, :], ALU.add)

    def layer(self, l):
        st = self.stages
        self.apos = 32768
        sq = [self.alloc([512], BF16) for _ in range(2)]
        rstd = [self.alloc([512], F32) for _ in range(2)]
        if l == 0:
            save = self.apos
            self.apos = 0
            xt = [self.alloc([D], F32) for _ in range(4)]
            self.apos = save
            for blk in range(4):
                self.load_x_blk(blk, xt)
                self.norm(l, 0, (sq, rstd), blks=(blk,))
        else:
            self.norm(l, 0, (sq, rstd))
        self.apos = 0
        oT = self.alloc([8, SEQ], BF16)
        if st is None or 'a' in st:
            self.branch_a(l, oT)
        if st is None or 'b' in st:
            self.branch_b(l, oT)
        if st is None or 'c' in st:
            self.branch_c(l, oT)
        if st is not None and 'dump_o' in st:
            return oT
        if st is None or 'm' in st:
            self.merge(l, oT)
        if st is None or 'f' in st:
            self.ffn(l)
        return None

    def build(self):
        self.load_consts()
        self.early_last = []
        for l in range(self.nl):
            self.layer(l)
        last = self.early_last + self.store_x(range(8, 16))
        S = self.S
        S.emit(None, self.sems, self.dsems)
        nc = self.nc
        sems, dsems = self.sems, self.dsems
        with nc.Block() as block:
            @block.tensor
            def _(e):
                S.emit_engine('pe', e, sems, dsems)

            @block.scalar
            def _(e):
                S.emit_engine('act', e, sems, dsems)

            @block.vector
            def _(e):
                S.emit_engine('dve', e, sems, dsems)

            @block.gpsimd
            def _(e):
                S.emit_engine('pool', e, sems, dsems)

            @block.sync
            def _(e):
                S.emit_engine('sp', e, sems, dsems)
                S.final_waits('sp', e, sems, dsems, last)
        self.st.close()
        return nc


_CONSTS = None


def _consts():
    global _CONSTS
    if _CONSTS is None:
        _CONSTS = dict(rope=_rope_tables(), cmats=_const_mats(), band=_band_mask())
    return _CONSTS


def _tile_k(w):
    L, K, N = w.shape
    t = np.asarray(w, np.float32).reshape(L, K // 128, 128, N // 128, 128).transpose(0, 3, 2, 1, 4)
    return np.ascontiguousarray(t).reshape(L, N // 128, 128, (K // 128) * 128)


def _tile_w_in(w_in):
    t = _tile_k(w_in)
    L = t.shape[0]
    dups = []
    for g in range(2):
        wk = np.asarray(w_in, np.float32)[:, :, B_K + g * 64:B_K + (g + 1) * 64]
        wk = wk.reshape(L, 8, 128, 64).transpose(0, 2, 1, 3)
        dups.append(np.concatenate([wk, wk], axis=-1).reshape(L, 1, 128, 1024))
    return np.ascontiguousarray(np.concatenate([t] + dups, axis=1))


def _run(nl, x, w_in, w_branch, w_out, w_up, w_down, params, gtab, stages=None):
    prog = Prog(nl, stages)
    nc = prog.build()
    c = _consts()
    f = lambda a: np.ascontiguousarray(a, dtype=np.float32)
    shared = dict(w_in=_tile_w_in(w_in), w_branch=_tile_k(w_branch), w_out=_tile_k(w_out), w_up=_tile_k(w_up),
                  w_down=_tile_k(w_down),
                  params=f(params), rope=c['rope'], cmats=c['cmats'], band=c['band'], gtab=f(gtab))
    nb = x.shape[0]
    in_maps = [dict(shared, x=f(x[b])) for b in range(nb)]
    res = run_bass_kernel_spmd(nc, in_maps, core_ids=list(range(nb)))
    return np.stack([r["out"] for r in res.results], 0)


def kernel(x, w_in, b_gate, qk_gain, rel_pos_bias, w_branch, w_out, norm_mix, norm_ffn, w_up, conv_w, conv_b, w_down):
    x = np.asarray(x, np.float32)
    params = _pack_params(np.asarray(b_gate), np.asarray(qk_gain), np.asarray(norm_mix), np.asarray(norm_ffn),
                          np.asarray(conv_w), np.asarray(conv_b))
    gtab = _bias_tables(np.asarray(rel_pos_bias, np.float32))
    return _run(NL, x, w_in, w_branch, w_out, w_up, w_down, params, gtab).astype(np.float32)
```

```python
import numpy as np
import concourse.bass as bass
import concourse.mybir as mybir
from concourse.bass_utils import run_bass_kernel_spmd

F32 = mybir.dt.float32
BF16 = mybir.dt.bfloat16
ALU = mybir.AluOpType
AF = mybir.ActivationFunctionType
DT_SIZE = {F32: 4, BF16: 2}


def _dsize(dt):
    return DT_SIZE[dt]


class Sched:
    DMA_ROT = 6

    def __init__(self, nc):
        self.nc = nc
        self.ops = []
        self.acc = {}
        self.eng_ops = {e: [] for e in ('pe', 'act', 'dve', 'pool', 'sp')}

    @staticmethod
    def region(ap, whole=False):
        t = ap.tensor
        name = t.name
        space = str(ap.space)
        if 'DRAM' in space.upper() or 'HBM' in space.upper() or type(t).__name__.startswith('DRam'):
            return (name, 0, 1 << 30, 0, 1 << 40, 'dram')
        pat = ap.ap
        pstep, pcnt = pat[0]
        off = int(ap.offset)
        es = _dsize(ap.dtype)
        if pstep == 0:
            p0, fo = 0, off
            pcnt = 1
        else:
            p0, fo = divmod(off, pstep)
        ext = 0
        for st, cnt in pat[1:]:
            ext += abs(st) * (cnt - 1)
        b0 = fo * es
        b1 = (fo + ext + 1) * es
        kind = 'psum' if 'PSUM' in space.upper() or type(t).__name__.startswith('PSum') else 'sbuf'
        if kind == 'psum':
            return (name, 0, 128, (b0 // 2048) * 2048, ((b1 + 2047) // 2048) * 2048, kind)
        return (name, p0, p0 + pcnt, b0, b1, kind)

    def op(self, eng, fn, reads=(), writes=(), dma=False):
        opid = len(self.ops)
        deps = set()
        regs = [(self.region(a), False) for a in reads] + [(self.region(a), True) for a in writes]
        for (name, p0, p1, b0, b1, kind), is_w in regs:
            if kind == 'dram':
                continue
            lst = self.acc.setdefault(name, [])
            w = is_w or kind == 'psum'
            for e in lst:
                if e[0] < p1 and p0 < e[1] and e[2] < b1 and b0 < e[3] and (w or e[5]):
                    if e[4] != opid:
                        deps.add(e[4])
        for (name, p0, p1, b0, b1, kind), is_w in regs:
            if kind == 'dram':
                continue
            lst = self.acc[name]
            w = is_w or kind == 'psum'
            if w:
                lst[:] = [e for e in lst if not (p0 <= e[0] and e[1] <= p1 and b0 <= e[2] and e[3] <= b1)]
                lst.append([p0, p1, b0, b1, opid, True, eng])
            else:
                rep = False
                if not dma:
                    for e in lst:
                        if (not e[5]) and e[6] == eng and e[0] == p0 and e[1] == p1 and e[2] == b0 and e[3] == b1 \
                                and not self.ops[e[4]]['dma']:
                            e[4] = opid
                            rep = True
                            break
                if not rep:
                    lst.append([p0, p1, b0, b1, opid, False, eng])
        if eng == 'pe':
            deps = {d for d in deps if not (self.ops[d]['eng'] == 'pe' and not self.ops[d]['dma'])}
        self.ops.append(dict(eng=eng, fn=fn, deps=deps, dma=dma))
        self.eng_ops[eng].append(opid)
        return opid

    def emit(self, block_engs, sems, dma_sems):
        ops = self.ops
        dma_idx = {}
        cnt = {e: 0 for e in self.eng_ops}
        for i, o in enumerate(ops):
            if o['dma']:
                dma_idx[i] = cnt[o['eng']]
                cnt[o['eng']] += 1
        R = self.DMA_ROT
        needed = set()
        for i, o in enumerate(ops):
            for d in o['deps']:
                if not ops[d]['dma']:
                    needed.add(d)
        signo = {}
        c = {e: 0 for e in self.eng_ops}
        for i, o in enumerate(ops):
            if (not o['dma']) and i in needed:
                c[o['eng']] += 1
                signo[i] = c[o['eng']]
        self.signo = signo
        self.dma_idx = dma_idx
        self.n_waits = 0

    def emit_engine(self, eng, engobj, sems, dma_sems):
        ops = self.ops
        R = self.DMA_ROT
        waited = {}

        def wait(key, sem, val):
            if waited.get(key, 0) >= val:
                return
            waited[key] = val
            engobj.wait_ge(sem, val)
            self.n_waits += 1

        for i in self.eng_ops[eng]:
            o = ops[i]
            for d in sorted(o['deps']):
                od = ops[d]
                if od['dma']:
                    k = self.dma_idx[d]
                    wait(('d', od['eng'], k % R), dma_sems[od['eng']][k % R], 16 * (k // R + 1))
                else:
                    wait(('c', od['eng']), sems[od['eng']], self.signo[d])
            if o['dma']:
                k = self.dma_idx[i]
                if k >= R:
                    wait(('d', eng, k % R), dma_sems[eng][k % R], 16 * (k // R))
                ins = o['fn'](engobj)
                ins.then_inc(dma_sems[eng][k % R], 16)
            else:
                ins = o['fn'](engobj)
                if i in self.signo:
                    ins.then_inc(sems[eng], 1)

    def final_waits(self, eng, engobj, sems, dma_sems, opids):
        R = self.DMA_ROT
        for d in opids:
            od = self.ops[d]
            if od['dma']:
                k = self.dma_idx[d]
                engobj.wait_ge(dma_sems[od['eng']][k % R], 16 * (k // R + 1))
            else:
                engobj.wait_ge(sems[od['eng']], self.signo[d])


D = 1024
SEQ = 2048
NL = 2
IN_W = 5376
A_Q, A_K, A_V = 0, 256, 512
B_Q, B_K, B_V = 768, 1280, 1408
C_Q, C_K, C_V = 1536, 1792, 2048
GATE0 = 2304
DFF = 3072
EPS = 1e-6
NEG = -30000.0
VW = 64
ARENA_BYTES = 104 * 1024
STG_OFF = 96 * 1024
NU_INT, NU_FULL = 22, 14


def _param_layout():
    off = {}
    n = 0
    for name, cnt in (('normg', NL * 2 * 8), ('bgate', NL * 24), ('convw', NL * 3 * 48), ('convb', NL * 48),
                      ('qkg', NL * 6), ('eps', 1)):
        off[name] = n
        n += cnt
    return off, n


POFF, NPAR = _param_layout()


def _pack_params(b_gate, qk_gain, norm_mix, norm_ffn, conv_w, conv_b):
    P = np.zeros((128, NPAR), np.float32)
    for l in range(NL):
        P[:, POFF['normg'] + (l * 2 + 0) * 8:POFF['normg'] + (l * 2 + 0) * 8 + 8] = norm_mix[l].reshape(8, 128).T
        P[:, POFF['normg'] + (l * 2 + 1) * 8:POFF['normg'] + (l * 2 + 1) * 8 + 8] = norm_ffn[l].reshape(8, 128).T
        P[:, POFF['bgate'] + l * 24:POFF['bgate'] + l * 24 + 24] = b_gate[l].reshape(24, 128).T
        for j in range(3):
            o = POFF['convw'] + (l * 3 + j) * 48
            P[:, o:o + 48] = conv_w[l, j].reshape(48, 128).T
        o = POFF['convb'] + l * 48
        P[:, o:o + 48] = conv_b[l].reshape(48, 128).T
        for br in range(3):
            for qk in range(2):
                P[:, POFF['qkg'] + l * 6 + br * 2 + qk] = np.tile(qk_gain[l, br, qk], 2)
    P[:, POFF['eps']] = EPS
    return P


def _rope_tables():
    t = np.arange(SEQ)

    def ang(pos, dim):
        inv = (np.float32(10000.0) ** (-np.arange(0, dim, 2, dtype=np.float32) / np.float32(dim))).astype(np.float32)
        return (pos.astype(np.float32)[:, None] * inv[None, :]).astype(np.float32)

    a1 = ang(t, 64)
    a2 = np.concatenate([ang(t // 64, 32), ang(t % 64, 32)], axis=-1)
    out = []
    for a in (a1, a2):
        idx = (np.arange(128) % 64) % 32
        out.append(np.ascontiguousarray(np.cos(a).astype(np.float32)[:, idx].T))
        out.append(np.ascontiguousarray(np.sin(a).astype(np.float32)[:, idx].T))
    return np.stack(out, 0)


def _const_mats():
    M = np.zeros((5, 128, 128), np.float32)
    M[4, 0, 0:64] = 1.0
    M[4, 32, 64:128] = 1.0
    M[0] = np.eye(128, dtype=np.float32)
    M[1] = 1.0
    M[2, :64, :64] = 1.0
    M[2, 64:, 64:] = 1.0
    for d in range(128):
        if d % 64 < 32:
            M[3, d + 32, d] = -1.0
        else:
            M[3, d - 32, d] = 1.0
    return M


def _band_mask():
    kk = np.arange(128)[:, None]
    qq = np.arange(256)[None, :]
    return np.where((kk <= qq) & (kk >= qq - 128), 0.0, 8.0 * NEG).astype(np.float32)


def _bias_tables(rpb):
    a = (np.arange(128) // 64)[:, None, None]
    cp = (np.arange(128) % 64)[:, None, None]
    c = np.arange(64)[None, None, :]
    cs = np.clip(c - 8, 0, 48)
    col_ok = (cp >= cs) & (cp < cs + 16)
    dc = np.clip(cp - c, -15, 15) + 15
    outs = []
    for (u_lo, nu, lo, hi) in ((-10, NU_INT, -4, 3), (-6, NU_FULL, -7, 7)):
        u = (u_lo + np.arange(nu))[None, :, None]
        dr = a - u
        ok = (dr >= lo) & (dr <= hi) & col_ok
        dri = np.clip(dr + 7, 0, 14)
        g = rpb[:, :, dri, dc]
        g = np.where(ok[None, None], g, np.float32(NEG)).astype(np.float32)
        outs.append(g.reshape(NL, 4, 128, nu * 64))
    return np.ascontiguousarray(np.concatenate(outs, axis=-1))


class Prog:
    def __init__(self, n_layers, stages=None):
        from contextlib import ExitStack
        self.nl = n_layers
        self.stages = stages
        nc = self.nc = bass.Bass("TRN2", target_bir_lowering=False)
        L = n_layers
        dr = lambda name, shape, kind="ExternalInput": nc.dram_tensor(name, shape, F32, kind=kind).ap()
        self.x = dr("x", [SEQ, D])
        self.w_in = dr("w_in", [L, 44, 128, 1024])
        self.w_br = dr("w_branch", [L, 8, 128, 1024])
        self.w_out = dr("w_out", [L, 8, 128, 1024])
        self.w_up = dr("w_up", [L, 48, 128, 1024])
        self.w_down = dr("w_down", [L, 8, 128, 24 * 128])
        self.params = dr("params", [128, NPAR])
        self.rope = dr("rope", [4, 128, SEQ])
        self.cmats = dr("cmats", [5, 128, 128])
        self.band = dr("band", [128, 256])
        self.gtab = dr("gtab", [L, 4, 128, (NU_INT + NU_FULL) * 64])
        self.out = dr("out", [SEQ, D], kind="ExternalOutput")
        self.st = ExitStack()
        E = self.st.enter_context
        self.xT = E(nc.sbuf_tensor("xT", [128, 8, SEQ], F32))
        self.hT = E(nc.sbuf_tensor("hT", [128, 8, SEQ], BF16))
        self.arena = E(nc.sbuf_tensor("arena", [128, ARENA_BYTES // 4], F32))
        self.par = E(nc.sbuf_tensor("par", [128, NPAR], F32))
        self.cm = E(nc.sbuf_tensor("cm", [128, 5, 128], F32))
        self.cmb = E(nc.sbuf_tensor("cmb", [128, 4, 128], BF16))
        self.bandm = E(nc.sbuf_tensor("bandm", [128, 256], BF16))
        self.pp = [E(nc.psum_tensor(f"pp{i}", [128, 1024], F32)) for i in range(4)]
        self.ps = [self.pp[i // 2][:, (i % 2) * 512:(i % 2) * 512 + 512] for i in range(8)]
        self.sems = {e: E(nc.semaphore(f"s_{e}")) for e in ('pe', 'act', 'dve', 'pool', 'sp')}
        self.dsems = {e: [E(nc.semaphore(f"d_{e}{i}")) for i in range(Sched.DMA_ROT)] for e in ('sp', 'pool')}
        self.stg = [self.arena[:, (STG_OFF + 4096 * i) // 4:(STG_OFF + 4096 * (i + 1)) // 4] for i in range(2)]
        self.nstg = 0
        self.ncast = 0
        self.lq = []
        self.inflight = []
        self.S = Sched(nc)
        self.apos = 0
        self.rr = 0

    def alloc(self, shape, dt):
        n = int(np.prod(shape)) * _dsize(dt)
        n = (n + 63) // 64 * 64
        o = self.apos
        assert o + n <= STG_OFF, (o, n)
        self.apos = o + n
        v = self.arena[:, o // 4:(o + n) // 4]
        if dt != F32:
            v = v.bitcast(dt)
        v = v[:, 0:int(np.prod(shape))]
        if len(shape) == 2:
            v = v.rearrange("p (a b) -> p a b", a=shape[0])
        elif len(shape) == 3:
            v = v.rearrange("p (a b c) -> p a b c", a=shape[0], b=shape[1])
        elif len(shape) == 4:
            v = v.rearrange("p (a b c d) -> p a b c d", a=shape[0], b=shape[1], c=shape[2])
        return v

    def mm(self, out, lhsT, rhs, start=True, stop=True, tp=None):
        kw = {} if tp is None else dict(tile_position=tp)
        return self.S.op('pe', lambda e: e.matmul(out, lhsT=lhsT, rhs=rhs, start=start, stop=stop, **kw),
                         reads=[lhsT, rhs], writes=[out])

    def tr(self, out, in_, ident):
        return self.S.op('pe', lambda e: e.transpose(out, in_, ident), reads=[in_, ident], writes=[out])

    def act(self, out, in_, func, bias=None, scale=None):
        kw = {}
        rd = [in_]
        if bias is not None:
            kw['bias'] = bias
            if not isinstance(bias, float):
                rd.append(bias)
        if scale is not None:
            kw['scale'] = scale
            if not isinstance(scale, float):
                rd.append(scale)
        return self.S.op('act', lambda e: e.activation(out=out, in_=in_, func=func, **kw), reads=rd, writes=[out])

    def tt(self, eng, out, in0, in1, op):
        return self.S.op(eng, lambda e: e.tensor_tensor(out=out, in0=in0, in1=in1, op=op), reads=[in0, in1], writes=[out])

    def stt(self, eng, out, in0, scalar, in1, op0, op1):
        rd = [in0, in1] + ([] if isinstance(scalar, float) else [scalar])
        return self.S.op(eng, lambda e: e.scalar_tensor_tensor(out=out, in0=in0, scalar=scalar, in1=in1, op0=op0, op1=op1),
                         reads=rd, writes=[out])

    def cp(self, eng, out, in_):
        if eng == 'act':
            return self.act(out, in_, AF.Copy)
        return self.S.op(eng, lambda e: e.tensor_copy(out=out, in_=in_), reads=[in_], writes=[out])

    def memset(self, eng, out, val):
        return self.S.op(eng, lambda e: e.memset(out, val), writes=[out])

    def recip(self, out, in_):
        return self.S.op('dve', lambda e: e.reciprocal(out=out, in_=in_), reads=[in_], writes=[out])

    def dma(self, eng, out, in_):
        return self.S.op(eng, lambda e: e.dma_start(out=out, in_=in_), reads=[in_], writes=[out], dma=True)

    def pcol(self, name, idx):
        o = POFF[name] + idx
        return self.par[:, o:o + 1]

    def alt(self):
        self.rr += 1
        return 'dve' if self.rr % 2 else 'pool'

    def load_consts(self):
        self.dma('sp', self.par[:], self.params[:, :])
        self.dma('sp', self.cm[:], self.cmats.rearrange("m p n -> p m n"))
        self.cp('dve', self.cmb[:], self.cm[:, 0:4, :])
        self.dma('sp', self.stg[0][:, 0:256], self.band[:, :])
        self.cp('dve', self.bandm[:], self.stg[0][:, 0:256])
        self.ident = self.cm[:, 0, :]
        self.ones_f = self.cm[:, 1, :]
        self.perm_f = self.cm[:, 3, :]
        self.sel_f = self.cm[:, 4, :]
        self.ident_b = self.cmb[:, 0, :]
        self.ones_b = self.cmb[:, 1, :]
        self.bones_b = self.cmb[:, 2, :]
        self.perm_b = self.cmb[:, 3, :]

    def load_x(self):
        self.apos = 0
        xt = [self.alloc([D], F32) for _ in range(2)]
        for t in range(16):
            b = xt[t % 2]
            self.dma('sp', b, self.x[t * 128:(t + 1) * 128, :])
            for half in range(2):
                p = self.ps[(2 * t + half) % 4]
                for j in range(4):
                    c = 4 * half + j
                    self.tr(p[:, j * 128:(j + 1) * 128], b[:, c * 128:(c + 1) * 128], self.ident)
                self.cp('act' if half else 'dve', self.xT[:, 4 * half:4 * half + 4, t * 128:(t + 1) * 128],
                        p[:, :].rearrange("p (j n) -> p j n", j=4))

    def store_x(self):
        self.apos = 0
        ot = [self.alloc([D], F32) for _ in range(2)]
        last = []
        for t in range(16):
            b = ot[t % 2]
            for half in range(2):
                p = self.ps[(2 * t + half) % 4]
                for j in range(4):
                    c = 4 * half + j
                    self.tr(p[:, j * 128:(j + 1) * 128], self.xT[:, c, t * 128:(t + 1) * 128], self.ident)
                self.cp('act' if half else 'dve', b[:, 512 * half:512 * half + 512], p[:, :])
            last.append(self.dma('sp', self.out[t * 128:(t + 1) * 128, :], b))
        return last

    def norm(self, l, which, work):
        sq, rstd = work
        for blk in range(4):
            bs = slice(blk * 512, (blk + 1) * 512)
            pn = self.ps[blk % 2]
            for c in range(8):
                s = sq[c % 2]
                self.act(s, self.xT[:, c, bs], AF.Square)
                self.mm(pn[:, :], self.ones_b, s, start=(c == 0), stop=(c == 7))
            r = rstd[blk % 2]
            self.act(r, pn[:, :], AF.Ln, bias=self.pcol('eps', 0), scale=1.0 / D)
            self.act(r, r, AF.Exp, scale=-0.5)
            for c in range(8):
                self.stt('dve', self.hT[:, c, bs], self.xT[:, c, bs],
                         self.pcol('normg', (l * 2 + which) * 8 + c), r, ALU.mult, ALU.mult)

    def slab_load(self, dst, src, ceng=None):
        d2 = dst.rearrange("p k n -> p (k n)")
        n = d2.shape[1]
        for o in range(0, n, 1024):
            self.lq.append((d2[:, o:o + 1024], src[:, o:o + 1024], ceng))

    def pump(self):
        for (st, d, ceng) in self.inflight:
            self.ncast += 1
            eng = ceng if ceng is not None else ('act' if self.ncast % 2 else 'dve')
            self.cp(eng, d, st)
        self.inflight = []
        while self.lq and len(self.inflight) < 2:
            d, src, ceng = self.lq.pop(0)
            st = self.stg[self.nstg % 2]
            self.nstg += 1
            self.dma('sp', st, src)
            self.inflight.append((st, d, ceng))

    def drain(self):
        while self.lq or self.inflight:
            self.pump()

    def prep_qk(self, l, specs, tabs, work, slabs, gidx, vnext=None):
        w_in = self.w_in
        units = []
        for ci, (dst, cols, qk) in enumerate(specs):
            for blk in range(4):
                units.append((ci, dst, cols, qk, blk))

        def stage1(u):
            ci, dst, cols, qk, blk = units[u]
            sl = slabs[ci % 2]
            if blk == 0:
                if ci == 0:
                    self.slab_load(sl, w_in[l, cols])
                self.drain()
                if ci + 1 < len(specs):
                    self.slab_load(slabs[(ci + 1) % 2], w_in[l, specs[ci + 1][1]])
                elif vnext is not None:
                    self.slab_load(vnext[0], w_in[l, vnext[1]])
            self.pump()
            gain = self.pcol('qkg', l * 6 + gidx * 2 + qk)
            sqb, rstdb, qnb, t1b, t2b = work[u % 2]
            bs = slice(blk * 512, (blk + 1) * 512)
            qp = self.ps[u % 2]
            for kc in range(8):
                self.mm(qp[:, :], sl[:, kc, :], self.hT[:, kc, bs], start=(kc == 0), stop=(kc == 7))
            self.act(sqb, qp[:, :], AF.Square)
            sp_ = self.ps[2 + u % 2]
            self.mm(sp_[:, :], self.bones_b, sqb)
            self.act(rstdb, sp_[:, :], AF.Ln, bias=self.pcol('eps', 0), scale=1.0 / 64)
            self.act(rstdb, rstdb, AF.Exp, scale=-0.5)
            if tabs is None:
                self.stt('dve', dst[:, bs], qp[:, :], gain, rstdb, ALU.mult, ALU.mult)
            else:
                cos, sin = tabs
                self.stt('dve', qnb, qp[:, :], gain, rstdb, ALU.mult, ALU.mult)
                self.tt('dve', t1b, qnb, cos[:, bs], ALU.mult)
                self.tt('dve', t2b, qnb, sin[:, bs], ALU.mult)

        def stage2(u):
            ci, dst, cols, qk, blk = units[u]
            if tabs is None:
                return
            sqb, rstdb, qnb, t1b, t2b = work[u % 2]
            bs = slice(blk * 512, (blk + 1) * 512)
            rp = self.ps[4 + u % 2]
            self.mm(rp[:, :], self.ident_b, t1b, start=True, stop=False)
            self.mm(rp[:, :], self.perm_b, t2b, start=False, stop=True)
            self.cp('act', dst[:, bs], rp[:, :])

        stage1(0)
        for u in range(len(units)):
            if u + 1 < len(units):
                stage1(u + 1)
            stage2(u)

    def calc_vt(self, slab, VT):
        self.drain()
        for blk in range(4):
            bs = slice(blk * 512, (blk + 1) * 512)
            vp = self.ps[4 + blk % 2]
            for kc in range(8):
                self.mm(vp[:, :], slab[:, kc, :], self.hT[:, kc, bs], start=(kc == 0), stop=(kc == 7))
            self.cp('dve' if blk % 2 else 'act', VT[:, bs], vp[:, :])

    def v_tiles(self, VT, vdst4_fn, tok_fn, nkt=16, split=None):
        pbf = self.pp[3].bitcast(BF16)
        for k4 in range(nkt // 4):
            pb = pbf[:, (k4 % 2) * 1024:(k4 % 2) * 1024 + 512]
            for j in range(4):
                self.tr(pb[:, j * 128:(j + 1) * 128], VT[:, tok_fn(4 * k4 + j)], self.ident_b)
            src = pb.rearrange("p (j n) -> p j n", j=4) if split is None else \
                pb.rearrange("p (j g d) -> p j g d", j=4, g=split)
            self.cp('dve' if k4 % 2 else 'act', vdst4_fn(k4), src)

    def finalize(self, o_src_num, o_src_den, dst, osb_den_row, nq):
        rec = osb_den_row
        self.act(rec, o_src_den, AF.Ln)
        self.act(rec, rec, AF.Exp, scale=-1.0)
        bp = self.ps[7]
        self.mm(bp[0:64, 0:nq], self.ones_f[64:65, 0:64], rec)
        self.tt('dve', dst, o_src_num, bp[0:64, 0:nq], ALU.mult)

    def branch_b(self, l, oT):
        VB = 72
        self.apos = 32768
        V = self.alloc([16, 2, VB], BF16)
        VT = self.alloc([SEQ], BF16)
        cos = self.alloc([SEQ], F32)
        sin = self.alloc([SEQ], F32)
        self.dma('sp', cos, self.rope[2])
        self.dma('sp', sin, self.rope[3])
        base = self.apos
        for g in range(2):
            self.apos = base
            qT = self.alloc([2, SEQ], BF16)
            kT = self.alloc([SEQ], BF16)
            mark = self.apos
            work = [(self.alloc([512], BF16), self.alloc([512], F32), self.alloc([512], F32), self.alloc([512], BF16),
                     self.alloc([512], BF16)) for _ in range(2)]
            slabs = [self.alloc([8, 128], BF16) for _ in range(2)]
            specs = [(qT[:, 0, :], 6 + 2 * g, 0),
                     (qT[:, 1, :], 7 + 2 * g, 0),
                     (kT, 42 + g, 1)]
            self.prep_qk(l, specs, (cos, sin), work, slabs, 1, vnext=((slabs[1], 11) if g == 0 else None))
            if g == 0:
                self.calc_vt(slabs[1], VT)
                self.memset('dve', V[:, :, :, 64:65], 1.0)
                self.v_tiles(VT, lambda k4: V[:, 4 * k4:4 * k4 + 4, :, 0:64],
                             lambda kt: slice(kt * 128, (kt + 1) * 128), split=2)
            self.apos = mark
            NP = 3
            P = [self.alloc([1024], BF16) for _ in range(NP)]
            osb = [self.alloc([512], F32) for _ in range(2)]
            rec = [self.alloc([1024], F32) for _ in range(2)]
            tiles = [(hp2, qc, kt) for hp2 in range(2) for qc in range(4) for kt in range(16)]

            def s_mm(i):
                hp2, qc, kt = tiles[i]
                sp_ = self.pp[i % 2]
                for e in range(2):
                    self.mm(sp_[:, 512 * e:512 * e + 512], kT[64 * e:64 * e + 64, kt * 128:(kt + 1) * 128],
                            qT[64 * e:64 * e + 64, hp2, qc * 512:(qc + 1) * 512])

            pending = None
            s_mm(0)
            for i, (hp2, qc, kt) in enumerate(tiles):
                if i + 1 < len(tiles):
                    s_mm(i + 1)
                Pt = P[i % NP]
                self.act(Pt, self.pp[i % 2][:, :], AF.Exp, scale=0.125)
                j = hp2 * 4 + qc
                ob = self.pp[2 + j % 2]
                for e in range(2):
                    self.mm(ob[0:65, 512 * e:512 * e + 512], V[:, kt, g, 0:65], Pt[:, 512 * e:512 * e + 512],
                            start=(kt == 0), stop=(kt == 15))
                if pending is not None and i - pending[0] >= 3:
                    pending[1]()
                    pending = None
                if kt == 15:
                    def fin(ob=ob, k=j % 2, hp2=hp2, qc=qc):
                        r_ = rec[k]
                        for e in range(2):
                            self.act(r_[64:65, 512 * e:512 * e + 512], ob[64:65, 512 * e:512 * e + 512], AF.Ln)
                        self.act(r_[64:65, :], r_[64:65, :], AF.Exp, scale=-1.0)
                        for e in range(2):
                            self.cp('dve', osb[k][64 * e:64 * e + 64, :], ob[0:64, 512 * e:512 * e + 512])
                        for e in range(2):
                            self.mm(ob[64 * e:64 * e + 64, 0:512], self.ones_f[64:65, 0:64], r_[64:65, 512 * e:512 * e + 512],
                                    tp=(64, 64 * e))
                        self.tt('dve', oT[:, 2 + 2 * g + hp2, qc * 512:(qc + 1) * 512], osb[k], ob[:, 0:512], ALU.mult)
                    pending = (i, fin)
            if pending is not None:
                pending[1]()

    def branch_a(self, l, oT):
        self.apos = 32768
        cos = self.alloc([SEQ], F32)
        sin = self.alloc([SEQ], F32)
        self.dma('sp', cos, self.rope[0])
        self.dma('sp', sin, self.rope[1])
        base = self.apos
        for hp in range(2):
            self.apos = base
            qT = self.alloc([SEQ], BF16)
            kT = self.alloc([SEQ], BF16)
            V = self.alloc([3, 16, 2, VW], BF16)
            VT = self.alloc([SEQ], BF16)
            mark = self.apos
            slabs = [self.alloc([8, 128], BF16) for _ in range(2)]
            work = [(self.alloc([512], BF16), self.alloc([512], F32), self.alloc([512], F32), self.alloc([512], BF16),
                     self.alloc([512], BF16)) for _ in range(2)]
            specs = [(qT, hp, 0), (kT, 2 + hp, 1)]
            self.prep_qk(l, specs, (cos, sin), work, slabs, 0, vnext=(slabs[0], 4 + hp))
            self.calc_vt(slabs[0], VT)
            pats = [(1, 64), (4, 64), (16, 64)]
            for p, (dil, rad) in enumerate(pats):
                nts = 16 // dil

                def tok(kt, dil=dil, nts=nts):
                    r, j = kt // nts, kt % nts
                    s0 = r + dil * 128 * j
                    return slice(s0, s0 + dil * 127 + 1, dil)

                self.v_tiles(VT, lambda k4, p=p: V[:, p, 4 * k4:4 * k4 + 4, :, :].rearrange("p k e d -> p k (e d)"), tok)
            self.apos = mark
            oacc = self.alloc([SEQ], F32)
            dacc = self.alloc([SEQ], F32)
            NP = 3
            P = [self.alloc([2, 256], BF16) for _ in range(NP)]
            self.memset('dve', oacc, 0.0)
            self.memset('dve', dacc[0:33, :], 1.0)
            self.memset('dve', dacc[0:1, :], 0.0)
            self.memset('dve', dacc[32:33, :], 0.0)
            for k in range(2):
                self.memset('dve', self.pp[2 + k][0:33, 512:1024], 0.0)
            tl = []
            for p, (dil, rad) in enumerate(pats):
                Lp = SEQ // dil
                nts = 16 // dil
                for kt in range(16):
                    r, j = kt // nts, kt % nts
                    ql0 = max(0, 128 * j - 64)
                    ql1 = min(Lp, 128 * j + 192)
                    nq = ql1 - ql0
                    mo = ql0 - (128 * j - 64)
                    ks0 = r + dil * 128 * j
                    ksl = slice(ks0, ks0 + dil * 127 + 1, dil)
                    qs0 = r + dil * ql0
                    qsl = slice(qs0, qs0 + dil * (nq - 1) + 1, dil)
                    tl.append((p, kt, nq, mo, ksl, qsl))

            def s_stage(i):
                p, kt, nq, mo, ksl, qsl = tl[i]
                sb = self.pp[i % 2]
                for e in range(2):
                    self.mm(sb[:, 512 * e:512 * e + nq], kT[64 * e:64 * e + 64, ksl], qT[64 * e:64 * e + 64, qsl],
                            start=True, stop=False)
                for e in range(2):
                    self.mm(sb[:, 512 * e:512 * e + nq], self.ident_b, self.bandm[:, mo:mo + nq], start=False, stop=True)

            s_stage(0)
            for i, (p, kt, nq, mo, ksl, qsl) in enumerate(tl):
                if i + 1 < len(tl):
                    s_stage(i + 1)
                sb = self.pp[i % 2]
                Pt = P[i % NP]
                self.act(Pt[:, :, 0:nq], sb[:, :].rearrange("p (e n) -> p e n", e=2)[:, :, 0:nq], AF.Exp, scale=0.125)
                ob = self.pp[2 + i % 2]
                for e in range(2):
                    self.mm(ob[64 * e:64 * e + 64, 0:nq], V[:, p, kt, e, 0:64], Pt[:, e, 0:nq], tp=(0, 64 * e))
                for e in range(2):
                    self.mm(ob[32 * e:32 * e + 1, 512:512 + nq], self.ones_b[:, 0:1], Pt[:, e, 0:nq], tp=(0, 32 * e))
                self.tt('dve', oacc[:, qsl], oacc[:, qsl], ob[:, 0:nq], ALU.add)
                self.tt('dve', dacc[0:33, qsl], dacc[0:33, qsl], ob[0:33, 512:512 + nq], ALU.add)
            self.act(dacc[0:33, :], dacc[0:33, :], AF.Ln)
            self.act(dacc[0:33, :], dacc[0:33, :], AF.Exp, scale=-1.0)
            for blk in range(4):
                bs = slice(blk * 512, (blk + 1) * 512)
                bp = self.ps[blk % 2]
                self.mm(bp[:, :], self.sel_f[0:33, :], dacc[0:33, bs])
                self.tt('dve', oT[:, hp, bs], oacc[:, bs], bp[:, :], ALU.mult)

    def branch_c(self, l, oT):
        GW = (NU_INT + NU_FULL) * 64
        for hp in range(2):
            self.apos = 32768
            qT = self.alloc([SEQ], BF16)
            kT = self.alloc([SEQ], BF16)
            V = self.alloc([16, 2, VW], BF16)
            slabs = [self.alloc([8, 128], BF16) for _ in range(2)]
            G = [self.alloc([GW], F32) for _ in range(2)]
            work = [(self.alloc([512], BF16), self.alloc([512], F32), None, None, None) for _ in range(2)]
            NP = 3
            P = [self.alloc([2, 512], BF16) for _ in range(NP)]
            mark_s = self.apos
            VT = self.alloc([SEQ], BF16)
            self.apos = mark_s
            sbf = [self.alloc([2, 512], F32) for _ in range(2)]
            osb = [self.alloc([512], F32) for _ in range(2)]
            rec = [self.alloc([512], F32) for _ in range(2)]
            for e in range(2):
                self.dma('sp', G[e], self.gtab[l, 2 * hp + e])
            specs = [(qT, 12 + hp, 0), (kT, 14 + hp, 1)]
            self.prep_qk(l, specs, None, work, slabs, 2, vnext=(slabs[0], 16 + hp))
            self.calc_vt(slabs[0], VT)
            self.v_tiles(VT, lambda k4: V[:, 4 * k4:4 * k4 + 4, :, :].rearrange("p k e d -> p k (e d)"),
                         lambda kt: slice(kt * 128, (kt + 1) * 128))
            chunks = []
            chunks.append((0, 256, [(j, NU_INT * 64 + (6 - 2 * j) * 64) for j in range(4)]))
            for ii in range(3):
                R0 = 4 + 8 * ii
                chunks.append((64 * R0, 512, [(j, (10 - (2 * j - R0)) * 64) for j in range(4 * ii, 4 * ii + 8)]))
            chunks.append((64 * 28, 256, [(12 + jj, NU_INT * 64 + (10 - 2 * jj) * 64) for jj in range(4)]))
            for k in range(2):
                self.memset('dve', self.pp[2 + k][0:33, 512:1024], 1.0)
            flat = []
            for k, (q0, nq, tl) in enumerate(chunks):
                for ti, (j, goff) in enumerate(tl):
                    flat.append((k, q0, nq, ti, len(tl), j, goff))

            def s_stage(i):
                k, q0, nq, ti, nt, j, goff = flat[i]
                sb = self.pp[i % 2]
                for e in range(2):
                    self.mm(sb[:, 512 * e:512 * e + nq], kT[64 * e:64 * e + 64, j * 128:(j + 1) * 128],
                            qT[64 * e:64 * e + 64, q0:q0 + nq])

            pending = None
            s_stage(0)
            for i, (k, q0, nq, ti, nt, j, goff) in enumerate(flat):
                if i + 1 < len(flat):
                    s_stage(i + 1)
                sb = self.pp[i % 2]
                ob = self.pp[2 + k % 2]
                sf = sbf[i % 2]
                for e in range(2):
                    self.stt('dve', sf[:, e, 0:nq], sb[:, 512 * e:512 * e + nq], 0.125, G[e][:, goff:goff + nq], ALU.mult, ALU.add)
                Pt = P[i % NP]
                self.act(Pt[:, :, 0:nq], sf[:, :, 0:nq], AF.Exp)
                for e in range(2):
                    self.mm(ob[64 * e:64 * e + 64, 0:nq], V[:, j, e, 0:64], Pt[:, e, 0:nq],
                            start=(ti == 0), stop=(ti == nt - 1), tp=(0, 64 * e))
                for e in range(2):
                    self.mm(ob[32 * e:32 * e + 1, 512:512 + nq], self.ones_b[:, 0:1], Pt[:, e, 0:nq],
                            start=(ti == 0), stop=(ti == nt - 1), tp=(0, 32 * e))
                if pending is not None and i - pending[0] >= 2:
                    pending[1]()
                    pending = None
                if ti == nt - 1:
                    def fin(ob=ob, k=k, q0=q0, nq=nq):
                        r_ = rec[k % 2]
                        self.act(r_[0:33, 0:nq], ob[0:33, 512:512 + nq], AF.Ln)
                        self.act(r_[0:33, 0:nq], r_[0:33, 0:nq], AF.Exp, scale=-1.0)
                        self.cp('dve', osb[k % 2][:, 0:nq], ob[:, 0:nq])
                        self.mm(ob[:, 512:512 + nq], self.sel_f[0:33, :], r_[0:33, 0:nq])
                        self.tt('dve', oT[:, 6 + hp, q0:q0 + nq], osb[k % 2][:, 0:nq], ob[:, 512:512 + nq], ALU.mult)
                    pending = (i, fin)
            if pending is not None:
                pending[1]()

    def merge(self, l, oT):
        self.apos = 32768
        mT = self.alloc([8, SEQ], BF16)
        wg = [[self.alloc([8, 128], BF16) for _ in range(3)] for _ in range(2)]
        wb = [self.alloc([8, 128], BF16) for _ in range(2)]
        gsb = [self.alloc([512], F32) for _ in range(3)]
        acc = [self.alloc([512], F32) for _ in range(2)]
        tmp = [self.alloc([512], F32) for _ in range(2)]
        wo = [wg[0][0], wg[0][1]]
        brk = [(0, 2), (2, 6), (6, 8)]

        def load(m):
            for b in range(3):
                self.slab_load(wg[m % 2][b], self.w_in[l, 18 + 8 * b + m])
            self.slab_load(wb[m % 2], self.w_br[l, m])

        load(0)
        self.drain()
        n = 0
        for m in range(8):
            if m + 1 < 8:
                load(m + 1)
            else:
                self.slab_load(wo[0], self.w_out[l, 0])
            for blk in range(4):
                self.pump()
                bs = slice(blk * 512, (blk + 1) * 512)
                for b in range(3):
                    gp = self.ps[n % 3]
                    for kc in range(8):
                        self.mm(gp[:, :], wg[m % 2][b][:, kc, :], self.hT[:, kc, bs], start=(kc == 0), stop=(kc == 7))
                    self.act(gsb[b], gp[:, :], AF.Sigmoid, bias=self.pcol('bgate', l * 24 + b * 8 + m))
                    yp = self.ps[3 + n % 3]
                    k0, k1 = brk[b]
                    for kc in range(k0, k1):
                        self.mm(yp[:, :], wb[m % 2][:, kc, :], oT[:, kc, bs], start=(kc == k0), stop=(kc == k1 - 1))
                    a = acc[(m * 4 + blk) % 2]
                    if b == 0:
                        self.tt('dve', a, gsb[b], yp[:, :], ALU.mult)
                    elif b == 1:
                        self.tt('dve', tmp[0], gsb[b], yp[:, :], ALU.mult)
                        self.tt('dve', a, a, tmp[0], ALU.add)
                    else:
                        self.tt('dve', tmp[1], gsb[b], yp[:, :], ALU.mult)
                        self.tt('dve', mT[:, m, bs], a, tmp[1], ALU.add)
                    n += 1
        self.drain()
        for m in range(8):
            if m + 1 < 8:
                self.slab_load(wo[(m + 1) % 2], self.w_out[l, m + 1])
            for blk in range(4):
                self.pump()
                bs = slice(blk * 512, (blk + 1) * 512)
                op_ = self.ps[6 + (m * 4 + blk) % 2]
                for kc in range(8):
                    self.mm(op_[:, :], wo[m % 2][:, kc, :], mT[:, kc, bs], start=(kc == 0), stop=(kc == 7))
                self.tt('dve', self.xT[:, m, bs], self.xT[:, m, bs], op_[:, :], ALU.add)

    def ffn(self, l):
        self.apos = 0
        sq = [self.alloc([512], BF16) for _ in range(2)]
        rstd = [self.alloc([512], F32) for _ in range(2)]
        self.norm(l, 1, (sq, rstd))
        self.apos = 0
        gT = self.alloc([24, 1024], BF16)
        NU = 1026
        mark_r = self.apos
        rawS = [[self.alloc([NU], F32) for _ in range(2)] for _ in range(2)]
        accS = [[self.alloc([NU], F32) for _ in range(2)] for _ in range(2)]
        wu = [[self.alloc([8, 128], BF16) for _ in range(2)] for _ in range(2)]
        end_ = self.apos
        self.apos = mark_r
        wd = [self.alloc([24, 128], BF16) for _ in range(2)]
        self.apos = end_
        for half in range(2):
            T0 = 1024 * half
            ua = max(0, T0 - 1)
            ub = min(SEQ, T0 + 1025)
            nu = ub - ua
            o0 = T0 - ua
            blocks = [(0, 512), (512, 1024), (1024, nu)] if half == 0 else [(0, 1), (1, 513), (513, nu)]

            def load(fc):
                self.slab_load(wu[fc % 2][0], self.w_up[l, fc])
                self.slab_load(wu[fc % 2][1], self.w_up[l, 24 + fc])

            load(0)
            self.drain()
            for fc in range(24):
                if fc + 1 < 24:
                    load(fc + 1)
                else:
                    self.slab_load(wd[0], self.w_down[l, 0])
                raw, accb = rawS[fc % 2], accS[fc % 2]
                for gv in range(2):
                    self.pump()
                    f = fc + 24 * gv
                    r_, a_ = raw[gv], accb[gv]
                    for bi, (b0, b1) in enumerate(blocks):
                        up = self.ps[(gv * 3 + bi) % 6]
                        for kc in range(8):
                            self.mm(up[:, 0:b1 - b0], wu[fc % 2][gv][:, kc, :], self.hT[:, kc, ua + b0:ua + b1],
                                    start=(kc == 0), stop=(kc == 7))
                        self.cp('dve' if (gv == 1 and b1 - b0 == 512 and bi == 1) else 'act', r_[:, b0:b1], up[:, 0:b1 - b0])
                        self.act(a_[:, b0:b1], up[:, 0:b1 - b0], AF.Identity, bias=self.pcol('convb', l * 48 + f),
                                 scale=self.pcol('convw', (l * 3 + 1) * 48 + f))
                    self.stt('dve', a_[:, 1:nu], r_[:, 0:nu - 1], self.pcol('convw', (l * 3 + 0) * 48 + f), a_[:, 1:nu],
                             ALU.mult, ALU.add)
                    self.stt('dve', a_[:, 0:nu - 1], r_[:, 1:nu], self.pcol('convw', (l * 3 + 2) * 48 + f),
                             a_[:, 0:nu - 1], ALU.mult, ALU.add)
                self.act(accb[0][:, o0:o0 + 1024], accb[0][:, o0:o0 + 1024], AF.Gelu_apprx_tanh)
                self.tt('dve', gT[:, fc, :], accb[0][:, o0:o0 + 1024], accb[1][:, o0:o0 + 1024], ALU.mult)
            self.drain()
            for m in range(8):
                if m + 1 < 8:
                    self.slab_load(wd[(m + 1) % 2], self.w_down[l, m + 1])
                for blk in range(2):
                    self.pump()
                    dp = self.ps[6 + (m * 2 + blk) % 2]
                    for fc in range(24):
                        self.mm(dp[:, :], wd[m % 2][:, fc, :], gT[:, fc, blk * 512:(blk + 1) * 512], start=(fc == 0), stop=(fc == 23))
                    ts = slice(T0 + blk * 512, T0 + (blk + 1) * 512)
                    self.tt('dve', self.xT[:, m, ts], self.xT[:, m, ts], dp[:, :], ALU.add)

    def layer(self, l):
        st = self.stages
        self.apos = 32768
        sq = [self.alloc([512], BF16) for _ in range(2)]
        rstd = [self.alloc([512], F32) for _ in range(2)]
        self.norm(l, 0, (sq, rstd))
        self.apos = 0
        oT = self.alloc([8, SEQ], BF16)
        if st is None or 'a' in st:
            self.branch_a(l, oT)
        if st is None or 'b' in st:
            self.branch_b(l, oT)
        if st is None or 'c' in st:
            self.branch_c(l, oT)
        if st is not None and 'dump_o' in st:
            return oT
        if st is None or 'm' in st:
            self.merge(l, oT)
        if st is None or 'f' in st:
            self.ffn(l)
        return None

    def build(self):
        self.load_consts()
        self.load_x()
        for l in range(self.nl):
            self.layer(l)
        last = self.store_x()
        S = self.S
        S.emit(None, self.sems, self.dsems)
        nc = self.nc
        sems, dsems = self.sems, self.dsems
        with nc.Block() as block:
            @block.tensor
            def _(e):
                S.emit_engine('pe', e, sems, dsems)

            @block.scalar
            def _(e):
                S.emit_engine('act', e, sems, dsems)

            @block.vector
            def _(e):
                S.emit_engine('dve', e, sems, dsems)

            @block.gpsimd
            def _(e):
                S.emit_engine('pool', e, sems, dsems)

            @block.sync
            def _(e):
                S.emit_engine('sp', e, sems, dsems)
                S.final_waits('sp', e, sems, dsems, last)
        self.st.close()
        return nc


_CONSTS = None


def _consts():
    global _CONSTS
    if _CONSTS is None:
        _CONSTS = dict(rope=_rope_tables(), cmats=_const_mats(), band=_band_mask())
    return _CONSTS


def _tile_k(w):
    L, K, N = w.shape
    t = np.asarray(w, np.float32).reshape(L, K // 128, 128, N // 128, 128).transpose(0, 3, 2, 1, 4)
    return np.ascontiguousarray(t).reshape(L, N // 128, 128, (K // 128) * 128)


def _tile_w_in(w_in):
    t = _tile_k(w_in)
    L = t.shape[0]
    dups = []
    for g in range(2):
        wk = np.asarray(w_in, np.float32)[:, :, B_K + g * 64:B_K + (g + 1) * 64]
        wk = wk.reshape(L, 8, 128, 64).transpose(0, 2, 1, 3)
        dups.append(np.concatenate([wk, wk], axis=-1).reshape(L, 1, 128, 1024))
    return np.ascontiguousarray(np.concatenate([t] + dups, axis=1))


def _run(nl, x, w_in, w_branch, w_out, w_up, w_down, params, gtab, stages=None):
    prog = Prog(nl, stages)
    nc = prog.build()
    c = _consts()
    f = lambda a: np.ascontiguousarray(a, dtype=np.float32)
    shared = dict(w_in=_tile_w_in(w_in), w_branch=_tile_k(w_branch), w_out=_tile_k(w_out), w_up=_tile_k(w_up),
                  w_down=_tile_k(w_down),
                  params=f(params), rope=c['rope'], cmats=c['cmats'], band=c['band'], gtab=f(gtab))
    nb = x.shape[0]
    in_maps = [dict(shared, x=f(x[b])) for b in range(nb)]
    res = run_bass_kernel_spmd(nc, in_maps, core_ids=list(range(nb)))
    return np.stack([r["out"] for r in res.results], 0)


def kernel(x, w_in, b_gate, qk_gain, rel_pos_bias, w_branch, w_out, norm_mix, norm_ffn, w_up, conv_w, conv_b, w_down):
    x = np.asarray(x, np.float32)
    params = _pack_params(np.asarray(b_gate), np.asarray(qk_gain), np.asarray(norm_mix), np.asarray(norm_ffn),
                          np.asarray(conv_w), np.asarray(conv_b))
    gtab = _bias_tables(np.asarray(rel_pos_bias, np.float32))
    return _run(NL, x, w_in, w_branch, w_out, w_up, w_down, params, gtab).astype(np.float32)
```

```python
import numpy as np
import concourse.bass as bass
import concourse.mybir as mybir
from concourse.bass_utils import run_bass_kernel_spmd

F32 = mybir.dt.float32
BF16 = mybir.dt.bfloat16
ALU = mybir.AluOpType
AF = mybir.ActivationFunctionType
DT_SIZE = {F32: 4, BF16: 2}


def _dsize(dt):
    return DT_SIZE[dt]


class Sched:
    DMA_ROT = 6

    def __init__(self, nc):
        self.nc = nc
        self.ops = []
        self.acc = {}
        self.eng_ops = {e: [] for e in ('pe', 'act', 'dve', 'pool', 'sp')}

    @staticmethod
    def region(ap, whole=False):
        t = ap.tensor
        name = t.name
        space = str(ap.space)
        if 'DRAM' in space.upper() or 'HBM' in space.upper() or type(t).__name__.startswith('DRam'):
            return (name, 0, 1 << 30, 0, 1 << 40, 'dram')
        pat = ap.ap
        pstep, pcnt = pat[0]
        off = int(ap.offset)
        es = _dsize(ap.dtype)
        if pstep == 0:
            p0, fo = 0, off
            pcnt = 1
        else:
            p0, fo = divmod(off, pstep)
        ext = 0
        for st, cnt in pat[1:]:
            ext += abs(st) * (cnt - 1)
        b0 = fo * es
        b1 = (fo + ext + 1) * es
        kind = 'psum' if 'PSUM' in space.upper() or type(t).__name__.startswith('PSum') else 'sbuf'
        if kind == 'psum':
            return (name, 0, 128, (b0 // 2048) * 2048, ((b1 + 2047) // 2048) * 2048, kind)
        return (name, p0, p0 + pcnt, b0, b1, kind)

    def op(self, eng, fn, reads=(), writes=(), dma=False):
        opid = len(self.ops)
        deps = set()
        regs = [(self.region(a), False) for a in reads] + [(self.region(a), True) for a in writes]
        for (name, p0, p1, b0, b1, kind), is_w in regs:
            if kind == 'dram':
                continue
            lst = self.acc.setdefault(name, [])
            w = is_w or kind == 'psum'
            for e in lst:
                if e[0] < p1 and p0 < e[1] and e[2] < b1 and b0 < e[3] and (w or e[5]):
                    if e[4] != opid:
                        deps.add(e[4])
        for (name, p0, p1, b0, b1, kind), is_w in regs:
            if kind == 'dram':
                continue
            lst = self.acc[name]
            w = is_w or kind == 'psum'
            if w:
                lst[:] = [e for e in lst if not (p0 <= e[0] and e[1] <= p1 and b0 <= e[2] and e[3] <= b1)]
                lst.append([p0, p1, b0, b1, opid, True, eng])
            else:
                rep = False
                if not dma:
                    for e in lst:
                        if (not e[5]) and e[6] == eng and e[0] == p0 and e[1] == p1 and e[2] == b0 and e[3] == b1 \
                                and not self.ops[e[4]]['dma']:
                            e[4] = opid
                            rep = True
                            break
                if not rep:
                    lst.append([p0, p1, b0, b1, opid, False, eng])
        if eng == 'pe':
            deps = {d for d in deps if not (self.ops[d]['eng'] == 'pe' and not self.ops[d]['dma'])}
        self.ops.append(dict(eng=eng, fn=fn, deps=deps, dma=dma))
        self.eng_ops[eng].append(opid)
        return opid

    def emit(self, block_engs, sems, dma_sems):
        ops = self.ops
        dma_idx = {}
        cnt = {e: 0 for e in self.eng_ops}
        for i, o in enumerate(ops):
            if o['dma']:
                dma_idx[i] = cnt[o['eng']]
                cnt[o['eng']] += 1
        R = self.DMA_ROT
        needed = set()
        for i, o in enumerate(ops):
            for d in o['deps']:
                if not ops[d]['dma']:
                    needed.add(d)
        signo = {}
        c = {e: 0 for e in self.eng_ops}
        for i, o in enumerate(ops):
            if (not o['dma']) and i in needed:
                c[o['eng']] += 1
                signo[i] = c[o['eng']]
        self.signo = signo
        self.dma_idx = dma_idx
        self.n_waits = 0

    def emit_engine(self, eng, engobj, sems, dma_sems):
        ops = self.ops
        R = self.DMA_ROT
        waited = {}

        def wait(key, sem, val):
            if waited.get(key, 0) >= val:
                return
            waited[key] = val
            engobj.wait_ge(sem, val)
            self.n_waits += 1

        for i in self.eng_ops[eng]:
            o = ops[i]
            for d in sorted(o['deps']):
                od = ops[d]
                if od['dma']:
                    k = self.dma_idx[d]
                    wait(('d', od['eng'], k % R), dma_sems[od['eng']][k % R], 16 * (k // R + 1))
                else:
                    wait(('c', od['eng']), sems[od['eng']], self.signo[d])
            if o['dma']:
                k = self.dma_idx[i]
                if k >= R:
                    wait(('d', eng, k % R), dma_sems[eng][k % R], 16 * (k // R))
                ins = o['fn'](engobj)
                ins.then_inc(dma_sems[eng][k % R], 16)
            else:
                ins = o['fn'](engobj)
                if i in self.signo:
                    ins.then_inc(sems[eng], 1)

    def final_waits(self, eng, engobj, sems, dma_sems, opids):
        R = self.DMA_ROT
        for d in opids:
            od = self.ops[d]
            if od['dma']:
                k = self.dma_idx[d]
                engobj.wait_ge(dma_sems[od['eng']][k % R], 16 * (k // R + 1))
            else:
                engobj.wait_ge(sems[od['eng']], self.signo[d])


D = 1024
SEQ = 2048
NL = 2
IN_W = 5376
A_Q, A_K, A_V = 0, 256, 512
B_Q, B_K, B_V = 768, 1280, 1408
C_Q, C_K, C_V = 1536, 1792, 2048
GATE0 = 2304
DFF = 3072
EPS = 1e-6
NEG = -30000.0
VW = 64
ARENA_BYTES = 104 * 1024
STG_OFF = 96 * 1024
NU_INT, NU_FULL = 22, 14


def _param_layout():
    off = {}
    n = 0
    for name, cnt in (('normg', NL * 2 * 8), ('bgate', NL * 24), ('convw', NL * 3 * 48), ('convb', NL * 48),
                      ('qkg', NL * 6), ('eps', 1)):
        off[name] = n
        n += cnt
    return off, n


POFF, NPAR = _param_layout()


def _pack_params(b_gate, qk_gain, norm_mix, norm_ffn, conv_w, conv_b):
    P = np.zeros((128, NPAR), np.float32)
    for l in range(NL):
        P[:, POFF['normg'] + (l * 2 + 0) * 8:POFF['normg'] + (l * 2 + 0) * 8 + 8] = norm_mix[l].reshape(8, 128).T
        P[:, POFF['normg'] + (l * 2 + 1) * 8:POFF['normg'] + (l * 2 + 1) * 8 + 8] = norm_ffn[l].reshape(8, 128).T
        P[:, POFF['bgate'] + l * 24:POFF['bgate'] + l * 24 + 24] = b_gate[l].reshape(24, 128).T
        for j in range(3):
            o = POFF['convw'] + (l * 3 + j) * 48
            P[:, o:o + 48] = conv_w[l, j].reshape(48, 128).T
        o = POFF['convb'] + l * 48
        P[:, o:o + 48] = conv_b[l].reshape(48, 128).T
        for br in range(3):
            for qk in range(2):
                P[:, POFF['qkg'] + l * 6 + br * 2 + qk] = np.tile(qk_gain[l, br, qk], 2)
    P[:, POFF['eps']] = EPS
    return P


def _rope_tables():
    t = np.arange(SEQ)

    def ang(pos, dim):
        inv = (np.float32(10000.0) ** (-np.arange(0, dim, 2, dtype=np.float32) / np.float32(dim))).astype(np.float32)
        return (pos.astype(np.float32)[:, None] * inv[None, :]).astype(np.float32)

    a1 = ang(t, 64)
    a2 = np.concatenate([ang(t // 64, 32), ang(t % 64, 32)], axis=-1)
    out = []
    for a in (a1, a2):
        idx = (np.arange(128) % 64) % 32
        out.append(np.ascontiguousarray(np.cos(a).astype(np.float32)[:, idx].T))
        out.append(np.ascontiguousarray(np.sin(a).astype(np.float32)[:, idx].T))
    return np.stack(out, 0)


def _const_mats():
    M = np.zeros((5, 128, 128), np.float32)
    M[4, 0, 0:64] = 1.0
    M[4, 32, 64:128] = 1.0
    M[0] = np.eye(128, dtype=np.float32)
    M[1] = 1.0
    M[2, :64, :64] = 1.0
    M[2, 64:, 64:] = 1.0
    for d in range(128):
        if d % 64 < 32:
            M[3, d + 32, d] = -1.0
        else:
            M[3, d - 32, d] = 1.0
    return M


def _band_mask():
    kk = np.arange(128)[:, None]
    qq = np.arange(256)[None, :]
    return np.where((kk <= qq) & (kk >= qq - 128), 0.0, 8.0 * NEG).astype(np.float32)


def _bias_tables(rpb):
    a = (np.arange(128) // 64)[:, None, None]
    cp = (np.arange(128) % 64)[:, None, None]
    c = np.arange(64)[None, None, :]
    cs = np.clip(c - 8, 0, 48)
    col_ok = (cp >= cs) & (cp < cs + 16)
    dc = np.clip(cp - c, -15, 15) + 15
    outs = []
    for (u_lo, nu, lo, hi) in ((-10, NU_INT, -4, 3), (-6, NU_FULL, -7, 7)):
        u = (u_lo + np.arange(nu))[None, :, None]
        dr = a - u
        ok = (dr >= lo) & (dr <= hi) & col_ok
        dri = np.clip(dr + 7, 0, 14)
        g = rpb[:, :, dri, dc]
        g = np.where(ok[None, None], g, np.float32(NEG)).astype(np.float32)
        outs.append(g.reshape(NL, 4, 128, nu * 64))
    return np.ascontiguousarray(np.concatenate(outs, axis=-1))


class Prog:
    def __init__(self, n_layers, stages=None):
        from contextlib import ExitStack
        self.nl = n_layers
        self.stages = stages
        nc = self.nc = bass.Bass("TRN2", target_bir_lowering=False)
        L = n_layers
        dr = lambda name, shape, kind="ExternalInput": nc.dram_tensor(name, shape, F32, kind=kind).ap()
        self.x = dr("x", [SEQ, D])
        self.w_in = dr("w_in", [L, 44, 128, 1024])
        self.w_br = dr("w_branch", [L, 8, 128, 1024])
        self.w_out = dr("w_out", [L, 8, 128, 1024])
        self.w_up = dr("w_up", [L, 48, 128, 1024])
        self.w_down = dr("w_down", [L, 8, 128, 24 * 128])
        self.params = dr("params", [128, NPAR])
        self.rope = dr("rope", [4, 128, SEQ])
        self.cmats = dr("cmats", [5, 128, 128])
        self.band = dr("band", [128, 256])
        self.gtab = dr("gtab", [L, 4, 128, (NU_INT + NU_FULL) * 64])
        self.out = dr("out", [SEQ, D], kind="ExternalOutput")
        self.st = ExitStack()
        E = self.st.enter_context
        self.xT = E(nc.sbuf_tensor("xT", [128, 8, SEQ], F32))
        self.hT = E(nc.sbuf_tensor("hT", [128, 8, SEQ], BF16))
        self.arena = E(nc.sbuf_tensor("arena", [128, ARENA_BYTES // 4], F32))
        self.par = E(nc.sbuf_tensor("par", [128, NPAR], F32))
        self.cm = E(nc.sbuf_tensor("cm", [128, 5, 128], F32))
        self.cmb = E(nc.sbuf_tensor("cmb", [128, 4, 128], BF16))
        self.bandm = E(nc.sbuf_tensor("bandm", [128, 256], BF16))
        self.pp = [E(nc.psum_tensor(f"pp{i}", [128, 1024], F32)) for i in range(4)]
        self.ps = [self.pp[i // 2][:, (i % 2) * 512:(i % 2) * 512 + 512] for i in range(8)]
        self.sems = {e: E(nc.semaphore(f"s_{e}")) for e in ('pe', 'act', 'dve', 'pool', 'sp')}
        self.dsems = {e: [E(nc.semaphore(f"d_{e}{i}")) for i in range(Sched.DMA_ROT)] for e in ('sp', 'pool')}
        self.stg = [self.arena[:, (STG_OFF + 4096 * i) // 4:(STG_OFF + 4096 * (i + 1)) // 4] for i in range(2)]
        self.nstg = 0
        self.ncast = 0
        self.lq = []
        self.inflight = []
        self.S = Sched(nc)
        self.apos = 0
        self.rr = 0

    def alloc(self, shape, dt):
        n = int(np.prod(shape)) * _dsize(dt)
        n = (n + 63) // 64 * 64
        o = self.apos
        assert o + n <= STG_OFF, (o, n)
        self.apos = o + n
        v = self.arena[:, o // 4:(o + n) // 4]
        if dt != F32:
            v = v.bitcast(dt)
        v = v[:, 0:int(np.prod(shape))]
        if len(shape) == 2:
            v = v.rearrange("p (a b) -> p a b", a=shape[0])
        elif len(shape) == 3:
            v = v.rearrange("p (a b c) -> p a b c", a=shape[0], b=shape[1])
        elif len(shape) == 4:
            v = v.rearrange("p (a b c d) -> p a b c d", a=shape[0], b=shape[1], c=shape[2])
        return v

    def mm(self, out, lhsT, rhs, start=True, stop=True, tp=None):
        kw = {} if tp is None else dict(tile_position=tp)
        return self.S.op('pe', lambda e: e.matmul(out, lhsT=lhsT, rhs=rhs, start=start, stop=stop, **kw),
                         reads=[lhsT, rhs], writes=[out])

    def tr(self, out, in_, ident):
        return self.S.op('pe', lambda e: e.transpose(out, in_, ident), reads=[in_, ident], writes=[out])

    def act(self, out, in_, func, bias=None, scale=None):
        kw = {}
        rd = [in_]
        if bias is not None:
            kw['bias'] = bias
            if not isinstance(bias, float):
                rd.append(bias)
        if scale is not None:
            kw['scale'] = scale
            if not isinstance(scale, float):
                rd.append(scale)
        return self.S.op('act', lambda e: e.activation(out=out, in_=in_, func=func, **kw), reads=rd, writes=[out])

    def tt(self, eng, out, in0, in1, op):
        return self.S.op(eng, lambda e: e.tensor_tensor(out=out, in0=in0, in1=in1, op=op), reads=[in0, in1], writes=[out])

    def stt(self, eng, out, in0, scalar, in1, op0, op1):
        rd = [in0, in1] + ([] if isinstance(scalar, float) else [scalar])
        return self.S.op(eng, lambda e: e.scalar_tensor_tensor(out=out, in0=in0, scalar=scalar, in1=in1, op0=op0, op1=op1),
                         reads=rd, writes=[out])

    def cp(self, eng, out, in_):
        if eng == 'act':
            return self.act(out, in_, AF.Copy)
        return self.S.op(eng, lambda e: e.tensor_copy(out=out, in_=in_), reads=[in_], writes=[out])

    def memset(self, eng, out, val):
        return self.S.op(eng, lambda e: e.memset(out, val), writes=[out])

    def recip(self, out, in_):
        return self.S.op('dve', lambda e: e.reciprocal(out=out, in_=in_), reads=[in_], writes=[out])

    def dma(self, eng, out, in_):
        return self.S.op(eng, lambda e: e.dma_start(out=out, in_=in_), reads=[in_], writes=[out], dma=True)

    def pcol(self, name, idx):
        o = POFF[name] + idx
        return self.par[:, o:o + 1]

    def alt(self):
        self.rr += 1
        return 'dve' if self.rr % 2 else 'pool'

    def load_consts(self):
        self.dma('sp', self.par[:], self.params[:, :])
        self.dma('sp', self.cm[:], self.cmats.rearrange("m p n -> p m n"))
        self.cp('dve', self.cmb[:], self.cm[:, 0:4, :])
        self.dma('sp', self.stg[0][:, 0:256], self.band[:, :])
        self.cp('dve', self.bandm[:], self.stg[0][:, 0:256])
        self.ident = self.cm[:, 0, :]
        self.ones_f = self.cm[:, 1, :]
        self.perm_f = self.cm[:, 3, :]
        self.sel_f = self.cm[:, 4, :]
        self.ident_b = self.cmb[:, 0, :]
        self.ones_b = self.cmb[:, 1, :]
        self.bones_b = self.cmb[:, 2, :]
        self.perm_b = self.cmb[:, 3, :]

    def load_x(self):
        self.apos = 0
        xt = [self.alloc([D], F32) for _ in range(2)]
        for t in range(16):
            b = xt[t % 2]
            self.dma('sp', b, self.x[t * 128:(t + 1) * 128, :])
            for half in range(2):
                p = self.ps[(2 * t + half) % 4]
                for j in range(4):
                    c = 4 * half + j
                    self.tr(p[:, j * 128:(j + 1) * 128], b[:, c * 128:(c + 1) * 128], self.ident)
                self.cp('act' if half else 'dve', self.xT[:, 4 * half:4 * half + 4, t * 128:(t + 1) * 128],
                        p[:, :].rearrange("p (j n) -> p j n", j=4))

    def store_x(self):
        self.apos = 0
        ot = [self.alloc([D], F32) for _ in range(2)]
        last = []
        for t in range(16):
            b = ot[t % 2]
            for half in range(2):
                p = self.ps[(2 * t + half) % 4]
                for j in range(4):
                    c = 4 * half + j
                    self.tr(p[:, j * 128:(j + 1) * 128], self.xT[:, c, t * 128:(t + 1) * 128], self.ident)
                self.cp('act' if half else 'dve', b[:, 512 * half:512 * half + 512], p[:, :])
            last.append(self.dma('sp', self.out[t * 128:(t + 1) * 128, :], b))
        return last

    def norm(self, l, which, work):
        sq, rstd = work
        for blk in range(4):
            bs = slice(blk * 512, (blk + 1) * 512)
            pn = self.ps[blk % 2]
            for c in range(8):
                s = sq[c % 2]
                self.act(s, self.xT[:, c, bs], AF.Square)
                self.mm(pn[:, :], self.ones_b, s, start=(c == 0), stop=(c == 7))
            r = rstd[blk % 2]
            self.act(r, pn[:, :], AF.Ln, bias=self.pcol('eps', 0), scale=1.0 / D)
            self.act(r, r, AF.Exp, scale=-0.5)
            for c in range(8):
                self.stt('dve', self.hT[:, c, bs], self.xT[:, c, bs],
                         self.pcol('normg', (l * 2 + which) * 8 + c), r, ALU.mult, ALU.mult)

    def slab_load(self, dst, src, ceng=None):
        d2 = dst.rearrange("p k n -> p (k n)")
        n = d2.shape[1]
        for o in range(0, n, 1024):
            self.lq.append((d2[:, o:o + 1024], src[:, o:o + 1024], ceng))

    def pump(self):
        for (st, d, ceng) in self.inflight:
            self.ncast += 1
            eng = ceng if ceng is not None else ('act' if self.ncast % 2 else 'dve')
            self.cp(eng, d, st)
        self.inflight = []
        while self.lq and len(self.inflight) < 2:
            d, src, ceng = self.lq.pop(0)
            st = self.stg[self.nstg % 2]
            self.nstg += 1
            self.dma('sp', st, src)
            self.inflight.append((st, d, ceng))

    def drain(self):
        while self.lq or self.inflight:
            self.pump()

    def prep_qk(self, l, specs, tabs, work, slabs, gidx, vnext=None):
        w_in = self.w_in
        units = []
        for ci, (dst, cols, qk) in enumerate(specs):
            for blk in range(4):
                units.append((ci, dst, cols, qk, blk))

        def stage1(u):
            ci, dst, cols, qk, blk = units[u]
            sl = slabs[ci % 2]
            if blk == 0:
                if ci == 0:
                    self.slab_load(sl, w_in[l, cols])
                self.drain()
                if ci + 1 < len(specs):
                    self.slab_load(slabs[(ci + 1) % 2], w_in[l, specs[ci + 1][1]])
                elif vnext is not None:
                    self.slab_load(vnext[0], w_in[l, vnext[1]])
            self.pump()
            gain = self.pcol('qkg', l * 6 + gidx * 2 + qk)
            sqb, rstdb, qnb, t1b, t2b = work[u % 2]
            bs = slice(blk * 512, (blk + 1) * 512)
            qp = self.ps[u % 2]
            for kc in range(8):
                self.mm(qp[:, :], sl[:, kc, :], self.hT[:, kc, bs], start=(kc == 0), stop=(kc == 7))
            self.act(sqb, qp[:, :], AF.Square)
            sp_ = self.ps[2 + u % 2]
            self.mm(sp_[:, :], self.bones_b, sqb)
            self.act(rstdb, sp_[:, :], AF.Ln, bias=self.pcol('eps', 0), scale=1.0 / 64)
            self.act(rstdb, rstdb, AF.Exp, scale=-0.5)
            if tabs is None:
                self.stt('dve', dst[:, bs], qp[:, :], gain, rstdb, ALU.mult, ALU.mult)
            else:
                cos, sin = tabs
                self.stt('dve', qnb, qp[:, :], gain, rstdb, ALU.mult, ALU.mult)
                self.tt('dve', t1b, qnb, cos[:, bs], ALU.mult)
                self.tt('dve', t2b, qnb, sin[:, bs], ALU.mult)

        def stage2(u):
            ci, dst, cols, qk, blk = units[u]
            if tabs is None:
                return
            sqb, rstdb, qnb, t1b, t2b = work[u % 2]
            bs = slice(blk * 512, (blk + 1) * 512)
            rp = self.ps[4 + u % 2]
            self.mm(rp[:, :], self.ident_b, t1b, start=True, stop=False)
            self.mm(rp[:, :], self.perm_b, t2b, start=False, stop=True)
            self.cp('act', dst[:, bs], rp[:, :])

        stage1(0)
        for u in range(len(units)):
            if u + 1 < len(units):
                stage1(u + 1)
            stage2(u)

    def calc_vt(self, slab, VT):
        self.drain()
        for blk in range(4):
            bs = slice(blk * 512, (blk + 1) * 512)
            vp = self.ps[4 + blk % 2]
            for kc in range(8):
                self.mm(vp[:, :], slab[:, kc, :], self.hT[:, kc, bs], start=(kc == 0), stop=(kc == 7))
            self.cp('dve' if blk % 2 else 'act', VT[:, bs], vp[:, :])

    def v_tiles(self, VT, vdst4_fn, tok_fn, nkt=16, split=None):
        pbf = self.pp[3].bitcast(BF16)
        for k4 in range(nkt // 4):
            pb = pbf[:, (k4 % 2) * 1024:(k4 % 2) * 1024 + 512]
            for j in range(4):
                self.tr(pb[:, j * 128:(j + 1) * 128], VT[:, tok_fn(4 * k4 + j)], self.ident_b)
            src = pb.rearrange("p (j n) -> p j n", j=4) if split is None else \
                pb.rearrange("p (j g d) -> p j g d", j=4, g=split)
            self.cp('dve' if k4 % 2 else 'act', vdst4_fn(k4), src)

    def finalize(self, o_src_num, o_src_den, dst, osb_den_row, nq):
        rec = osb_den_row
        self.act(rec, o_src_den, AF.Ln)
        self.act(rec, rec, AF.Exp, scale=-1.0)
        bp = self.ps[7]
        self.mm(bp[0:64, 0:nq], self.ones_f[64:65, 0:64], rec)
        self.tt('dve', dst, o_src_num, bp[0:64, 0:nq], ALU.mult)

    def branch_b(self, l, oT):
        VB = 72
        self.apos = 32768
        V = self.alloc([16, 2, VB], BF16)
        VT = self.alloc([SEQ], BF16)
        cos = self.alloc([SEQ], F32)
        sin = self.alloc([SEQ], F32)
        self.dma('sp', cos, self.rope[2])
        self.dma('sp', sin, self.rope[3])
        base = self.apos
        for g in range(2):
            self.apos = base
            qT = self.alloc([2, SEQ], BF16)
            kT = self.alloc([SEQ], BF16)
            mark = self.apos
            work = [(self.alloc([512], BF16), self.alloc([512], F32), self.alloc([512], F32), self.alloc([512], BF16),
                     self.alloc([512], BF16)) for _ in range(2)]
            slabs = [self.alloc([8, 128], BF16) for _ in range(2)]
            specs = [(qT[:, 0, :], 6 + 2 * g, 0),
                     (qT[:, 1, :], 7 + 2 * g, 0),
                     (kT, 42 + g, 1)]
            self.prep_qk(l, specs, (cos, sin), work, slabs, 1, vnext=((slabs[1], 11) if g == 0 else None))
            if g == 0:
                self.calc_vt(slabs[1], VT)
                self.memset('dve', V[:, :, :, 64:65], 1.0)
                self.v_tiles(VT, lambda k4: V[:, 4 * k4:4 * k4 + 4, :, 0:64],
                             lambda kt: slice(kt * 128, (kt + 1) * 128), split=2)
            self.apos = mark
            NP = 3
            P = [self.alloc([1024], BF16) for _ in range(NP)]
            osb = [self.alloc([512], F32) for _ in range(2)]
            rec = [self.alloc([1024], F32) for _ in range(2)]
            tiles = [(hp2, qc, kt) for hp2 in range(2) for qc in range(4) for kt in range(16)]

            def s_mm(i):
                hp2, qc, kt = tiles[i]
                sp_ = self.pp[i % 2]
                for e in range(2):
                    self.mm(sp_[:, 512 * e:512 * e + 512], kT[64 * e:64 * e + 64, kt * 128:(kt + 1) * 128],
                            qT[64 * e:64 * e + 64, hp2, qc * 512:(qc + 1) * 512])

            pending = None
            s_mm(0)
            for i, (hp2, qc, kt) in enumerate(tiles):
                if i + 1 < len(tiles):
                    s_mm(i + 1)
                Pt = P[i % NP]
                self.act(Pt, self.pp[i % 2][:, :], AF.Exp, scale=0.125)
                j = hp2 * 4 + qc
                ob = self.pp[2 + j % 2]
                for e in range(2):
                    self.mm(ob[0:65, 512 * e:512 * e + 512], V[:, kt, g, 0:65], Pt[:, 512 * e:512 * e + 512],
                            start=(kt == 0), stop=(kt == 15))
                if pending is not None and i - pending[0] >= 3:
                    pending[1]()
                    pending = None
                if kt == 15:
                    def fin(ob=ob, k=j % 2, hp2=hp2, qc=qc):
                        r_ = rec[k]
                        for e in range(2):
                            self.act(r_[64:65, 512 * e:512 * e + 512], ob[64:65, 512 * e:512 * e + 512], AF.Ln)
                        self.act(r_[64:65, :], r_[64:65, :], AF.Exp, scale=-1.0)
                        for e in range(2):
                            self.cp('dve', osb[k][64 * e:64 * e + 64, :], ob[0:64, 512 * e:512 * e + 512])
                        for e in range(2):
                            self.mm(ob[64 * e:64 * e + 64, 0:512], self.ones_f[64:65, 0:64], r_[64:65, 512 * e:512 * e + 512],
                                    tp=(64, 64 * e))
                        self.tt('dve', oT[:, 2 + 2 * g + hp2, qc * 512:(qc + 1) * 512], osb[k], ob[:, 0:512], ALU.mult)
                    pending = (i, fin)
            if pending is not None:
                pending[1]()

    def branch_a(self, l, oT):
        self.apos = 32768
        cos = self.alloc([SEQ], F32)
        sin = self.alloc([SEQ], F32)
        self.dma('sp', cos, self.rope[0])
        self.dma('sp', sin, self.rope[1])
        base = self.apos
        for hp in range(2):
            self.apos = base
            qT = self.alloc([SEQ], BF16)
            kT = self.alloc([SEQ], BF16)
            V = self.alloc([3, 16, 2, VW], BF16)
            VT = self.alloc([SEQ], BF16)
            mark = self.apos
            slabs = [self.alloc([8, 128], BF16) for _ in range(2)]
            work = [(self.alloc([512], BF16), self.alloc([512], F32), self.alloc([512], F32), self.alloc([512], BF16),
                     self.alloc([512], BF16)) for _ in range(2)]
            specs = [(qT, hp, 0), (kT, 2 + hp, 1)]
            self.prep_qk(l, specs, (cos, sin), work, slabs, 0, vnext=(slabs[0], 4 + hp))
            self.calc_vt(slabs[0], VT)
            pats = [(1, 64), (4, 64), (16, 64)]
            for p, (dil, rad) in enumerate(pats):
                nts = 16 // dil

                def tok(kt, dil=dil, nts=nts):
                    r, j = kt // nts, kt % nts
                    s0 = r + dil * 128 * j
                    return slice(s0, s0 + dil * 127 + 1, dil)

                self.v_tiles(VT, lambda k4, p=p: V[:, p, 4 * k4:4 * k4 + 4, :, :].rearrange("p k e d -> p k (e d)"), tok)
            self.apos = mark
            oacc = self.alloc([SEQ], F32)
            dacc = self.alloc([SEQ], F32)
            NP = 3
            P = [self.alloc([2, 256], BF16) for _ in range(NP)]
            self.memset('dve', oacc, 0.0)
            self.memset('dve', dacc[0:33, :], 1.0)
            self.memset('dve', dacc[0:1, :], 0.0)
            self.memset('dve', dacc[32:33, :], 0.0)
            for k in range(2):
                self.memset('dve', self.pp[2 + k][0:33, 512:1024], 0.0)
            tl = []
            for p, (dil, rad) in enumerate(pats):
                Lp = SEQ // dil
                nts = 16 // dil
                for kt in range(16):
                    r, j = kt // nts, kt % nts
                    ql0 = max(0, 128 * j - 64)
                    ql1 = min(Lp, 128 * j + 192)
                    nq = ql1 - ql0
                    mo = ql0 - (128 * j - 64)
                    ks0 = r + dil * 128 * j
                    ksl = slice(ks0, ks0 + dil * 127 + 1, dil)
                    qs0 = r + dil * ql0
                    qsl = slice(qs0, qs0 + dil * (nq - 1) + 1, dil)
                    tl.append((p, kt, nq, mo, ksl, qsl))

            def s_stage(i):
                p, kt, nq, mo, ksl, qsl = tl[i]
                sb = self.pp[i % 2]
                for e in range(2):
                    self.mm(sb[:, 512 * e:512 * e + nq], kT[64 * e:64 * e + 64, ksl], qT[64 * e:64 * e + 64, qsl],
                            start=True, stop=False)
                for e in range(2):
                    self.mm(sb[:, 512 * e:512 * e + nq], self.ident_b, self.bandm[:, mo:mo + nq], start=False, stop=True)

            s_stage(0)
            for i, (p, kt, nq, mo, ksl, qsl) in enumerate(tl):
                if i + 1 < len(tl):
                    s_stage(i + 1)
                sb = self.pp[i % 2]
                Pt = P[i % NP]
                self.act(Pt[:, :, 0:nq], sb[:, :].rearrange("p (e n) -> p e n", e=2)[:, :, 0:nq], AF.Exp, scale=0.125)
                ob = self.pp[2 + i % 2]
                for e in range(2):
                    self.mm(ob[64 * e:64 * e + 64, 0:nq], V[:, p, kt, e, 0:64], Pt[:, e, 0:nq], tp=(0, 64 * e))
                for e in range(2):
                    self.mm(ob[32 * e:32 * e + 1, 512:512 + nq], self.ones_b[:, 0:1], Pt[:, e, 0:nq], tp=(0, 32 * e))
                self.tt('dve', oacc[:, qsl], oacc[:, qsl], ob[:, 0:nq], ALU.add)
                self.tt('dve', dacc[0:33, qsl], dacc[0:33, qsl], ob[0:33, 512:512 + nq], ALU.add)
            self.act(dacc[0:33, :], dacc[0:33, :], AF.Ln)
            self.act(dacc[0:33, :], dacc[0:33, :], AF.Exp, scale=-1.0)
            for blk in range(4):
                bs = slice(blk * 512, (blk + 1) * 512)
                bp = self.ps[blk % 2]
                self.mm(bp[:, :], self.sel_f[0:33, :], dacc[0:33, bs])
                self.tt('dve', oT[:, hp, bs], oacc[:, bs], bp[:, :], ALU.mult)

    def branch_c(self, l, oT):
        GW = (NU_INT + NU_FULL) * 64
        for hp in range(2):
            self.apos = 32768
            qT = self.alloc([SEQ], BF16)
            kT = self.alloc([SEQ], BF16)
            V = self.alloc([16, 2, VW], BF16)
            slabs = [self.alloc([8, 128], BF16) for _ in range(2)]
            G = [self.alloc([GW], F32) for _ in range(2)]
            work = [(self.alloc([512], BF16), self.alloc([512], F32), None, None, None) for _ in range(2)]
            NP = 3
            P = [self.alloc([2, 512], BF16) for _ in range(NP)]
            mark_s = self.apos
            VT = self.alloc([SEQ], BF16)
            self.apos = mark_s
            sbf = [self.alloc([2, 512], F32) for _ in range(2)]
            osb = [self.alloc([512], F32) for _ in range(2)]
            rec = [self.alloc([512], F32) for _ in range(2)]
            for e in range(2):
                self.dma('sp', G[e], self.gtab[l, 2 * hp + e])
            specs = [(qT, 12 + hp, 0), (kT, 14 + hp, 1)]
            self.prep_qk(l, specs, None, work, slabs, 2, vnext=(slabs[0], 16 + hp))
            self.calc_vt(slabs[0], VT)
            self.v_tiles(VT, lambda k4: V[:, 4 * k4:4 * k4 + 4, :, :].rearrange("p k e d -> p k (e d)"),
                         lambda kt: slice(kt * 128, (kt + 1) * 128))
            chunks = []
            chunks.append((0, 256, [(j, NU_INT * 64 + (6 - 2 * j) * 64) for j in range(4)]))
            for ii in range(3):
                R0 = 4 + 8 * ii
                chunks.append((64 * R0, 512, [(j, (10 - (2 * j - R0)) * 64) for j in range(4 * ii, 4 * ii + 8)]))
            chunks.append((64 * 28, 256, [(12 + jj, NU_INT * 64 + (10 - 2 * jj) * 64) for jj in range(4)]))
            for k in range(2):
                self.memset('dve', self.pp[2 + k][0:33, 512:1024], 1.0)
            flat = []
            for k, (q0, nq, tl) in enumerate(chunks):
                for ti, (j, goff) in enumerate(tl):
                    flat.append((k, q0, nq, ti, len(tl), j, goff))

            def s_stage(i):
                k, q0, nq, ti, nt, j, goff = flat[i]
                sb = self.pp[i % 2]
                for e in range(2):
                    self.mm(sb[:, 512 * e:512 * e + nq], kT[64 * e:64 * e + 64, j * 128:(j + 1) * 128],
                            qT[64 * e:64 * e + 64, q0:q0 + nq])

            pending = None
            s_stage(0)
            for i, (k, q0, nq, ti, nt, j, goff) in enumerate(flat):
                if i + 1 < len(flat):
                    s_stage(i + 1)
                sb = self.pp[i % 2]
                ob = self.pp[2 + k % 2]
                sf = sbf[i % 2]
                for e in range(2):
                    self.stt('dve', sf[:, e, 0:nq], sb[:, 512 * e:512 * e + nq], 0.125, G[e][:, goff:goff + nq], ALU.mult, ALU.add)
                Pt = P[i % NP]
                self.act(Pt[:, :, 0:nq], sf[:, :, 0:nq], AF.Exp)
                for e in range(2):
                    self.mm(ob[64 * e:64 * e + 64, 0:nq], V[:, j, e, 0:64], Pt[:, e, 0:nq],
                            start=(ti == 0), stop=(ti == nt - 1), tp=(0, 64 * e))
                for e in range(2):
                    self.mm(ob[32 * e:32 * e + 1, 512:512 + nq], self.ones_b[:, 0:1], Pt[:, e, 0:nq],
                            start=(ti == 0), stop=(ti == nt - 1), tp=(0, 32 * e))
                if pending is not None and i - pending[0] >= 2:
                    pending[1]()
                    pending = None
                if ti == nt - 1:
                    def fin(ob=ob, k=k, q0=q0, nq=nq):
                        r_ = rec[k % 2]
                        self.act(r_[0:33, 0:nq], ob[0:33, 512:512 + nq], AF.Ln)
                        self.act(r_[0:33, 0:nq], r_[0:33, 0:nq], AF.Exp, scale=-1.0)
                        self.cp('dve', osb[k % 2][:, 0:nq], ob[:, 0:nq])
                        self.mm(ob[:, 512:512 + nq], self.sel_f[0:33, :], r_[0:33, 0:nq])
                        self.tt('dve', oT[:, 6 + hp, q0:q0 + nq], osb[k % 2][:, 0:nq], ob[:, 512:512 + nq], ALU.mult)
                    pending = (i, fin)
            if pending is not None:
                pending[1]()

    def merge(self, l, oT):
        self.apos = 32768
        mT = self.alloc([8, SEQ], BF16)
        wg = [[self.alloc([8, 128], BF16) for _ in range(3)] for _ in range(2)]
        wb = [self.alloc([8, 128], BF16) for _ in range(2)]
        gsb = [self.alloc([512], F32) for _ in range(3)]
        acc = [self.alloc([512], F32) for _ in range(2)]
        tmp = [self.alloc([512], F32) for _ in range(2)]
        wo = [wg[0][0], wg[0][1]]
        brk = [(0, 2), (2, 6), (6, 8)]

        def load(m):
            for b in range(3):
                self.slab_load(wg[m % 2][b], self.w_in[l, 18 + 8 * b + m])
            self.slab_load(wb[m % 2], self.w_br[l, m])

        load(0)
        self.drain()
        n = 0
        for m in range(8):
            if m + 1 < 8:
                load(m + 1)
            else:
                self.slab_load(wo[0], self.w_out[l, 0])
            for blk in range(4):
                self.pump()
                bs = slice(blk * 512, (blk + 1) * 512)
                for b in range(3):
                    gp = self.ps[n % 3]
                    for kc in range(8):
                        self.mm(gp[:, :], wg[m % 2][b][:, kc, :], self.hT[:, kc, bs], start=(kc == 0), stop=(kc == 7))
                    self.act(gsb[b], gp[:, :], AF.Sigmoid, bias=self.pcol('bgate', l * 24 + b * 8 + m))
                    yp = self.ps[3 + n % 3]
                    k0, k1 = brk[b]
                    for kc in range(k0, k1):
                        self.mm(yp[:, :], wb[m % 2][:, kc, :], oT[:, kc, bs], start=(kc == k0), stop=(kc == k1 - 1))
                    a = acc[(m * 4 + blk) % 2]
                    if b == 0:
                        self.tt('dve', a, gsb[b], yp[:, :], ALU.mult)
                    elif b == 1:
                        self.tt('dve', tmp[0], gsb[b], yp[:, :], ALU.mult)
                        self.tt('dve', a, a, tmp[0], ALU.add)
                    else:
                        self.tt('dve', tmp[1], gsb[b], yp[:, :], ALU.mult)
                        self.tt('dve', mT[:, m, bs], a, tmp[1], ALU.add)
                    n += 1
        self.drain()
        for m in range(8):
            if m + 1 < 8:
                self.slab_load(wo[(m + 1) % 2], self.w_out[l, m + 1])
            for blk in range(4):
                self.pump()
                bs = slice(blk * 512, (blk + 1) * 512)
                op_ = self.ps[6 + (m * 4 + blk) % 2]
                for kc in range(8):
                    self.mm(op_[:, :], wo[m % 2][:, kc, :], mT[:, kc, bs], start=(kc == 0), stop=(kc == 7))
                self.tt('dve', self.xT[:, m, bs], self.xT[:, m, bs], op_[:, :], ALU.add)

    def ffn(self, l):
        self.apos = 0
        sq = [self.alloc([512], BF16) for _ in range(2)]
        rstd = [self.alloc([512], F32) for _ in range(2)]
        self.norm(l, 1, (sq, rstd))
        self.apos = 0
        gT = self.alloc([24, 1024], BF16)
        NU = 1026
        mark_r = self.apos
        rawS = [[self.alloc([NU], F32) for _ in range(2)] for _ in range(2)]
        accS = [[self.alloc([NU], F32) for _ in range(2)] for _ in range(2)]
        wu = [[self.alloc([8, 128], BF16) for _ in range(2)] for _ in range(2)]
        end_ = self.apos
        self.apos = mark_r
        wd = [self.alloc([24, 128], BF16) for _ in range(2)]
        self.apos = end_
        for half in range(2):
            T0 = 1024 * half
            ua = max(0, T0 - 1)
            ub = min(SEQ, T0 + 1025)
            nu = ub - ua
            o0 = T0 - ua
            blocks = [(0, 512), (512, 1024), (1024, nu)] if half == 0 else [(0, 1), (1, 513), (513, nu)]

            def load(fc):
                self.slab_load(wu[fc % 2][0], self.w_up[l, fc])
                self.slab_load(wu[fc % 2][1], self.w_up[l, 24 + fc])

            load(0)
            self.drain()
            for fc in range(24):
                if fc + 1 < 24:
                    load(fc + 1)
                else:
                    self.slab_load(wd[0], self.w_down[l, 0])
                raw, accb = rawS[fc % 2], accS[fc % 2]
                for gv in range(2):
                    self.pump()
                    f = fc + 24 * gv
                    r_, a_ = raw[gv], accb[gv]
                    w0 = self.pcol('convw', (l * 3 + 0) * 48 + f)
                    w2 = self.pcol('convw', (l * 3 + 2) * 48 + f)
                    for bi in range(2):
                        up = self.ps[(gv * 3 + bi) % 6]
                        b0 = 512 * bi
                        for kc in range(8):
                            self.mm(up[:, :], wu[fc % 2][gv][:, kc, :], self.hT[:, kc, T0 + b0:T0 + b0 + 512],
                                    start=(kc == 0), stop=(kc == 7))
                        self.cp('act', r_[:, b0:b0 + 512], up[:, :])
                        self.act(a_[:, b0:b0 + 512], up[:, :], AF.Identity, bias=self.pcol('convb', l * 48 + f),
                                 scale=self.pcol('convw', (l * 3 + 1) * 48 + f))
                    hp_ = self.ps[(gv * 3 + 2) % 6]
                    ht = T0 + 1024 if half == 0 else T0 - 1
                    for kc in range(8):
                        self.mm(hp_[:, 0:1], wu[fc % 2][gv][:, kc, :], self.hT[:, kc, ht:ht + 1], start=(kc == 0), stop=(kc == 7))
                    self.stt('dve', a_[:, 1:1024], r_[:, 0:1023], w0, a_[:, 1:1024], ALU.mult, ALU.add)
                    self.stt('dve', a_[:, 0:1023], r_[:, 1:1024], w2, a_[:, 0:1023], ALU.mult, ALU.add)
                    if half == 0:
                        self.stt('dve', a_[:, 1023:1024], hp_[:, 0:1], w2, a_[:, 1023:1024], ALU.mult, ALU.add)
                    else:
                        self.stt('dve', a_[:, 0:1], hp_[:, 0:1], w0, a_[:, 0:1], ALU.mult, ALU.add)
                self.act(accb[0][:, 0:1024], accb[0][:, 0:1024], AF.Gelu_apprx_tanh)
                self.tt('dve', gT[:, fc, :], accb[0][:, 0:1024], accb[1][:, 0:1024], ALU.mult)
            self.drain()
            for m in range(8):
                if m + 1 < 8:
                    self.slab_load(wd[(m + 1) % 2], self.w_down[l, m + 1])
                for blk in range(2):
                    self.pump()
                    dp = self.ps[6 + (m * 2 + blk) % 2]
                    for fc in range(24):
                        self.mm(dp[:, :], wd[m % 2][:, fc, :], gT[:, fc, blk * 512:(blk + 1) * 512], start=(fc == 0), stop=(fc == 23))
                    ts = slice(T0 + blk * 512, T0 + (blk + 1) * 512)
                    self.tt('dve', self.xT[:, m, ts], self.xT[:, m, ts], dp[:, :], ALU.add)

    def layer(self, l):
        st = self.stages
        self.apos = 32768
        sq = [self.alloc([512], BF16) for _ in range(2)]
        rstd = [self.alloc([512], F32) for _ in range(2)]
        self.norm(l, 0, (sq, rstd))
        self.apos = 0
        oT = self.alloc([8, SEQ], BF16)
        if st is None or 'a' in st:
            self.branch_a(l, oT)
        if st is None or 'b' in st:
            self.branch_b(l, oT)
        if st is None or 'c' in st:
            self.branch_c(l, oT)
        if st is not None and 'dump_o' in st:
            return oT
        if st is None or 'm' in st:
            self.merge(l, oT)
        if st is None or 'f' in st:
            self.ffn(l)
        return None

    def build(self):
        self.load_consts()
        self.load_x()
        for l in range(self.nl):
            self.layer(l)
        last = self.store_x()
        S = self.S
        S.emit(None, self.sems, self.dsems)
        nc = self.nc
        sems, dsems = self.sems, self.dsems
        with nc.Block() as block:
            @block.tensor
            def _(e):
                S.emit_engine('pe', e, sems, dsems)

            @block.scalar
            def _(e):
                S.emit_engine('act', e, sems, dsems)

            @block.vector
            def _(e):
                S.emit_engine('dve', e, sems, dsems)

            @block.gpsimd
            def _(e):
                S.emit_engine('pool', e, sems, dsems)

            @block.sync
            def _(e):
                S.emit_engine('sp', e, sems, dsems)
                S.final_waits('sp', e, sems, dsems, last)
        self.st.close()
        return nc


_CONSTS = None


def _consts():
    global _CONSTS
    if _CONSTS is None:
        _CONSTS = dict(rope=_rope_tables(), cmats=_const_mats(), band=_band_mask())
    return _CONSTS


def _tile_k(w):
    L, K, N = w.shape
    t = np.asarray(w, np.float32).reshape(L, K // 128, 128, N // 128, 128).transpose(0, 3, 2, 1, 4)
    return np.ascontiguousarray(t).reshape(L, N // 128, 128, (K // 128) * 128)


def _tile_w_in(w_in):
    t = _tile_k(w_in)
    L = t.shape[0]
    dups = []
    for g in range(2):
        wk = np.asarray(w_in, np.float32)[:, :, B_K + g * 64:B_K + (g + 1) * 64]
        wk = wk.reshape(L, 8, 128, 64).transpose(0, 2, 1, 3)
        dups.append(np.concatenate([wk, wk], axis=-1).reshape(L, 1, 128, 1024))
    return np.ascontiguousarray(np.concatenate([t] + dups, axis=1))


def _run(nl, x, w_in, w_branch, w_out, w_up, w_down, params, gtab, stages=None):
    prog = Prog(nl, stages)
    nc = prog.build()
    c = _consts()
    f = lambda a: np.ascontiguousarray(a, dtype=np.float32)
    shared = dict(w_in=_tile_w_in(w_in), w_branch=_tile_k(w_branch), w_out=_tile_k(w_out), w_up=_tile_k(w_up),
                  w_down=_tile_k(w_down),
                  params=f(params), rope=c['rope'], cmats=c['cmats'], band=c['band'], gtab=f(gtab))
    nb = x.shape[0]
    in_maps = [dict(shared, x=f(x[b])) for b in range(nb)]
    res = run_bass_kernel_spmd(nc, in_maps, core_ids=list(range(nb)))
    return np.stack([r["out"] for r in res.results], 0)


def kernel(x, w_in, b_gate, qk_gain, rel_pos_bias, w_branch, w_out, norm_mix, norm_ffn, w_up, conv_w, conv_b, w_down):
    x = np.asarray(x, np.float32)
    params = _pack_params(np.asarray(b_gate), np.asarray(qk_gain), np.asarray(norm_mix), np.asarray(norm_ffn),
                          np.asarray(conv_w), np.asarray(conv_b))
    gtab = _bias_tables(np.asarray(rel_pos_bias, np.float32))
    return _run(NL, x, w_in, w_branch, w_out, w_up, w_down, params, gtab).astype(np.float32)
```

```python
import numpy as np
import concourse.bass as bass
import concourse.mybir as mybir
from concourse.bass_utils import run_bass_kernel_spmd

F32 = mybir.dt.float32
BF16 = mybir.dt.bfloat16
ALU = mybir.AluOpType
AF = mybir.ActivationFunctionType
DT_SIZE = {F32: 4, BF16: 2}


def _dsize(dt):
    return DT_SIZE[dt]


class Sched:
    DMA_ROT = 6

    def __init__(self, nc):
        self.nc = nc
        self.ops = []
        self.acc = {}
        self.eng_ops = {e: [] for e in ('pe', 'act', 'dve', 'pool', 'sp')}

    @staticmethod
    def region(ap, whole=False):
        t = ap.tensor
        name = t.name
        space = str(ap.space)
        if 'DRAM' in space.upper() or 'HBM' in space.upper() or type(t).__name__.startswith('DRam'):
            return (name, 0, 1 << 30, 0, 1 << 40, 'dram')
        pat = ap.ap
        pstep, pcnt = pat[0]
        off = int(ap.offset)
        es = _dsize(ap.dtype)
        if pstep == 0:
            p0, fo = 0, off
            pcnt = 1
        else:
            p0, fo = divmod(off, pstep)
        ext = 0
        for st, cnt in pat[1:]:
            ext += abs(st) * (cnt - 1)
        b0 = fo * es
        b1 = (fo + ext + 1) * es
        kind = 'psum' if 'PSUM' in space.upper() or type(t).__name__.startswith('PSum') else 'sbuf'
        if kind == 'psum':
            return (name, 0, 128, (b0 // 2048) * 2048, ((b1 + 2047) // 2048) * 2048, kind)
        return (name, p0, p0 + pcnt, b0, b1, kind)

    def op(self, eng, fn, reads=(), writes=(), dma=False):
        opid = len(self.ops)
        deps = set()
        regs = [(self.region(a), False) for a in reads] + [(self.region(a), True) for a in writes]
        for (name, p0, p1, b0, b1, kind), is_w in regs:
            if kind == 'dram':
                continue
            lst = self.acc.setdefault(name, [])
            w = is_w or kind == 'psum'
            for e in lst:
                if e[0] < p1 and p0 < e[1] and e[2] < b1 and b0 < e[3] and (w or e[5]):
                    if e[4] != opid:
                        deps.add(e[4])
        for (name, p0, p1, b0, b1, kind), is_w in regs:
            if kind == 'dram':
                continue
            lst = self.acc[name]
            w = is_w or kind == 'psum'
            if w:
                lst[:] = [e for e in lst if not (p0 <= e[0] and e[1] <= p1 and b0 <= e[2] and e[3] <= b1)]
                lst.append([p0, p1, b0, b1, opid, True, eng])
            else:
                rep = False
                if not dma:
                    for e in lst:
                        if (not e[5]) and e[6] == eng and e[0] == p0 and e[1] == p1 and e[2] == b0 and e[3] == b1 \
                                and not self.ops[e[4]]['dma']:
                            e[4] = opid
                            rep = True
                            break
                if not rep:
                    lst.append([p0, p1, b0, b1, opid, False, eng])
        if eng == 'pe':
            deps = {d for d in deps if not (self.ops[d]['eng'] == 'pe' and not self.ops[d]['dma'])}
        self.ops.append(dict(eng=eng, fn=fn, deps=deps, dma=dma))
        self.eng_ops[eng].append(opid)
        return opid

    def emit(self, block_engs, sems, dma_sems):
        ops = self.ops
        dma_idx = {}
        cnt = {e: 0 for e in self.eng_ops}
        for i, o in enumerate(ops):
            if o['dma']:
                dma_idx[i] = cnt[o['eng']]
                cnt[o['eng']] += 1
        R = self.DMA_ROT
        needed = set()
        for i, o in enumerate(ops):
            for d in o['deps']:
                if not ops[d]['dma']:
                    needed.add(d)
        signo = {}
        c = {e: 0 for e in self.eng_ops}
        for i, o in enumerate(ops):
            if (not o['dma']) and i in needed:
                c[o['eng']] += 1
                signo[i] = c[o['eng']]
        self.signo = signo
        self.dma_idx = dma_idx
        self.n_waits = 0

    def emit_engine(self, eng, engobj, sems, dma_sems):
        ops = self.ops
        R = self.DMA_ROT
        waited = {}

        def wait(key, sem, val):
            if waited.get(key, 0) >= val:
                return
            waited[key] = val
            engobj.wait_ge(sem, val)
            self.n_waits += 1

        for i in self.eng_ops[eng]:
            o = ops[i]
            for d in sorted(o['deps']):
                od = ops[d]
                if od['dma']:
                    k = self.dma_idx[d]
                    wait(('d', od['eng'], k % R), dma_sems[od['eng']][k % R], 16 * (k // R + 1))
                else:
                    wait(('c', od['eng']), sems[od['eng']], self.signo[d])
            if o['dma']:
                k = self.dma_idx[i]
                if k >= R:
                    wait(('d', eng, k % R), dma_sems[eng][k % R], 16 * (k // R))
                ins = o['fn'](engobj)
                ins.then_inc(dma_sems[eng][k % R], 16)
            else:
                ins = o['fn'](engobj)
                if i in self.signo:
                    ins.then_inc(sems[eng], 1)

    def final_waits(self, eng, engobj, sems, dma_sems, opids):
        R = self.DMA_ROT
        for d in opids:
            od = self.ops[d]
            if od['dma']:
                k = self.dma_idx[d]
                engobj.wait_ge(dma_sems[od['eng']][k % R], 16 * (k // R + 1))
            else:
                engobj.wait_ge(sems[od['eng']], self.signo[d])


D = 1024
SEQ = 2048
NL = 2
IN_W = 5376
A_Q, A_K, A_V = 0, 256, 512
B_Q, B_K, B_V = 768, 1280, 1408
C_Q, C_K, C_V = 1536, 1792, 2048
GATE0 = 2304
DFF = 3072
EPS = 1e-6
NEG = -30000.0
VW = 64
ARENA_BYTES = 104 * 1024
STG_OFF = 96 * 1024
NU_INT, NU_FULL = 22, 14


def _param_layout():
    off = {}
    n = 0
    for name, cnt in (('normg', NL * 2 * 8), ('bgate', NL * 24), ('convw', NL * 3 * 48), ('convb', NL * 48),
                      ('qkg', NL * 6), ('eps', 1)):
        off[name] = n
        n += cnt
    return off, n


POFF, NPAR = _param_layout()


def _pack_params(b_gate, qk_gain, norm_mix, norm_ffn, conv_w, conv_b):
    P = np.zeros((128, NPAR), np.float32)
    for l in range(NL):
        P[:, POFF['normg'] + (l * 2 + 0) * 8:POFF['normg'] + (l * 2 + 0) * 8 + 8] = norm_mix[l].reshape(8, 128).T
        P[:, POFF['normg'] + (l * 2 + 1) * 8:POFF['normg'] + (l * 2 + 1) * 8 + 8] = norm_ffn[l].reshape(8, 128).T
        P[:, POFF['bgate'] + l * 24:POFF['bgate'] + l * 24 + 24] = b_gate[l].reshape(24, 128).T
        for j in range(3):
            o = POFF['convw'] + (l * 3 + j) * 48
            P[:, o:o + 48] = conv_w[l, j].reshape(48, 128).T
        o = POFF['convb'] + l * 48
        P[:, o:o + 48] = conv_b[l].reshape(48, 128).T
        for br in range(3):
            for qk in range(2):
                P[:, POFF['qkg'] + l * 6 + br * 2 + qk] = np.tile(qk_gain[l, br, qk], 2)
    P[:, POFF['eps']] = EPS
    return P


def _rope_tables():
    t = np.arange(SEQ)

    def ang(pos, dim):
        inv = (np.float32(10000.0) ** (-np.arange(0, dim, 2, dtype=np.float32) / np.float32(dim))).astype(np.float32)
        return (pos.astype(np.float32)[:, None] * inv[None, :]).astype(np.float32)

    a1 = ang(t, 64)
    a2 = np.concatenate([ang(t // 64, 32), ang(t % 64, 32)], axis=-1)
    out = []
    for a in (a1, a2):
        idx = (np.arange(128) % 64) % 32
        out.append(np.ascontiguousarray(np.cos(a).astype(np.float32)[:, idx].T))
        out.append(np.ascontiguousarray(np.sin(a).astype(np.float32)[:, idx].T))
    return np.stack(out, 0)


def _const_mats():
    M = np.zeros((5, 128, 128), np.float32)
    M[4, 0, 0:64] = 1.0
    M[4, 32, 64:128] = 1.0
    M[0] = np.eye(128, dtype=np.float32)
    M[1] = 1.0
    M[2, :64, :64] = 1.0
    M[2, 64:, 64:] = 1.0
    for d in range(128):
        if d % 64 < 32:
            M[3, d + 32, d] = -1.0
        else:
            M[3, d - 32, d] = 1.0
    return M


def _band_mask():
    kk = np.arange(128)[:, None]
    qq = np.arange(256)[None, :]
    return np.where((kk <= qq) & (kk >= qq - 128), 0.0, 8.0 * NEG).astype(np.float32)


def _bias_tables(rpb):
    a = (np.arange(128) // 64)[:, None, None]
    cp = (np.arange(128) % 64)[:, None, None]
    c = np.arange(64)[None, None, :]
    cs = np.clip(c - 8, 0, 48)
    col_ok = (cp >= cs) & (cp < cs + 16)
    dc = np.clip(cp - c, -15, 15) + 15
    outs = []
    for (u_lo, nu, lo, hi) in ((-10, NU_INT, -4, 3), (-6, NU_FULL, -7, 7)):
        u = (u_lo + np.arange(nu))[None, :, None]
        dr = a - u
        ok = (dr >= lo) & (dr <= hi) & col_ok
        dri = np.clip(dr + 7, 0, 14)
        g = rpb[:, :, dri, dc]
        g = np.where(ok[None, None], g, np.float32(NEG)).astype(np.float32)
        outs.append(g.reshape(NL, 4, 128, nu * 64))
    return np.ascontiguousarray(np.concatenate(outs, axis=-1))


class Prog:
    def __init__(self, n_layers, stages=None):
        from contextlib import ExitStack
        self.nl = n_layers
        self.stages = stages
        nc = self.nc = bass.Bass("TRN2", target_bir_lowering=False)
        L = n_layers
        dr = lambda name, shape, kind="ExternalInput": nc.dram_tensor(name, shape, F32, kind=kind).ap()
        self.x = dr("x", [SEQ, D])
        self.w_in = dr("w_in", [L, 44, 128, 1024])
        self.w_br = dr("w_branch", [L, 8, 128, 1024])
        self.w_out = dr("w_out", [L, 8, 128, 1024])
        self.w_up = dr("w_up", [L, 48, 128, 1024])
        self.w_down = dr("w_down", [L, 8, 128, 24 * 128])
        self.params = dr("params", [128, NPAR])
        self.rope = dr("rope", [4, 128, SEQ])
        self.cmats = dr("cmats", [5, 128, 128])
        self.band = dr("band", [128, 256])
        self.gtab = dr("gtab", [L, 4, 128, (NU_INT + NU_FULL) * 64])
        self.out = dr("out", [SEQ, D], kind="ExternalOutput")
        self.st = ExitStack()
        E = self.st.enter_context
        self.xT = E(nc.sbuf_tensor("xT", [128, 8, SEQ], F32))
        self.hT = E(nc.sbuf_tensor("hT", [128, 8, SEQ], BF16))
        self.arena = E(nc.sbuf_tensor("arena", [128, ARENA_BYTES // 4], F32))
        self.par = E(nc.sbuf_tensor("par", [128, NPAR], F32))
        self.cm = E(nc.sbuf_tensor("cm", [128, 5, 128], F32))
        self.cmb = E(nc.sbuf_tensor("cmb", [128, 4, 128], BF16))
        self.bandm = E(nc.sbuf_tensor("bandm", [128, 256], BF16))
        self.pp = [E(nc.psum_tensor(f"pp{i}", [128, 1024], F32)) for i in range(4)]
        self.ps = [self.pp[i // 2][:, (i % 2) * 512:(i % 2) * 512 + 512] for i in range(8)]
        self.sems = {e: E(nc.semaphore(f"s_{e}")) for e in ('pe', 'act', 'dve', 'pool', 'sp')}
        self.dsems = {e: [E(nc.semaphore(f"d_{e}{i}")) for i in range(Sched.DMA_ROT)] for e in ('sp', 'pool')}
        self.stg = [self.arena[:, (STG_OFF + 4096 * i) // 4:(STG_OFF + 4096 * (i + 1)) // 4] for i in range(2)]
        self.nstg = 0
        self.ncast = 0
        self.lq = []
        self.inflight = []
        self.S = Sched(nc)
        self.apos = 0
        self.rr = 0

    def alloc(self, shape, dt):
        n = int(np.prod(shape)) * _dsize(dt)
        n = (n + 63) // 64 * 64
        o = self.apos
        assert o + n <= STG_OFF, (o, n)
        self.apos = o + n
        v = self.arena[:, o // 4:(o + n) // 4]
        if dt != F32:
            v = v.bitcast(dt)
        v = v[:, 0:int(np.prod(shape))]
        if len(shape) == 2:
            v = v.rearrange("p (a b) -> p a b", a=shape[0])
        elif len(shape) == 3:
            v = v.rearrange("p (a b c) -> p a b c", a=shape[0], b=shape[1])
        elif len(shape) == 4:
            v = v.rearrange("p (a b c d) -> p a b c d", a=shape[0], b=shape[1], c=shape[2])
        return v

    def mm(self, out, lhsT, rhs, start=True, stop=True, tp=None):
        kw = {} if tp is None else dict(tile_position=tp)
        return self.S.op('pe', lambda e: e.matmul(out, lhsT=lhsT, rhs=rhs, start=start, stop=stop, **kw),
                         reads=[lhsT, rhs], writes=[out])

    def tr(self, out, in_, ident):
        return self.S.op('pe', lambda e: e.transpose(out, in_, ident), reads=[in_, ident], writes=[out])

    def act(self, out, in_, func, bias=None, scale=None):
        kw = {}
        rd = [in_]
        if bias is not None:
            kw['bias'] = bias
            if not isinstance(bias, float):
                rd.append(bias)
        if scale is not None:
            kw['scale'] = scale
            if not isinstance(scale, float):
                rd.append(scale)
        return self.S.op('act', lambda e: e.activation(out=out, in_=in_, func=func, **kw), reads=rd, writes=[out])

    def tt(self, eng, out, in0, in1, op):
        return self.S.op(eng, lambda e: e.tensor_tensor(out=out, in0=in0, in1=in1, op=op), reads=[in0, in1], writes=[out])

    def stt(self, eng, out, in0, scalar, in1, op0, op1):
        rd = [in0, in1] + ([] if isinstance(scalar, float) else [scalar])
        return self.S.op(eng, lambda e: e.scalar_tensor_tensor(out=out, in0=in0, scalar=scalar, in1=in1, op0=op0, op1=op1),
                         reads=rd, writes=[out])

    def cp(self, eng, out, in_):
        if eng == 'act':
            return self.act(out, in_, AF.Copy)
        return self.S.op(eng, lambda e: e.tensor_copy(out=out, in_=in_), reads=[in_], writes=[out])

    def memset(self, eng, out, val):
        return self.S.op(eng, lambda e: e.memset(out, val), writes=[out])

    def recip(self, out, in_):
        return self.S.op('dve', lambda e: e.reciprocal(out=out, in_=in_), reads=[in_], writes=[out])

    def dma(self, eng, out, in_):
        return self.S.op(eng, lambda e: e.dma_start(out=out, in_=in_), reads=[in_], writes=[out], dma=True)

    def pcol(self, name, idx):
        o = POFF[name] + idx
        return self.par[:, o:o + 1]

    def alt(self):
        self.rr += 1
        return 'dve' if self.rr % 2 else 'pool'

    def load_consts(self):
        self.dma('sp', self.par[:], self.params[:, :])
        self.dma('sp', self.cm[:], self.cmats.rearrange("m p n -> p m n"))
        self.cp('dve', self.cmb[:], self.cm[:, 0:4, :])
        self.dma('sp', self.stg[0][:, 0:256], self.band[:, :])
        self.cp('dve', self.bandm[:], self.stg[0][:, 0:256])
        self.ident = self.cm[:, 0, :]
        self.ones_f = self.cm[:, 1, :]
        self.perm_f = self.cm[:, 3, :]
        self.sel_f = self.cm[:, 4, :]
        self.ident_b = self.cmb[:, 0, :]
        self.ones_b = self.cmb[:, 1, :]
        self.bones_b = self.cmb[:, 2, :]
        self.perm_b = self.cmb[:, 3, :]

    def load_x(self):
        self.apos = 0
        xt = [self.alloc([D], F32) for _ in range(2)]
        for t in range(16):
            b = xt[t % 2]
            self.dma('sp', b, self.x[t * 128:(t + 1) * 128, :])
            for half in range(2):
                p = self.ps[(2 * t + half) % 4]
                for j in range(4):
                    c = 4 * half + j
                    self.tr(p[:, j * 128:(j + 1) * 128], b[:, c * 128:(c + 1) * 128], self.ident)
                self.cp('act' if half else 'dve', self.xT[:, 4 * half:4 * half + 4, t * 128:(t + 1) * 128],
                        p[:, :].rearrange("p (j n) -> p j n", j=4))

    def store_x(self):
        self.apos = 0
        ot = [self.alloc([D], F32) for _ in range(2)]
        last = []
        for t in range(16):
            b = ot[t % 2]
            for half in range(2):
                p = self.ps[(2 * t + half) % 4]
                for j in range(4):
                    c = 4 * half + j
                    self.tr(p[:, j * 128:(j + 1) * 128], self.xT[:, c, t * 128:(t + 1) * 128], self.ident)
                self.cp('act' if half else 'dve', b[:, 512 * half:512 * half + 512], p[:, :])
            last.append(self.dma('sp', self.out[t * 128:(t + 1) * 128, :], b))
        return last

    def norm(self, l, which, work):
        sq, rstd = work
        for blk in range(4):
            bs = slice(blk * 512, (blk + 1) * 512)
            pn = self.ps[blk % 2]
            for c in range(8):
                s = sq[c % 2]
                self.act(s, self.xT[:, c, bs], AF.Square)
                self.mm(pn[:, :], self.ones_b, s, start=(c == 0), stop=(c == 7))
            r = rstd[blk % 2]
            self.act(r, pn[:, :], AF.Ln, bias=self.pcol('eps', 0), scale=1.0 / D)
            self.act(r, r, AF.Exp, scale=-0.5)
            for c in range(8):
                self.stt('dve', self.hT[:, c, bs], self.xT[:, c, bs],
                         self.pcol('normg', (l * 2 + which) * 8 + c), r, ALU.mult, ALU.mult)

    def slab_load(self, dst, src, ceng=None):
        d2 = dst.rearrange("p k n -> p (k n)")
        n = d2.shape[1]
        for o in range(0, n, 1024):
            self.lq.append((d2[:, o:o + 1024], src[:, o:o + 1024], ceng))

    def pump(self):
        for (st, d, ceng) in self.inflight:
            self.ncast += 1
            eng = ceng if ceng is not None else ('act' if self.ncast % 2 else 'dve')
            self.cp(eng, d, st)
        self.inflight = []
        while self.lq and len(self.inflight) < 2:
            d, src, ceng = self.lq.pop(0)
            st = self.stg[self.nstg % 2]
            self.nstg += 1
            self.dma('sp', st, src)
            self.inflight.append((st, d, ceng))

    def drain(self):
        while self.lq or self.inflight:
            self.pump()

    def prep_qk(self, l, specs, tabs, work, slabs, gidx, vnext=None):
        w_in = self.w_in
        units = []
        for ci, (dst, cols, qk) in enumerate(specs):
            for blk in range(4):
                units.append((ci, dst, cols, qk, blk))

        def stage1(u):
            ci, dst, cols, qk, blk = units[u]
            sl = slabs[ci % 2]
            if blk == 0:
                if ci == 0:
                    self.slab_load(sl, w_in[l, cols])
                self.drain()
                if ci + 1 < len(specs):
                    self.slab_load(slabs[(ci + 1) % 2], w_in[l, specs[ci + 1][1]])
                elif vnext is not None:
                    self.slab_load(vnext[0], w_in[l, vnext[1]])
            self.pump()
            gain = self.pcol('qkg', l * 6 + gidx * 2 + qk)
            sqb, rstdb, qnb, t1b, t2b = work[u % 2]
            bs = slice(blk * 512, (blk + 1) * 512)
            qp = self.ps[u % 2]
            for kc in range(8):
                self.mm(qp[:, :], sl[:, kc, :], self.hT[:, kc, bs], start=(kc == 0), stop=(kc == 7))
            self.act(sqb, qp[:, :], AF.Square)
            sp_ = self.ps[2 + u % 2]
            self.mm(sp_[:, :], self.bones_b, sqb)
            self.act(rstdb, sp_[:, :], AF.Ln, bias=self.pcol('eps', 0), scale=1.0 / 64)
            self.act(rstdb, rstdb, AF.Exp, scale=-0.5)
            if tabs is None:
                self.stt('dve', dst[:, bs], qp[:, :], gain, rstdb, ALU.mult, ALU.mult)
            else:
                cos, sin = tabs
                self.stt('dve', qnb, qp[:, :], gain, rstdb, ALU.mult, ALU.mult)
                self.tt('dve', t1b, qnb, cos[:, bs], ALU.mult)
                self.tt('dve', t2b, qnb, sin[:, bs], ALU.mult)

        def stage2(u):
            ci, dst, cols, qk, blk = units[u]
            if tabs is None:
                return
            sqb, rstdb, qnb, t1b, t2b = work[u % 2]
            bs = slice(blk * 512, (blk + 1) * 512)
            rp = self.ps[4 + u % 2]
            self.mm(rp[:, :], self.ident_b, t1b, start=True, stop=False)
            self.mm(rp[:, :], self.perm_b, t2b, start=False, stop=True)
            self.cp('act', dst[:, bs], rp[:, :])

        stage1(0)
        for u in range(len(units)):
            if u + 1 < len(units):
                stage1(u + 1)
            stage2(u)

    def calc_vt(self, slab, VT):
        self.drain()
        for blk in range(4):
            bs = slice(blk * 512, (blk + 1) * 512)
            vp = self.ps[4 + blk % 2]
            for kc in range(8):
                self.mm(vp[:, :], slab[:, kc, :], self.hT[:, kc, bs], start=(kc == 0), stop=(kc == 7))
            self.cp('dve' if blk % 2 else 'act', VT[:, bs], vp[:, :])

    def v_tiles(self, VT, vdst4_fn, tok_fn, nkt=16, split=None):
        pbf = self.pp[3].bitcast(BF16)
        for k4 in range(nkt // 4):
            pb = pbf[:, (k4 % 2) * 1024:(k4 % 2) * 1024 + 512]
            for j in range(4):
                self.tr(pb[:, j * 128:(j + 1) * 128], VT[:, tok_fn(4 * k4 + j)], self.ident_b)
            src = pb.rearrange("p (j n) -> p j n", j=4) if split is None else \
                pb.rearrange("p (j g d) -> p j g d", j=4, g=split)
            self.cp('dve' if k4 % 2 else 'act', vdst4_fn(k4), src)

    def finalize(self, o_src_num, o_src_den, dst, osb_den_row, nq):
        rec = osb_den_row
        self.act(rec, o_src_den, AF.Ln)
        self.act(rec, rec, AF.Exp, scale=-1.0)
        bp = self.ps[7]
        self.mm(bp[0:64, 0:nq], self.ones_f[64:65, 0:64], rec)
        self.tt('dve', dst, o_src_num, bp[0:64, 0:nq], ALU.mult)

    def branch_b(self, l, oT):
        VB = 128
        self.apos = 32768
        V = self.alloc([16, 2, VB], BF16)
        VT = self.alloc([SEQ], BF16)
        cos = self.alloc([SEQ], F32)
        sin = self.alloc([SEQ], F32)
        self.dma('sp', cos, self.rope[2])
        self.dma('sp', sin, self.rope[3])
        base = self.apos
        for g in range(2):
            self.apos = base
            qT = self.alloc([2, SEQ], BF16)
            kT = self.alloc([SEQ], BF16)
            mark = self.apos
            work = [(self.alloc([512], BF16), self.alloc([512], F32), self.alloc([512], F32), self.alloc([512], BF16),
                     self.alloc([512], BF16)) for _ in range(2)]
            slabs = [self.alloc([8, 128], BF16) for _ in range(2)]
            specs = [(qT[:, 0, :], 6 + 2 * g, 0),
                     (qT[:, 1, :], 7 + 2 * g, 0),
                     (kT, 42 + g, 1)]
            self.prep_qk(l, specs, (cos, sin), work, slabs, 1, vnext=((slabs[1], 11) if g == 0 else None))
            if g == 0:
                self.calc_vt(slabs[1], VT)
                self.memset('dve', V[:, :, :, 64:128], 1.0)
                self.v_tiles(VT, lambda k4: V[:, 4 * k4:4 * k4 + 4, :, 0:64],
                             lambda kt: slice(kt * 128, (kt + 1) * 128), split=2)
            self.apos = mark
            NP = 3
            P = [self.alloc([1024], BF16) for _ in range(NP)]
            rden = [[self.alloc([512], F32) for _ in range(2)] for _ in range(2)]
            rdlo = [[self.alloc([512], F32) for _ in range(2)] for _ in range(2)]
            tiles = [(hp2, qc, kt) for hp2 in range(2) for qc in range(4) for kt in range(16)]

            def s_mm(i):
                hp2, qc, kt = tiles[i]
                sp_ = self.pp[i % 2]
                for e in range(2):
                    self.mm(sp_[:, 512 * e:512 * e + 512], kT[64 * e:64 * e + 64, kt * 128:(kt + 1) * 128],
                            qT[64 * e:64 * e + 64, hp2, qc * 512:(qc + 1) * 512])

            pending = None
            s_mm(0)
            for i, (hp2, qc, kt) in enumerate(tiles):
                if i + 1 < len(tiles):
                    s_mm(i + 1)
                Pt = P[i % NP]
                self.act(Pt, self.pp[i % 2][:, :], AF.Exp, scale=0.125)
                j = hp2 * 4 + qc
                ob = self.pp[2 + j % 2]
                for e in range(2):
                    self.mm(ob[:, 512 * e:512 * e + 512], V[:, kt, g, :], Pt[:, 512 * e:512 * e + 512],
                            start=(kt == 0), stop=(kt == 15))
                if pending is not None and i - pending[0] >= 3:
                    pending[1]()
                    pending = None
                if kt == 15:
                    def fin(ob=ob, k=j % 2, hp2=hp2, qc=qc):
                        for e in range(2):
                            r_ = rden[k][e]
                            self.act(r_[64:128, :], ob[64:128, 512 * e:512 * e + 512], AF.Ln)
                            self.act(r_[64:128, :], r_[64:128, :], AF.Exp, scale=-1.0)
                            self.cp('dve', rdlo[k][e][0:64, :], r_[64:128, :])
                            self.tt('dve', oT[64 * e:64 * e + 64, 2 + 2 * g + hp2, qc * 512:(qc + 1) * 512],
                                    rdlo[k][e][0:64, :], ob[0:64, 512 * e:512 * e + 512], ALU.mult)
                    pending = (i, fin)
            if pending is not None:
                pending[1]()

    def branch_a(self, l, oT):
        self.apos = 32768
        cos = self.alloc([SEQ], F32)
        sin = self.alloc([SEQ], F32)
        self.dma('sp', cos, self.rope[0])
        self.dma('sp', sin, self.rope[1])
        base = self.apos
        for hp in range(2):
            self.apos = base
            qT = self.alloc([SEQ], BF16)
            kT = self.alloc([SEQ], BF16)
            V = self.alloc([3, 16, 2, VW], BF16)
            VT = self.alloc([SEQ], BF16)
            mark = self.apos
            slabs = [self.alloc([8, 128], BF16) for _ in range(2)]
            work = [(self.alloc([512], BF16), self.alloc([512], F32), self.alloc([512], F32), self.alloc([512], BF16),
                     self.alloc([512], BF16)) for _ in range(2)]
            specs = [(qT, hp, 0), (kT, 2 + hp, 1)]
            self.prep_qk(l, specs, (cos, sin), work, slabs, 0, vnext=(slabs[0], 4 + hp))
            self.calc_vt(slabs[0], VT)
            pats = [(1, 64), (4, 64), (16, 64)]
            for p, (dil, rad) in enumerate(pats):
                nts = 16 // dil

                def tok(kt, dil=dil, nts=nts):
                    r, j = kt // nts, kt % nts
                    s0 = r + dil * 128 * j
                    return slice(s0, s0 + dil * 127 + 1, dil)

                self.v_tiles(VT, lambda k4, p=p: V[:, p, 4 * k4:4 * k4 + 4, :, :].rearrange("p k e d -> p k (e d)"), tok)
            self.apos = mark
            oacc = self.alloc([SEQ], F32)
            dacc = self.alloc([SEQ], F32)
            NP = 3
            P = [self.alloc([2, 256], BF16) for _ in range(NP)]
            self.memset('dve', oacc, 0.0)
            self.memset('dve', dacc[0:33, :], 1.0)
            self.memset('dve', dacc[0:1, :], 0.0)
            self.memset('dve', dacc[32:33, :], 0.0)
            for k in range(2):
                self.memset('dve', self.pp[2 + k][0:33, 512:1024], 0.0)
            tl = []
            for p, (dil, rad) in enumerate(pats):
                Lp = SEQ // dil
                nts = 16 // dil
                for kt in range(16):
                    r, j = kt // nts, kt % nts
                    ql0 = max(0, 128 * j - 64)
                    ql1 = min(Lp, 128 * j + 192)
                    nq = ql1 - ql0
                    mo = ql0 - (128 * j - 64)
                    ks0 = r + dil * 128 * j
                    ksl = slice(ks0, ks0 + dil * 127 + 1, dil)
                    qs0 = r + dil * ql0
                    qsl = slice(qs0, qs0 + dil * (nq - 1) + 1, dil)
                    tl.append((p, kt, nq, mo, ksl, qsl))

            def s_stage(i):
                p, kt, nq, mo, ksl, qsl = tl[i]
                sb = self.pp[i % 2]
                for e in range(2):
                    self.mm(sb[:, 512 * e:512 * e + nq], kT[64 * e:64 * e + 64, ksl], qT[64 * e:64 * e + 64, qsl],
                            start=True, stop=False)
                for e in range(2):
                    self.mm(sb[:, 512 * e:512 * e + nq], self.ident_b, self.bandm[:, mo:mo + nq], start=False, stop=True)

            s_stage(0)
            for i, (p, kt, nq, mo, ksl, qsl) in enumerate(tl):
                if i + 1 < len(tl):
                    s_stage(i + 1)
                sb = self.pp[i % 2]
                Pt = P[i % NP]
                self.act(Pt[:, :, 0:nq], sb[:, :].rearrange("p (e n) -> p e n", e=2)[:, :, 0:nq], AF.Exp, scale=0.125)
                ob = self.pp[2 + i % 2]
                for e in range(2):
                    self.mm(ob[64 * e:64 * e + 64, 0:nq], V[:, p, kt, e, 0:64], Pt[:, e, 0:nq], tp=(0, 64 * e))
                for e in range(2):
                    self.mm(ob[32 * e:32 * e + 1, 512:512 + nq], self.ones_b[:, 0:1], Pt[:, e, 0:nq], tp=(0, 32 * e))
                self.tt('dve', oacc[:, qsl], oacc[:, qsl], ob[:, 0:nq], ALU.add)
                self.tt('dve', dacc[0:33, qsl], dacc[0:33, qsl], ob[0:33, 512:512 + nq], ALU.add)
            self.act(dacc[0:33, :], dacc[0:33, :], AF.Ln)
            self.act(dacc[0:33, :], dacc[0:33, :], AF.Exp, scale=-1.0)
            for blk in range(4):
                bs = slice(blk * 512, (blk + 1) * 512)
                bp = self.ps[blk % 2]
                self.mm(bp[:, :], self.sel_f[0:33, :], dacc[0:33, bs])
                self.tt('dve', oT[:, hp, bs], oacc[:, bs], bp[:, :], ALU.mult)

    def branch_c(self, l, oT):
        GW = (NU_INT + NU_FULL) * 64
        for hp in range(2):
            self.apos = 32768
            qT = self.alloc([SEQ], BF16)
            kT = self.alloc([SEQ], BF16)
            V = self.alloc([16, 2, VW], BF16)
            slabs = [self.alloc([8, 128], BF16) for _ in range(2)]
            G = [self.alloc([GW], F32) for _ in range(2)]
            work = [(self.alloc([512], BF16), self.alloc([512], F32), None, None, None) for _ in range(2)]
            NP = 3
            P = [self.alloc([2, 512], BF16) for _ in range(NP)]
            mark_s = self.apos
            VT = self.alloc([SEQ], BF16)
            self.apos = mark_s
            sbf = [self.alloc([2, 512], F32) for _ in range(2)]
            osb = [self.alloc([512], F32) for _ in range(2)]
            rec = [self.alloc([512], F32) for _ in range(2)]
            for e in range(2):
                self.dma('sp', G[e], self.gtab[l, 2 * hp + e])
            specs = [(qT, 12 + hp, 0), (kT, 14 + hp, 1)]
            self.prep_qk(l, specs, None, work, slabs, 2, vnext=(slabs[0], 16 + hp))
            self.calc_vt(slabs[0], VT)
            self.v_tiles(VT, lambda k4: V[:, 4 * k4:4 * k4 + 4, :, :].rearrange("p k e d -> p k (e d)"),
                         lambda kt: slice(kt * 128, (kt + 1) * 128))
            chunks = []
            chunks.append((0, 256, [(j, NU_INT * 64 + (6 - 2 * j) * 64) for j in range(4)]))
            for ii in range(3):
                R0 = 4 + 8 * ii
                chunks.append((64 * R0, 512, [(j, (10 - (2 * j - R0)) * 64) for j in range(4 * ii, 4 * ii + 8)]))
            chunks.append((64 * 28, 256, [(12 + jj, NU_INT * 64 + (10 - 2 * jj) * 64) for jj in range(4)]))
            for k in range(2):
                self.memset('dve', self.pp[2 + k][0:33, 512:1024], 1.0)
            flat = []
            for k, (q0, nq, tl) in enumerate(chunks):
                for ti, (j, goff) in enumerate(tl):
                    flat.append((k, q0, nq, ti, len(tl), j, goff))

            def s_stage(i):
                k, q0, nq, ti, nt, j, goff = flat[i]
                sb = self.pp[i % 2]
                for e in range(2):
                    self.mm(sb[:, 512 * e:512 * e + nq], kT[64 * e:64 * e + 64, j * 128:(j + 1) * 128],
                            qT[64 * e:64 * e + 64, q0:q0 + nq])

            pending = None
            s_stage(0)
            for i, (k, q0, nq, ti, nt, j, goff) in enumerate(flat):
                if i + 1 < len(flat):
                    s_stage(i + 1)
                sb = self.pp[i % 2]
                ob = self.pp[2 + k % 2]
                sf = sbf[i % 2]
                for e in range(2):
                    self.stt('dve', sf[:, e, 0:nq], sb[:, 512 * e:512 * e + nq], 0.125, G[e][:, goff:goff + nq], ALU.mult, ALU.add)
                Pt = P[i % NP]
                self.act(Pt[:, :, 0:nq], sf[:, :, 0:nq], AF.Exp)
                for e in range(2):
                    self.mm(ob[64 * e:64 * e + 64, 0:nq], V[:, j, e, 0:64], Pt[:, e, 0:nq],
                            start=(ti == 0), stop=(ti == nt - 1), tp=(0, 64 * e))
                for e in range(2):
                    self.mm(ob[32 * e:32 * e + 1, 512:512 + nq], self.ones_b[:, 0:1], Pt[:, e, 0:nq],
                            start=(ti == 0), stop=(ti == nt - 1), tp=(0, 32 * e))
                if pending is not None and i - pending[0] >= 2:
                    pending[1]()
                    pending = None
                if ti == nt - 1:
                    def fin(ob=ob, k=k, q0=q0, nq=nq):
                        r_ = rec[k % 2]
                        self.act(r_[0:33, 0:nq], ob[0:33, 512:512 + nq], AF.Ln)
                        self.act(r_[0:33, 0:nq], r_[0:33, 0:nq], AF.Exp, scale=-1.0)
                        self.cp('dve', osb[k % 2][:, 0:nq], ob[:, 0:nq])
                        self.mm(ob[:, 512:512 + nq], self.sel_f[0:33, :], r_[0:33, 0:nq])
                        self.tt('dve', oT[:, 6 + hp, q0:q0 + nq], osb[k % 2][:, 0:nq], ob[:, 512:512 + nq], ALU.mult)
                    pending = (i, fin)
            if pending is not None:
                pending[1]()

    def merge(self, l, oT):
        self.apos = 32768
        mT = self.alloc([8, SEQ], BF16)
        wg = [[self.alloc([8, 128], BF16) for _ in range(3)] for _ in range(2)]
        wb = [self.alloc([8, 128], BF16) for _ in range(2)]
        gsb = [self.alloc([512], F32) for _ in range(3)]
        acc = [self.alloc([512], F32) for _ in range(2)]
        tmp = [self.alloc([512], F32) for _ in range(2)]
        wo = [wg[0][0], wg[0][1]]
        brk = [(0, 2), (2, 6), (6, 8)]

        def load(m):
            for b in range(3):
                self.slab_load(wg[m % 2][b], self.w_in[l, 18 + 8 * b + m])
            self.slab_load(wb[m % 2], self.w_br[l, m])

        load(0)
        self.drain()
        n = 0
        for m in range(8):
            if m + 1 < 8:
                load(m + 1)
            else:
                self.slab_load(wo[0], self.w_out[l, 0])
            for blk in range(4):
                self.pump()
                bs = slice(blk * 512, (blk + 1) * 512)
                for b in range(3):
                    gp = self.ps[n % 3]
                    for kc in range(8):
                        self.mm(gp[:, :], wg[m % 2][b][:, kc, :], self.hT[:, kc, bs], start=(kc == 0), stop=(kc == 7))
                    self.act(gsb[b], gp[:, :], AF.Sigmoid, bias=self.pcol('bgate', l * 24 + b * 8 + m))
                    yp = self.ps[3 + n % 3]
                    k0, k1 = brk[b]
                    for kc in range(k0, k1):
                        self.mm(yp[:, :], wb[m % 2][:, kc, :], oT[:, kc, bs], start=(kc == k0), stop=(kc == k1 - 1))
                    a = acc[(m * 4 + blk) % 2]
                    if b == 0:
                        self.tt('dve', a, gsb[b], yp[:, :], ALU.mult)
                    elif b == 1:
                        self.tt('dve', tmp[0], gsb[b], yp[:, :], ALU.mult)
                        self.tt('dve', a, a, tmp[0], ALU.add)
                    else:
                        self.tt('dve', tmp[1], gsb[b], yp[:, :], ALU.mult)
                        self.tt('dve', mT[:, m, bs], a, tmp[1], ALU.add)
                    n += 1
        self.drain()
        for m in range(8):
            if m + 1 < 8:
                self.slab_load(wo[(m + 1) % 2], self.w_out[l, m + 1])
            for blk in range(4):
                self.pump()
                bs = slice(blk * 512, (blk + 1) * 512)
                op_ = self.ps[6 + (m * 4 + blk) % 2]
                for kc in range(8):
                    self.mm(op_[:, :], wo[m % 2][:, kc, :], mT[:, kc, bs], start=(kc == 0), stop=(kc == 7))
                self.tt('dve', self.xT[:, m, bs], self.xT[:, m, bs], op_[:, :], ALU.add)

    def ffn(self, l):
        self.apos = 0
        sq = [self.alloc([512], BF16) for _ in range(2)]
        rstd = [self.alloc([512], F32) for _ in range(2)]
        self.norm(l, 1, (sq, rstd))
        self.apos = 0
        gT = self.alloc([24, 1024], BF16)
        NU = 1026
        mark_r = self.apos
        rawS = [[self.alloc([NU], F32) for _ in range(2)] for _ in range(2)]
        accS = [[self.alloc([NU], F32) for _ in range(2)] for _ in range(2)]
        wu = [[self.alloc([8, 128], BF16) for _ in range(2)] for _ in range(2)]
        end_ = self.apos
        self.apos = mark_r
        wd = [self.alloc([24, 128], BF16) for _ in range(2)]
        self.apos = end_
        for half in range(2):
            T0 = 1024 * half
            ua = max(0, T0 - 1)
            ub = min(SEQ, T0 + 1025)
            nu = ub - ua
            o0 = T0 - ua
            blocks = [(0, 512), (512, 1024), (1024, nu)] if half == 0 else [(0, 1), (1, 513), (513, nu)]

            def load(fc):
                self.slab_load(wu[fc % 2][0], self.w_up[l, fc])
                self.slab_load(wu[fc % 2][1], self.w_up[l, 24 + fc])

            load(0)
            self.drain()
            for fc in range(24):
                if fc + 1 < 24:
                    load(fc + 1)
                else:
                    self.slab_load(wd[0], self.w_down[l, 0])
                raw, accb = rawS[fc % 2], accS[fc % 2]
                for gv in range(2):
                    self.pump()
                    f = fc + 24 * gv
                    r_, a_ = raw[gv], accb[gv]
                    w0 = self.pcol('convw', (l * 3 + 0) * 48 + f)
                    w2 = self.pcol('convw', (l * 3 + 2) * 48 + f)
                    for bi in range(2):
                        up = self.ps[(gv * 3 + bi) % 6]
                        b0 = 512 * bi
                        for kc in range(8):
                            self.mm(up[:, :], wu[fc % 2][gv][:, kc, :], self.hT[:, kc, T0 + b0:T0 + b0 + 512],
                                    start=(kc == 0), stop=(kc == 7))
                        self.cp('act', r_[:, b0:b0 + 512], up[:, :])
                        self.act(a_[:, b0:b0 + 512], up[:, :], AF.Identity, bias=self.pcol('convb', l * 48 + f),
                                 scale=self.pcol('convw', (l * 3 + 1) * 48 + f))
                    hp_ = self.ps[(gv * 3 + 2) % 6]
                    ht = T0 + 1024 if half == 0 else T0 - 1
                    for kc in range(8):
                        self.mm(hp_[:, 0:1], wu[fc % 2][gv][:, kc, :], self.hT[:, kc, ht:ht + 1], start=(kc == 0), stop=(kc == 7))
                    self.stt('dve', a_[:, 1:1024], r_[:, 0:1023], w0, a_[:, 1:1024], ALU.mult, ALU.add)
                    self.stt('dve', a_[:, 0:1023], r_[:, 1:1024], w2, a_[:, 0:1023], ALU.mult, ALU.add)
                    if half == 0:
                        self.stt('dve', a_[:, 1023:1024], hp_[:, 0:1], w2, a_[:, 1023:1024], ALU.mult, ALU.add)
                    else:
                        self.stt('dve', a_[:, 0:1], hp_[:, 0:1], w0, a_[:, 0:1], ALU.mult, ALU.add)
                self.act(accb[0][:, 0:1024], accb[0][:, 0:1024], AF.Gelu_apprx_tanh)
                self.tt('dve', gT[:, fc, :], accb[0][:, 0:1024], accb[1][:, 0:1024], ALU.mult)
            self.drain()
            for m in range(8):
                if m + 1 < 8:
                    self.slab_load(wd[(m + 1) % 2], self.w_down[l, m + 1])
                for blk in range(2):
                    self.pump()
                    dp = self.ps[6 + (m * 2 + blk) % 2]
                    for fc in range(24):
                        self.mm(dp[:, :], wd[m % 2][:, fc, :], gT[:, fc, blk * 512:(blk + 1) * 512], start=(fc == 0), stop=(fc == 23))
                    ts = slice(T0 + blk * 512, T0 + (blk + 1) * 512)
                    self.tt('dve', self.xT[:, m, ts], self.xT[:, m, ts], dp[:, :], ALU.add)

    def layer(self, l):
        st = self.stages
        self.apos = 32768
        sq = [self.alloc([512], BF16) for _ in range(2)]
        rstd = [self.alloc([512], F32) for _ in range(2)]
        self.norm(l, 0, (sq, rstd))
        self.apos = 0
        oT = self.alloc([8, SEQ], BF16)
        if st is None or 'a' in st:
            self.branch_a(l, oT)
        if st is None or 'b' in st:
            self.branch_b(l, oT)
        if st is None or 'c' in st:
            self.branch_c(l, oT)
        if st is not None and 'dump_o' in st:
            return oT
        if st is None or 'm' in st:
            self.merge(l, oT)
        if st is None or 'f' in st:
            self.ffn(l)
        return None

    def build(self):
        self.load_consts()
        self.load_x()
        for l in range(self.nl):
            self.layer(l)
        last = self.store_x()
        S = self.S
        S.emit(None, self.sems, self.dsems)
        nc = self.nc
        sems, dsems = self.sems, self.dsems
        with nc.Block() as block:
            @block.tensor
            def _(e):
                S.emit_engine('pe', e, sems, dsems)

            @block.scalar
            def _(e):
                S.emit_engine('act', e, sems, dsems)

            @block.vector
            def _(e):
                S.emit_engine('dve', e, sems, dsems)

            @block.gpsimd
            def _(e):
                S.emit_engine('pool', e, sems, dsems)

            @block.sync
            def _(e):
                S.emit_engine('sp', e, sems, dsems)
                S.final_waits('sp', e, sems, dsems, last)
        self.st.close()
        return nc


_CONSTS = None


def _consts():
    global _CONSTS
    if _CONSTS is None:
        _CONSTS = dict(rope=_rope_tables(), cmats=_const_mats(), band=_band_mask())
    return _CONSTS


def _tile_k(w):
    L, K, N = w.shape
    t = np.asarray(w, np.float32).reshape(L, K // 128, 128, N // 128, 128).transpose(0, 3, 2, 1, 4)
    return np.ascontiguousarray(t).reshape(L, N // 128, 128, (K // 128) * 128)


def _tile_w_in(w_in):
    t = _tile_k(w_in)
    L = t.shape[0]
    dups = []
    for g in range(2):
        wk = np.asarray(w_in, np.float32)[:, :, B_K + g * 64:B_K + (g + 1) * 64]
        wk = wk.reshape(L, 8, 128, 64).transpose(0, 2, 1, 3)
        dups.append(np.concatenate([wk, wk], axis=-1).reshape(L, 1, 128, 1024))
    return np.ascontiguousarray(np.concatenate([t] + dups, axis=1))


def _run(nl, x, w_in, w_branch, w_out, w_up, w_down, params, gtab, stages=None):
    prog = Prog(nl, stages)
    nc = prog.build()
    c = _consts()
    f = lambda a: np.ascontiguousarray(a, dtype=np.float32)
    shared = dict(w_in=_tile_w_in(w_in), w_branch=_tile_k(w_branch), w_out=_tile_k(w_out), w_up=_tile_k(w_up),
                  w_down=_tile_k(w_down),
                  params=f(params), rope=c['rope'], cmats=c['cmats'], band=c['band'], gtab=f(gtab))
    nb = x.shape[0]
    in_maps = [dict(shared, x=f(x[b])) for b in range(nb)]
    res = run_bass_kernel_spmd(nc, in_maps, core_ids=list(range(nb)))
    return np.stack([r["out"] for r in res.results], 0)


def kernel(x, w_in, b_gate, qk_gain, rel_pos_bias, w_branch, w_out, norm_mix, norm_ffn, w_up, conv_w, conv_b, w_down):
    x = np.asarray(x, np.float32)
    params = _pack_params(np.asarray(b_gate), np.asarray(qk_gain), np.asarray(norm_mix), np.asarray(norm_ffn),
                          np.asarray(conv_w), np.asarray(conv_b))
    gtab = _bias_tables(np.asarray(rel_pos_bias, np.float32))
    return _run(NL, x, w_in, w_branch, w_out, w_up, w_down, params, gtab).astype(np.float32)
```

```python
import numpy as np
import concourse.bass as bass
import concourse.mybir as mybir
from concourse.bass_utils import run_bass_kernel_spmd

F32 = mybir.dt.float32
BF16 = mybir.dt.bfloat16
ALU = mybir.AluOpType
AF = mybir.ActivationFunctionType
DT_SIZE = {F32: 4, BF16: 2}


def _dsize(dt):
    return DT_SIZE[dt]


class Sched:
    DMA_ROT = 6

    def __init__(self, nc):
        self.nc = nc
        self.ops = []
        self.acc = {}
        self.eng_ops = {e: [] for e in ('pe', 'act', 'dve', 'pool', 'sp')}

    @staticmethod
    def region(ap, whole=False):
        t = ap.tensor
        name = t.name
        space = str(ap.space)
        if 'DRAM' in space.upper() or 'HBM' in space.upper() or type(t).__name__.startswith('DRam'):
            return (name, 0, 1 << 30, 0, 1 << 40, 'dram')
        pat = ap.ap
        pstep, pcnt = pat[0]
        off = int(ap.offset)
        es = _dsize(ap.dtype)
        if pstep == 0:
            p0, fo = 0, off
            pcnt = 1
        else:
            p0, fo = divmod(off, pstep)
        ext = 0
        for st, cnt in pat[1:]:
            ext += abs(st) * (cnt - 1)
        b0 = fo * es
        b1 = (fo + ext + 1) * es
        kind = 'psum' if 'PSUM' in space.upper() or type(t).__name__.startswith('PSum') else 'sbuf'
        if kind == 'psum':
            return (name, 0, 128, (b0 // 2048) * 2048, ((b1 + 2047) // 2048) * 2048, kind)
        return (name, p0, p0 + pcnt, b0, b1, kind)

    def op(self, eng, fn, reads=(), writes=(), dma=False):
        opid = len(self.ops)
        deps = set()
        regs = [(self.region(a), False) for a in reads] + [(self.region(a), True) for a in writes]
        for (name, p0, p1, b0, b1, kind), is_w in regs:
            if kind == 'dram':
                continue
            lst = self.acc.setdefault(name, [])
            w = is_w or kind == 'psum'
            for e in lst:
                if e[0] < p1 and p0 < e[1] and e[2] < b1 and b0 < e[3] and (w or e[5]):
                    if e[4] != opid:
                        deps.add(e[4])
        for (name, p0, p1, b0, b1, kind), is_w in regs:
            if kind == 'dram':
                continue
            lst = self.acc[name]
            w = is_w or kind == 'psum'
            if w:
                lst[:] = [e for e in lst if not (p0 <= e[0] and e[1] <= p1 and b0 <= e[2] and e[3] <= b1)]
                lst.append([p0, p1, b0, b1, opid, True, eng])
            else:
                rep = False
                if not dma:
                    for e in lst:
                        if (not e[5]) and e[6] == eng and e[0] == p0 and e[1] == p1 and e[2] == b0 and e[3] == b1 \
                                and not self.ops[e[4]]['dma']:
                            e[4] = opid
                            rep = True
                            break
                if not rep:
                    lst.append([p0, p1, b0, b1, opid, False, eng])
        if eng == 'pe':
            deps = {d for d in deps if not (self.ops[d]['eng'] == 'pe' and not self.ops[d]['dma'])}
        self.ops.append(dict(eng=eng, fn=fn, deps=deps, dma=dma))
        self.eng_ops[eng].append(opid)
        return opid

    def emit(self, block_engs, sems, dma_sems):
        ops = self.ops
        dma_idx = {}
        cnt = {e: 0 for e in self.eng_ops}
        for i, o in enumerate(ops):
            if o['dma']:
                dma_idx[i] = cnt[o['eng']]
                cnt[o['eng']] += 1
        R = self.DMA_ROT
        needed = set()
        for i, o in enumerate(ops):
            for d in o['deps']:
                if not ops[d]['dma']:
                    needed.add(d)
        signo = {}
        c = {e: 0 for e in self.eng_ops}
        for i, o in enumerate(ops):
            if (not o['dma']) and i in needed:
                c[o['eng']] += 1
                signo[i] = c[o['eng']]
        self.signo = signo
        self.dma_idx = dma_idx
        self.n_waits = 0

    def emit_engine(self, eng, engobj, sems, dma_sems):
        ops = self.ops
        R = self.DMA_ROT
        waited = {}

        def wait(key, sem, val):
            if waited.get(key, 0) >= val:
                return
            waited[key] = val
            engobj.wait_ge(sem, val)
            self.n_waits += 1

        for i in self.eng_ops[eng]:
            o = ops[i]
            for d in sorted(o['deps']):
                od = ops[d]
                if od['dma']:
                    k = self.dma_idx[d]
                    wait(('d', od['eng'], k % R), dma_sems[od['eng']][k % R], 16 * (k // R + 1))
                else:
                    wait(('c', od['eng']), sems[od['eng']], self.signo[d])
            if o['dma']:
                k = self.dma_idx[i]
                if k >= R:
                    wait(('d', eng, k % R), dma_sems[eng][k % R], 16 * (k // R))
                ins = o['fn'](engobj)
                ins.then_inc(dma_sems[eng][k % R], 16)
            else:
                ins = o['fn'](engobj)
                if i in self.signo:
                    ins.then_inc(sems[eng], 1)

    def final_waits(self, eng, engobj, sems, dma_sems, opids):
        R = self.DMA_ROT
        for d in opids:
            od = self.ops[d]
            if od['dma']:
                k = self.dma_idx[d]
                engobj.wait_ge(dma_sems[od['eng']][k % R], 16 * (k // R + 1))
            else:
                engobj.wait_ge(sems[od['eng']], self.signo[d])


D = 1024
SEQ = 2048
NL = 2
IN_W = 5376
A_Q, A_K, A_V = 0, 256, 512
B_Q, B_K, B_V = 768, 1280, 1408
C_Q, C_K, C_V = 1536, 1792, 2048
GATE0 = 2304
DFF = 3072
EPS = 1e-6
NEG = -30000.0
VW = 64
ARENA_BYTES = 104 * 1024
STG_OFF = 96 * 1024
NU_INT, NU_FULL = 22, 14


def _param_layout():
    off = {}
    n = 0
    for name, cnt in (('normg', NL * 2 * 8), ('bgate', NL * 24), ('convw', NL * 3 * 48), ('convb', NL * 48),
                      ('qkg', NL * 6), ('eps', 1)):
        off[name] = n
        n += cnt
    return off, n


POFF, NPAR = _param_layout()


def _pack_params(b_gate, qk_gain, norm_mix, norm_ffn, conv_w, conv_b):
    P = np.zeros((128, NPAR), np.float32)
    for l in range(NL):
        P[:, POFF['normg'] + (l * 2 + 0) * 8:POFF['normg'] + (l * 2 + 0) * 8 + 8] = norm_mix[l].reshape(8, 128).T
        P[:, POFF['normg'] + (l * 2 + 1) * 8:POFF['normg'] + (l * 2 + 1) * 8 + 8] = norm_ffn[l].reshape(8, 128).T
        P[:, POFF['bgate'] + l * 24:POFF['bgate'] + l * 24 + 24] = b_gate[l].reshape(24, 128).T
        for j in range(3):
            o = POFF['convw'] + (l * 3 + j) * 48
            P[:, o:o + 48] = conv_w[l, j].reshape(48, 128).T
        o = POFF['convb'] + l * 48
        P[:, o:o + 48] = conv_b[l].reshape(48, 128).T
        for br in range(3):
            for qk in range(2):
                P[:, POFF['qkg'] + l * 6 + br * 2 + qk] = np.tile(qk_gain[l, br, qk], 2)
    P[:, POFF['eps']] = EPS
    return P


def _rope_tables():
    t = np.arange(SEQ)

    def ang(pos, dim):
        inv = (np.float32(10000.0) ** (-np.arange(0, dim, 2, dtype=np.float32) / np.float32(dim))).astype(np.float32)
        return (pos.astype(np.float32)[:, None] * inv[None, :]).astype(np.float32)

    a1 = ang(t, 64)
    a2 = np.concatenate([ang(t // 64, 32), ang(t % 64, 32)], axis=-1)
    out = []
    for a in (a1, a2):
        idx = (np.arange(128) % 64) % 32
        out.append(np.ascontiguousarray(np.cos(a).astype(np.float32)[:, idx].T))
        out.append(np.ascontiguousarray(np.sin(a).astype(np.float32)[:, idx].T))
    return np.stack(out, 0)


def _const_mats():
    M = np.zeros((5, 128, 128), np.float32)
    M[4, 0, 0:64] = 1.0
    M[4, 32, 64:128] = 1.0
    M[0] = np.eye(128, dtype=np.float32)
    M[1] = 1.0
    M[2, :64, :64] = 1.0
    M[2, 64:, 64:] = 1.0
    for d in range(128):
        if d % 64 < 32:
            M[3, d + 32, d] = -1.0
        else:
            M[3, d - 32, d] = 1.0
    return M


def _band_mask():
    kk = np.arange(128)[:, None]
    qq = np.arange(256)[None, :]
    return np.where((kk <= qq) & (kk >= qq - 128), 0.0, 8.0 * NEG).astype(np.float32)


def _bias_tables(rpb):
    a = (np.arange(128) // 64)[:, None, None]
    cp = (np.arange(128) % 64)[:, None, None]
    c = np.arange(64)[None, None, :]
    cs = np.clip(c - 8, 0, 48)
    col_ok = (cp >= cs) & (cp < cs + 16)
    dc = np.clip(cp - c, -15, 15) + 15
    outs = []
    for (u_lo, nu, lo, hi) in ((-10, NU_INT, -4, 3), (-6, NU_FULL, -7, 7)):
        u = (u_lo + np.arange(nu))[None, :, None]
        dr = a - u
        ok = (dr >= lo) & (dr <= hi) & col_ok
        dri = np.clip(dr + 7, 0, 14)
        g = rpb[:, :, dri, dc]
        g = np.where(ok[None, None], g, np.float32(NEG)).astype(np.float32)
        outs.append(g.reshape(NL, 4, 128, nu * 64))
    return np.ascontiguousarray(np.concatenate(outs, axis=-1))


class Prog:
    def __init__(self, n_layers, stages=None):
        from contextlib import ExitStack
        self.nl = n_layers
        self.stages = stages
        nc = self.nc = bass.Bass("TRN2", target_bir_lowering=False)
        L = n_layers
        dr = lambda name, shape, kind="ExternalInput": nc.dram_tensor(name, shape, F32, kind=kind).ap()
        self.x = dr("x", [SEQ, D])
        self.w_in = dr("w_in", [L, 44, 128, 1024])
        self.w_br = dr("w_branch", [L, 8, 128, 1024])
        self.w_out = dr("w_out", [L, 8, 128, 1024])
        self.w_up = dr("w_up", [L, 48, 128, 1024])
        self.w_down = dr("w_down", [L, 8, 128, 24 * 128])
        self.params = dr("params", [128, NPAR])
        self.rope = dr("rope", [4, 128, SEQ])
        self.cmats = dr("cmats", [5, 128, 128])
        self.band = dr("band", [128, 256])
        self.gtab = dr("gtab", [L, 4, 128, (NU_INT + NU_FULL) * 64])
        self.out = dr("out", [SEQ, D], kind="ExternalOutput")
        self.st = ExitStack()
        E = self.st.enter_context
        self.xT = E(nc.sbuf_tensor("xT", [128, 8, SEQ], F32))
        self.hT = E(nc.sbuf_tensor("hT", [128, 8, SEQ], BF16))
        self.arena = E(nc.sbuf_tensor("arena", [128, ARENA_BYTES // 4], F32))
        self.par = E(nc.sbuf_tensor("par", [128, NPAR], F32))
        self.cm = E(nc.sbuf_tensor("cm", [128, 5, 128], F32))
        self.cmb = E(nc.sbuf_tensor("cmb", [128, 4, 128], BF16))
        self.bandm = E(nc.sbuf_tensor("bandm", [128, 256], BF16))
        self.pp = [E(nc.psum_tensor(f"pp{i}", [128, 1024], F32)) for i in range(4)]
        self.ps = [self.pp[i // 2][:, (i % 2) * 512:(i % 2) * 512 + 512] for i in range(8)]
        self.sems = {e: E(nc.semaphore(f"s_{e}")) for e in ('pe', 'act', 'dve', 'pool', 'sp')}
        self.dsems = {e: [E(nc.semaphore(f"d_{e}{i}")) for i in range(Sched.DMA_ROT)] for e in ('sp', 'pool')}
        self.stg = [self.arena[:, (STG_OFF + 4096 * i) // 4:(STG_OFF + 4096 * (i + 1)) // 4] for i in range(2)]
        self.nstg = 0
        self.ncast = 0
        self.lq = []
        self.inflight = []
        self.S = Sched(nc)
        self.apos = 0
        self.rr = 0

    def alloc(self, shape, dt):
        n = int(np.prod(shape)) * _dsize(dt)
        n = (n + 63) // 64 * 64
        o = self.apos
        assert o + n <= STG_OFF, (o, n)
        self.apos = o + n
        v = self.arena[:, o // 4:(o + n) // 4]
        if dt != F32:
            v = v.bitcast(dt)
        v = v[:, 0:int(np.prod(shape))]
        if len(shape) == 2:
            v = v.rearrange("p (a b) -> p a b", a=shape[0])
        elif len(shape) == 3:
            v = v.rearrange("p (a b c) -> p a b c", a=shape[0], b=shape[1])
        elif len(shape) == 4:
            v = v.rearrange("p (a b c d) -> p a b c d", a=shape[0], b=shape[1], c=shape[2])
        return v

    def mm(self, out, lhsT, rhs, start=True, stop=True, tp=None):
        kw = {} if tp is None else dict(tile_position=tp)
        return self.S.op('pe', lambda e: e.matmul(out, lhsT=lhsT, rhs=rhs, start=start, stop=stop, **kw),
                         reads=[lhsT, rhs], writes=[out])

    def tr(self, out, in_, ident):
        return self.S.op('pe', lambda e: e.transpose(out, in_, ident), reads=[in_, ident], writes=[out])

    def act(self, out, in_, func, bias=None, scale=None):
        kw = {}
        rd = [in_]
        if bias is not None:
            kw['bias'] = bias
            if not isinstance(bias, float):
                rd.append(bias)
        if scale is not None:
            kw['scale'] = scale
            if not isinstance(scale, float):
                rd.append(scale)
        return self.S.op('act', lambda e: e.activation(out=out, in_=in_, func=func, **kw), reads=rd, writes=[out])

    def tt(self, eng, out, in0, in1, op):
        return self.S.op(eng, lambda e: e.tensor_tensor(out=out, in0=in0, in1=in1, op=op), reads=[in0, in1], writes=[out])

    def stt(self, eng, out, in0, scalar, in1, op0, op1):
        rd = [in0, in1] + ([] if isinstance(scalar, float) else [scalar])
        return self.S.op(eng, lambda e: e.scalar_tensor_tensor(out=out, in0=in0, scalar=scalar, in1=in1, op0=op0, op1=op1),
                         reads=rd, writes=[out])

    def cp(self, eng, out, in_):
        if eng == 'act':
            return self.act(out, in_, AF.Copy)
        return self.S.op(eng, lambda e: e.tensor_copy(out=out, in_=in_), reads=[in_], writes=[out])

    def memset(self, eng, out, val):
        return self.S.op(eng, lambda e: e.memset(out, val), writes=[out])

    def recip(self, out, in_):
        return self.S.op('dve', lambda e: e.reciprocal(out=out, in_=in_), reads=[in_], writes=[out])

    def dma(self, eng, out, in_):
        return self.S.op(eng, lambda e: e.dma_start(out=out, in_=in_), reads=[in_], writes=[out], dma=True)

    def pcol(self, name, idx):
        o = POFF[name] + idx
        return self.par[:, o:o + 1]

    def alt(self):
        self.rr += 1
        return 'dve' if self.rr % 2 else 'pool'

    def load_consts(self):
        self.dma('sp', self.par[:], self.params[:, :])
        self.dma('sp', self.cm[:], self.cmats.rearrange("m p n -> p m n"))
        self.cp('dve', self.cmb[:], self.cm[:, 0:4, :])
        self.dma('sp', self.stg[0][:, 0:256], self.band[:, :])
        self.cp('dve', self.bandm[:], self.stg[0][:, 0:256])
        self.ident = self.cm[:, 0, :]
        self.ones_f = self.cm[:, 1, :]
        self.perm_f = self.cm[:, 3, :]
        self.sel_f = self.cm[:, 4, :]
        self.ident_b = self.cmb[:, 0, :]
        self.ones_b = self.cmb[:, 1, :]
        self.bones_b = self.cmb[:, 2, :]
        self.perm_b = self.cmb[:, 3, :]

    def load_x(self):
        self.apos = 0
        xt = [self.alloc([D], F32) for _ in range(2)]
        for t in range(16):
            b = xt[t % 2]
            self.dma('sp', b, self.x[t * 128:(t + 1) * 128, :])
            for half in range(2):
                p = self.ps[(2 * t + half) % 4]
                for j in range(4):
                    c = 4 * half + j
                    self.tr(p[:, j * 128:(j + 1) * 128], b[:, c * 128:(c + 1) * 128], self.ident)
                self.cp('act' if half else 'dve', self.xT[:, 4 * half:4 * half + 4, t * 128:(t + 1) * 128],
                        p[:, :].rearrange("p (j n) -> p j n", j=4))

    def store_x(self):
        self.apos = 0
        ot = [self.alloc([D], F32) for _ in range(2)]
        last = []
        for t in range(16):
            b = ot[t % 2]
            for half in range(2):
                p = self.ps[(2 * t + half) % 4]
                for j in range(4):
                    c = 4 * half + j
                    self.tr(p[:, j * 128:(j + 1) * 128], self.xT[:, c, t * 128:(t + 1) * 128], self.ident)
                self.cp('act' if half else 'dve', b[:, 512 * half:512 * half + 512], p[:, :])
            last.append(self.dma('sp', self.out[t * 128:(t + 1) * 128, :], b))
        return last

    def norm(self, l, which, work):
        sq, rstd = work
        for blk in range(4):
            bs = slice(blk * 512, (blk + 1) * 512)
            pn = self.ps[blk % 2]
            for c in range(8):
                s = sq[c % 2]
                self.act(s, self.xT[:, c, bs], AF.Square)
                self.mm(pn[:, :], self.ones_b, s, start=(c == 0), stop=(c == 7))
            r = rstd[blk % 2]
            self.act(r, pn[:, :], AF.Ln, bias=self.pcol('eps', 0), scale=1.0 / D)
            self.act(r, r, AF.Exp, scale=-0.5)
            for c in range(8):
                self.stt('dve', self.hT[:, c, bs], self.xT[:, c, bs],
                         self.pcol('normg', (l * 2 + which) * 8 + c), r, ALU.mult, ALU.mult)

    def slab_load(self, dst, src, ceng=None):
        d2 = dst.rearrange("p k n -> p (k n)")
        n = d2.shape[1]
        for o in range(0, n, 1024):
            self.lq.append((d2[:, o:o + 1024], src[:, o:o + 1024], ceng))

    def pump(self):
        for (st, d, ceng) in self.inflight:
            self.ncast += 1
            eng = ceng if ceng is not None else ('act' if self.ncast % 2 else 'dve')
            self.cp(eng, d, st)
        self.inflight = []
        while self.lq and len(self.inflight) < 2:
            d, src, ceng = self.lq.pop(0)
            st = self.stg[self.nstg % 2]
            self.nstg += 1
            self.dma('sp', st, src)
            self.inflight.append((st, d, ceng))

    def drain(self):
        while self.lq or self.inflight:
            self.pump()

    def prep_qk(self, l, specs, tabs, work, slabs, gidx, vnext=None):
        w_in = self.w_in
        units = []
        for ci, (dst, cols, qk) in enumerate(specs):
            for blk in range(4):
                units.append((ci, dst, cols, qk, blk))

        def stage1(u):
            ci, dst, cols, qk, blk = units[u]
            sl = slabs[ci % 2]
            if blk == 0:
                if ci == 0:
                    self.slab_load(sl, w_in[l, cols])
                self.drain()
                if ci + 1 < len(specs):
                    self.slab_load(slabs[(ci + 1) % 2], w_in[l, specs[ci + 1][1]])
                elif vnext is not None:
                    self.slab_load(vnext[0], w_in[l, vnext[1]])
            self.pump()
            gain = self.pcol('qkg', l * 6 + gidx * 2 + qk)
            sqb, rstdb, qnb, t1b, t2b = work[u % 2]
            bs = slice(blk * 512, (blk + 1) * 512)
            qp = self.ps[u % 2]
            for kc in range(8):
                self.mm(qp[:, :], sl[:, kc, :], self.hT[:, kc, bs], start=(kc == 0), stop=(kc == 7))
            self.act(sqb, qp[:, :], AF.Square)
            sp_ = self.ps[2 + u % 2]
            self.mm(sp_[:, :], self.bones_b, sqb)
            self.act(rstdb, sp_[:, :], AF.Ln, bias=self.pcol('eps', 0), scale=1.0 / 64)
            self.act(rstdb, rstdb, AF.Exp, scale=-0.5)
            if tabs is None:
                self.stt('dve', dst[:, bs], qp[:, :], gain, rstdb, ALU.mult, ALU.mult)
            else:
                cos, sin = tabs
                self.stt('dve', qnb, qp[:, :], gain, rstdb, ALU.mult, ALU.mult)
                self.tt('dve', t1b, qnb, cos[:, bs], ALU.mult)
                self.tt('dve', t2b, qnb, sin[:, bs], ALU.mult)

        def stage2(u):
            ci, dst, cols, qk, blk = units[u]
            if tabs is None:
                return
            sqb, rstdb, qnb, t1b, t2b = work[u % 2]
            bs = slice(blk * 512, (blk + 1) * 512)
            rp = self.ps[4 + u % 2]
            self.mm(rp[:, :], self.ident_b, t1b, start=True, stop=False)
            self.mm(rp[:, :], self.perm_b, t2b, start=False, stop=True)
            self.cp('act', dst[:, bs], rp[:, :])

        stage1(0)
        for u in range(len(units)):
            if u + 1 < len(units):
                stage1(u + 1)
            stage2(u)

    def calc_vt(self, slab, VT):
        self.drain()
        for blk in range(4):
            bs = slice(blk * 512, (blk + 1) * 512)
            vp = self.ps[4 + blk % 2]
            for kc in range(8):
                self.mm(vp[:, :], slab[:, kc, :], self.hT[:, kc, bs], start=(kc == 0), stop=(kc == 7))
            self.cp('dve' if blk % 2 else 'act', VT[:, bs], vp[:, :])

    def v_tiles(self, VT, vdst4_fn, tok_fn, nkt=16, split=None):
        pbf = self.pp[3].bitcast(BF16)
        for k4 in range(nkt // 4):
            pb = pbf[:, (k4 % 2) * 1024:(k4 % 2) * 1024 + 512]
            for j in range(4):
                self.tr(pb[:, j * 128:(j + 1) * 128], VT[:, tok_fn(4 * k4 + j)], self.ident_b)
            src = pb.rearrange("p (j n) -> p j n", j=4) if split is None else \
                pb.rearrange("p (j g d) -> p j g d", j=4, g=split)
            self.cp('dve' if k4 % 2 else 'act', vdst4_fn(k4), src)

    def finalize(self, o_src_num, o_src_den, dst, osb_den_row, nq):
        rec = osb_den_row
        self.act(rec, o_src_den, AF.Ln)
        self.act(rec, rec, AF.Exp, scale=-1.0)
        bp = self.ps[7]
        self.mm(bp[0:64, 0:nq], self.ones_f[64:65, 0:64], rec)
        self.tt('dve', dst, o_src_num, bp[0:64, 0:nq], ALU.mult)

    def branch_b(self, l, oT):
        VB = 128
        self.apos = 32768
        V = self.alloc([16, 2, VB], BF16)
        VT = self.alloc([SEQ], BF16)
        cos = self.alloc([SEQ], F32)
        sin = self.alloc([SEQ], F32)
        self.dma('sp', cos, self.rope[2])
        self.dma('sp', sin, self.rope[3])
        base = self.apos
        for g in range(2):
            self.apos = base
            qT = self.alloc([2, SEQ], BF16)
            kT = self.alloc([SEQ], BF16)
            mark = self.apos
            work = [(self.alloc([512], BF16), self.alloc([512], F32), self.alloc([512], F32), self.alloc([512], BF16),
                     self.alloc([512], BF16)) for _ in range(2)]
            slabs = [self.alloc([8, 128], BF16) for _ in range(2)]
            specs = [(qT[:, 0, :], 6 + 2 * g, 0),
                     (qT[:, 1, :], 7 + 2 * g, 0),
                     (kT, 42 + g, 1)]
            self.prep_qk(l, specs, (cos, sin), work, slabs, 1, vnext=((slabs[1], 11) if g == 0 else None))
            if g == 0:
                self.calc_vt(slabs[1], VT)
                self.memset('dve', V[:, :, :, 64:128], 1.0)
                self.v_tiles(VT, lambda k4: V[:, 4 * k4:4 * k4 + 4, :, 0:64],
                             lambda kt: slice(kt * 128, (kt + 1) * 128), split=2)
            self.apos = mark
            NP = 3
            P = [self.alloc([1024], BF16) for _ in range(NP)]
            rden = [[self.alloc([512], F32) for _ in range(2)] for _ in range(2)]
            rdlo = [[self.alloc([512], F32) for _ in range(2)] for _ in range(2)]
            tiles = [(hp2, qc, kt) for hp2 in range(2) for qc in range(4) for kt in range(16)]

            def s_mm(i):
                hp2, qc, kt = tiles[i]
                sp_ = self.pp[i % 2]
                for e in range(2):
                    self.mm(sp_[:, 512 * e:512 * e + 512], kT[64 * e:64 * e + 64, kt * 128:(kt + 1) * 128],
                            qT[64 * e:64 * e + 64, hp2, qc * 512:(qc + 1) * 512])

            pending = None
            s_mm(0)
            for i, (hp2, qc, kt) in enumerate(tiles):
                if i + 1 < len(tiles):
                    s_mm(i + 1)
                Pt = P[i % NP]
                self.act(Pt, self.pp[i % 2][:, :], AF.Exp, scale=0.125)
                j = hp2 * 4 + qc
                ob = self.pp[2 + j % 2]
                for e in range(2):
                    self.mm(ob[:, 512 * e:512 * e + 512], V[:, kt, g, :], Pt[:, 512 * e:512 * e + 512],
                            start=(kt == 0), stop=(kt == 15))
                if pending is not None and i - pending[0] >= 3:
                    pending[1]()
                    pending = None
                if kt == 15:
                    def fin(ob=ob, k=j % 2, hp2=hp2, qc=qc):
                        for e in range(2):
                            r_ = rden[k][e]
                            self.recip(r_[64:128, :], ob[64:128, 512 * e:512 * e + 512])
                            self.cp('dve', rdlo[k][e][0:64, :], r_[64:128, :])
                            self.tt('dve', oT[64 * e:64 * e + 64, 2 + 2 * g + hp2, qc * 512:(qc + 1) * 512],
                                    rdlo[k][e][0:64, :], ob[0:64, 512 * e:512 * e + 512], ALU.mult)
                    pending = (i, fin)
            if pending is not None:
                pending[1]()

    def branch_a(self, l, oT):
        self.apos = 32768
        cos = self.alloc([SEQ], F32)
        sin = self.alloc([SEQ], F32)
        self.dma('sp', cos, self.rope[0])
        self.dma('sp', sin, self.rope[1])
        base = self.apos
        for hp in range(2):
            self.apos = base
            qT = self.alloc([SEQ], BF16)
            kT = self.alloc([SEQ], BF16)
            V = self.alloc([3, 16, 2, VW], BF16)
            VT = self.alloc([SEQ], BF16)
            mark = self.apos
            slabs = [self.alloc([8, 128], BF16) for _ in range(2)]
            work = [(self.alloc([512], BF16), self.alloc([512], F32), self.alloc([512], F32), self.alloc([512], BF16),
                     self.alloc([512], BF16)) for _ in range(2)]
            specs = [(qT, hp, 0), (kT, 2 + hp, 1)]
            self.prep_qk(l, specs, (cos, sin), work, slabs, 0, vnext=(slabs[0], 4 + hp))
            self.calc_vt(slabs[0], VT)
            pats = [(1, 64), (4, 64), (16, 64)]
            for p, (dil, rad) in enumerate(pats):
                nts = 16 // dil

                def tok(kt, dil=dil, nts=nts):
                    r, j = kt // nts, kt % nts
                    s0 = r + dil * 128 * j
                    return slice(s0, s0 + dil * 127 + 1, dil)

                self.v_tiles(VT, lambda k4, p=p: V[:, p, 4 * k4:4 * k4 + 4, :, :].rearrange("p k e d -> p k (e d)"), tok)
            self.apos = mark
            oacc = self.alloc([SEQ], F32)
            dacc = self.alloc([SEQ], F32)
            NP = 3
            P = [self.alloc([2, 256], BF16) for _ in range(NP)]
            self.memset('dve', oacc, 0.0)
            self.memset('dve', dacc[0:33, :], 1.0)
            self.memset('dve', dacc[0:1, :], 0.0)
            self.memset('dve', dacc[32:33, :], 0.0)
            for k in range(2):
                self.memset('dve', self.pp[2 + k][0:33, 512:1024], 0.0)
            tl = []
            for p, (dil, rad) in enumerate(pats):
                Lp = SEQ // dil
                nts = 16 // dil
                for kt in range(16):
                    r, j = kt // nts, kt % nts
                    ql0 = max(0, 128 * j - 64)
                    ql1 = min(Lp, 128 * j + 192)
                    nq = ql1 - ql0
                    mo = ql0 - (128 * j - 64)
                    ks0 = r + dil * 128 * j
                    ksl = slice(ks0, ks0 + dil * 127 + 1, dil)
                    qs0 = r + dil * ql0
                    qsl = slice(qs0, qs0 + dil * (nq - 1) + 1, dil)
                    tl.append((p, kt, nq, mo, ksl, qsl))

            def s_stage(i):
                p, kt, nq, mo, ksl, qsl = tl[i]
                sb = self.pp[i % 2]
                for e in range(2):
                    self.mm(sb[:, 512 * e:512 * e + nq], kT[64 * e:64 * e + 64, ksl], qT[64 * e:64 * e + 64, qsl],
                            start=True, stop=False)
                for e in range(2):
                    self.mm(sb[:, 512 * e:512 * e + nq], self.ident_b, self.bandm[:, mo:mo + nq], start=False, stop=True)

            s_stage(0)
            for i, (p, kt, nq, mo, ksl, qsl) in enumerate(tl):
                if i + 1 < len(tl):
                    s_stage(i + 1)
                sb = self.pp[i % 2]
                Pt = P[i % NP]
                self.act(Pt[:, :, 0:nq], sb[:, :].rearrange("p (e n) -> p e n", e=2)[:, :, 0:nq], AF.Exp, scale=0.125)
                ob = self.pp[2 + i % 2]
                for e in range(2):
                    self.mm(ob[64 * e:64 * e + 64, 0:nq], V[:, p, kt, e, 0:64], Pt[:, e, 0:nq], tp=(0, 64 * e))
                for e in range(2):
                    self.mm(ob[32 * e:32 * e + 1, 512:512 + nq], self.ones_b[:, 0:1], Pt[:, e, 0:nq], tp=(0, 32 * e))
                self.tt('dve', oacc[:, qsl], oacc[:, qsl], ob[:, 0:nq], ALU.add)
                self.tt('dve', dacc[0:33, qsl], dacc[0:33, qsl], ob[0:33, 512:512 + nq], ALU.add)
            self.act(dacc[0:33, :], dacc[0:33, :], AF.Ln)
            self.act(dacc[0:33, :], dacc[0:33, :], AF.Exp, scale=-1.0)
            for blk in range(4):
                bs = slice(blk * 512, (blk + 1) * 512)
                bp = self.ps[blk % 2]
                self.mm(bp[:, :], self.sel_f[0:33, :], dacc[0:33, bs])
                self.tt('dve', oT[:, hp, bs], oacc[:, bs], bp[:, :], ALU.mult)

    def branch_c(self, l, oT):
        GW = (NU_INT + NU_FULL) * 64
        for hp in range(2):
            self.apos = 32768
            qT = self.alloc([SEQ], BF16)
            kT = self.alloc([SEQ], BF16)
            V = self.alloc([16, 2, VW], BF16)
            slabs = [self.alloc([8, 128], BF16) for _ in range(2)]
            G = [self.alloc([GW], F32) for _ in range(2)]
            work = [(self.alloc([512], BF16), self.alloc([512], F32), None, None, None) for _ in range(2)]
            NP = 3
            P = [self.alloc([2, 512], BF16) for _ in range(NP)]
            mark_s = self.apos
            VT = self.alloc([SEQ], BF16)
            self.apos = mark_s
            sbf = [self.alloc([2, 512], F32) for _ in range(2)]
            osb = [self.alloc([512], F32) for _ in range(2)]
            rec = [self.alloc([512], F32) for _ in range(2)]
            for e in range(2):
                self.dma('sp', G[e], self.gtab[l, 2 * hp + e])
            specs = [(qT, 12 + hp, 0), (kT, 14 + hp, 1)]
            self.prep_qk(l, specs, None, work, slabs, 2, vnext=(slabs[0], 16 + hp))
            self.calc_vt(slabs[0], VT)
            self.v_tiles(VT, lambda k4: V[:, 4 * k4:4 * k4 + 4, :, :].rearrange("p k e d -> p k (e d)"),
                         lambda kt: slice(kt * 128, (kt + 1) * 128))
            chunks = []
            chunks.append((0, 256, [(j, NU_INT * 64 + (6 - 2 * j) * 64) for j in range(4)]))
            for ii in range(3):
                R0 = 4 + 8 * ii
                chunks.append((64 * R0, 512, [(j, (10 - (2 * j - R0)) * 64) for j in range(4 * ii, 4 * ii + 8)]))
            chunks.append((64 * 28, 256, [(12 + jj, NU_INT * 64 + (10 - 2 * jj) * 64) for jj in range(4)]))
            for k in range(2):
                self.memset('dve', self.pp[2 + k][0:33, 512:1024], 1.0)
            flat = []
            for k, (q0, nq, tl) in enumerate(chunks):
                for ti, (j, goff) in enumerate(tl):
                    flat.append((k, q0, nq, ti, len(tl), j, goff))

            def s_stage(i):
                k, q0, nq, ti, nt, j, goff = flat[i]
                sb = self.pp[i % 2]
                for e in range(2):
                    self.mm(sb[:, 512 * e:512 * e + nq], kT[64 * e:64 * e + 64, j * 128:(j + 1) * 128],
                            qT[64 * e:64 * e + 64, q0:q0 + nq])

            pending = None
            s_stage(0)
            for i, (k, q0, nq, ti, nt, j, goff) in enumerate(flat):
                if i + 1 < len(flat):
                    s_stage(i + 1)
                sb = self.pp[i % 2]
                ob = self.pp[2 + k % 2]
                sf = sbf[i % 2]
                for e in range(2):
                    self.stt('dve', sf[:, e, 0:nq], sb[:, 512 * e:512 * e + nq], 0.125, G[e][:, goff:goff + nq], ALU.mult, ALU.add)
                Pt = P[i % NP]
                self.act(Pt[:, :, 0:nq], sf[:, :, 0:nq], AF.Exp)
                for e in range(2):
                    self.mm(ob[64 * e:64 * e + 64, 0:nq], V[:, j, e, 0:64], Pt[:, e, 0:nq],
                            start=(ti == 0), stop=(ti == nt - 1), tp=(0, 64 * e))
                for e in range(2):
                    self.mm(ob[32 * e:32 * e + 1, 512:512 + nq], self.ones_b[:, 0:1], Pt[:, e, 0:nq],
                            start=(ti == 0), stop=(ti == nt - 1), tp=(0, 32 * e))
                if pending is not None and i - pending[0] >= 2:
                    pending[1]()
                    pending = None
                if ti == nt - 1:
                    def fin(ob=ob, k=k, q0=q0, nq=nq):
                        r_ = rec[k % 2]
                        self.act(r_[0:33, 0:nq], ob[0:33, 512:512 + nq], AF.Ln)
                        self.act(r_[0:33, 0:nq], r_[0:33, 0:nq], AF.Exp, scale=-1.0)
                        self.cp('dve', osb[k % 2][:, 0:nq], ob[:, 0:nq])
                        self.mm(ob[:, 512:512 + nq], self.sel_f[0:33, :], r_[0:33, 0:nq])
                        self.tt('dve', oT[:, 6 + hp, q0:q0 + nq], osb[k % 2][:, 0:nq], ob[:, 512:512 + nq], ALU.mult)
                    pending = (i, fin)
            if pending is not None:
                pending[1]()

    def merge(self, l, oT):
        self.apos = 32768
        mT = self.alloc([8, SEQ], BF16)
        wg = [[self.alloc([8, 128], BF16) for _ in range(3)] for _ in range(2)]
        wb = [self.alloc([8, 128], BF16) for _ in range(2)]
        gsb = [self.alloc([512], F32) for _ in range(3)]
        acc = [self.alloc([512], F32) for _ in range(2)]
        tmp = [self.alloc([512], F32) for _ in range(2)]
        wo = [wg[0][0], wg[0][1]]
        brk = [(0, 2), (2, 6), (6, 8)]

        def load(m):
            for b in range(3):
                self.slab_load(wg[m % 2][b], self.w_in[l, 18 + 8 * b + m])
            self.slab_load(wb[m % 2], self.w_br[l, m])

        load(0)
        self.drain()
        n = 0
        for m in range(8):
            if m + 1 < 8:
                load(m + 1)
            else:
                self.slab_load(wo[0], self.w_out[l, 0])
            for blk in range(4):
                self.pump()
                bs = slice(blk * 512, (blk + 1) * 512)
                for b in range(3):
                    gp = self.ps[n % 3]
                    for kc in range(8):
                        self.mm(gp[:, :], wg[m % 2][b][:, kc, :], self.hT[:, kc, bs], start=(kc == 0), stop=(kc == 7))
                    self.act(gsb[b], gp[:, :], AF.Sigmoid, bias=self.pcol('bgate', l * 24 + b * 8 + m))
                    yp = self.ps[3 + n % 3]
                    k0, k1 = brk[b]
                    for kc in range(k0, k1):
                        self.mm(yp[:, :], wb[m % 2][:, kc, :], oT[:, kc, bs], start=(kc == k0), stop=(kc == k1 - 1))
                    a = acc[(m * 4 + blk) % 2]
                    if b == 0:
                        self.tt('dve', a, gsb[b], yp[:, :], ALU.mult)
                    elif b == 1:
                        self.tt('dve', tmp[0], gsb[b], yp[:, :], ALU.mult)
                        self.tt('dve', a, a, tmp[0], ALU.add)
                    else:
                        self.tt('dve', tmp[1], gsb[b], yp[:, :], ALU.mult)
                        self.tt('dve', mT[:, m, bs], a, tmp[1], ALU.add)
                    n += 1
        self.drain()
        for m in range(8):
            if m + 1 < 8:
                self.slab_load(wo[(m + 1) % 2], self.w_out[l, m + 1])
            for blk in range(4):
                self.pump()
                bs = slice(blk * 512, (blk + 1) * 512)
                op_ = self.ps[6 + (m * 4 + blk) % 2]
                for kc in range(8):
                    self.mm(op_[:, :], wo[m % 2][:, kc, :], mT[:, kc, bs], start=(kc == 0), stop=(kc == 7))
                self.tt('dve', self.xT[:, m, bs], self.xT[:, m, bs], op_[:, :], ALU.add)

    def ffn(self, l):
        self.apos = 0
        sq = [self.alloc([512], BF16) for _ in range(2)]
        rstd = [self.alloc([512], F32) for _ in range(2)]
        self.norm(l, 1, (sq, rstd))
        self.apos = 0
        gT = self.alloc([24, 1024], BF16)
        NU = 1026
        mark_r = self.apos
        rawS = [[self.alloc([NU], F32) for _ in range(2)] for _ in range(2)]
        accS = [[self.alloc([NU], F32) for _ in range(2)] for _ in range(2)]
        wu = [[self.alloc([8, 128], BF16) for _ in range(2)] for _ in range(2)]
        end_ = self.apos
        self.apos = mark_r
        wd = [self.alloc([24, 128], BF16) for _ in range(2)]
        self.apos = end_
        for half in range(2):
            T0 = 1024 * half
            ua = max(0, T0 - 1)
            ub = min(SEQ, T0 + 1025)
            nu = ub - ua
            o0 = T0 - ua
            blocks = [(0, 512), (512, 1024), (1024, nu)] if half == 0 else [(0, 1), (1, 513), (513, nu)]

            def load(fc):
                self.slab_load(wu[fc % 2][0], self.w_up[l, fc])
                self.slab_load(wu[fc % 2][1], self.w_up[l, 24 + fc])

            load(0)
            self.drain()
            for fc in range(24):
                if fc + 1 < 24:
                    load(fc + 1)
                else:
                    self.slab_load(wd[0], self.w_down[l, 0])
                raw, accb = rawS[fc % 2], accS[fc % 2]
                for gv in range(2):
                    self.pump()
                    f = fc + 24 * gv
                    r_, a_ = raw[gv], accb[gv]
                    w0 = self.pcol('convw', (l * 3 + 0) * 48 + f)
                    w2 = self.pcol('convw', (l * 3 + 2) * 48 + f)
                    for bi in range(2):
                        up = self.ps[(gv * 3 + bi) % 6]
                        b0 = 512 * bi
                        for kc in range(8):
                            self.mm(up[:, :], wu[fc % 2][gv][:, kc, :], self.hT[:, kc, T0 + b0:T0 + b0 + 512],
                                    start=(kc == 0), stop=(kc == 7))
                        self.cp('act', r_[:, b0:b0 + 512], up[:, :])
                        self.act(a_[:, b0:b0 + 512], up[:, :], AF.Identity, bias=self.pcol('convb', l * 48 + f),
                                 scale=self.pcol('convw', (l * 3 + 1) * 48 + f))
                    hp_ = self.ps[(gv * 3 + 2) % 6]
                    ht = T0 + 1024 if half == 0 else T0 - 1
                    for kc in range(8):
                        self.mm(hp_[:, 0:1], wu[fc % 2][gv][:, kc, :], self.hT[:, kc, ht:ht + 1], start=(kc == 0), stop=(kc == 7))
                    self.stt('dve', a_[:, 1:1024], r_[:, 0:1023], w0, a_[:, 1:1024], ALU.mult, ALU.add)
                    self.stt('dve', a_[:, 0:1023], r_[:, 1:1024], w2, a_[:, 0:1023], ALU.mult, ALU.add)
                    if half == 0:
                        self.stt('dve', a_[:, 1023:1024], hp_[:, 0:1], w2, a_[:, 1023:1024], ALU.mult, ALU.add)
                    else:
                        self.stt('dve', a_[:, 0:1], hp_[:, 0:1], w0, a_[:, 0:1], ALU.mult, ALU.add)
                self.act(accb[0][:, 0:1024], accb[0][:, 0:1024], AF.Gelu_apprx_tanh)
                self.tt('dve', gT[:, fc, :], accb[0][:, 0:1024], accb[1][:, 0:1024], ALU.mult)
            self.drain()
            for m in range(8):
                if m + 1 < 8:
                    self.slab_load(wd[(m + 1) % 2], self.w_down[l, m + 1])
                for blk in range(2):
                    self.pump()
                    dp = self.ps[6 + (m * 2 + blk) % 2]
                    for fc in range(24):
                        self.mm(dp[:, :], wd[m % 2][:, fc, :], gT[:, fc, blk * 512:(blk + 1) * 512], start=(fc == 0), stop=(fc == 23))
                    ts = slice(T0 + blk * 512, T0 + (blk + 1) * 512)
                    self.tt('dve', self.xT[:, m, ts], self.xT[:, m, ts], dp[:, :], ALU.add)

    def layer(self, l):
        st = self.stages
        self.apos = 32768
        sq = [self.alloc([512], BF16) for _ in range(2)]
        rstd = [self.alloc([512], F32) for _ in range(2)]
        self.norm(l, 0, (sq, rstd))
        self.apos = 0
        oT = self.alloc([8, SEQ], BF16)
        if st is None or 'a' in st:
            self.branch_a(l, oT)
        if st is None or 'b' in st:
            self.branch_b(l, oT)
        if st is None or 'c' in st:
            self.branch_c(l, oT)
        if st is not None and 'dump_o' in st:
            return oT
        if st is None or 'm' in st:
            self.merge(l, oT)
        if st is None or 'f' in st:
            self.ffn(l)
        return None

    def build(self):
        self.load_consts()
        self.load_x()
        for l in range(self.nl):
            self.layer(l)
        last = self.store_x()
        S = self.S
        S.emit(None, self.sems, self.dsems)
        nc = self.nc
        sems, dsems = self.sems, self.dsems
        with nc.Block() as block:
            @block.tensor
            def _(e):
                S.emit_engine('pe', e, sems, dsems)

            @block.scalar
            def _(e):
                S.emit_engine('act', e, sems, dsems)

            @block.vector
            def _(e):
                S.emit_engine('dve', e, sems, dsems)

            @block.gpsimd
            def _(e):
                S.emit_engine('pool', e, sems, dsems)

            @block.sync
            def _(e):
                S.emit_engine('sp', e, sems, dsems)
                S.final_waits('sp', e, sems, dsems, last)
        self.st.close()
        return nc


_CONSTS = None


def _consts():
    global _CONSTS
    if _CONSTS is None:
        _CONSTS = dict(rope=_rope_tables(), cmats=_const_mats(), band=_band_mask())
    return _CONSTS


def _tile_k(w):
    L, K, N = w.shape
    t = np.asarray(w, np.float32).reshape(L, K // 128, 128, N // 128, 128).transpose(0, 3, 2, 1, 4)
    return np.ascontiguousarray(t).reshape(L, N // 128, 128, (K // 128) * 128)


def _tile_w_in(w_in):
    t = _tile_k(w_in)
    L = t.shape[0]
    dups = []
    for g in range(2):
        wk = np.asarray(w_in, np.float32)[:, :, B_K + g * 64:B_K + (g + 1) * 64]
        wk = wk.reshape(L, 8, 128, 64).transpose(0, 2, 1, 3)
        dups.append(np.concatenate([wk, wk], axis=-1).reshape(L, 1, 128, 1024))
    return np.ascontiguousarray(np.concatenate([t] + dups, axis=1))


def _run(nl, x, w_in, w_branch, w_out, w_up, w_down, params, gtab, stages=None):
    prog = Prog(nl, stages)
    nc = prog.build()
    c = _consts()
    f = lambda a: np.ascontiguousarray(a, dtype=np.float32)
    shared = dict(w_in=_tile_w_in(w_in), w_branch=_tile_k(w_branch), w_out=_tile_k(w_out), w_up=_tile_k(w_up),
                  w_down=_tile_k(w_down),
                  params=f(params), rope=c['rope'], cmats=c['cmats'], band=c['band'], gtab=f(gtab))
    nb = x.shape[0]
    in_maps = [dict(shared, x=f(x[b])) for b in range(nb)]
    res = run_bass_kernel_spmd(nc, in_maps, core_ids=list(range(nb)))
    return np.stack([r["out"] for r in res.results], 0)


def kernel(x, w_in, b_gate, qk_gain, rel_pos_bias, w_branch, w_out, norm_mix, norm_ffn, w_up, conv_w, conv_b, w_down):
    x = np.asarray(x, np.float32)
    params = _pack_params(np.asarray(b_gate), np.asarray(qk_gain), np.asarray(norm_mix), np.asarray(norm_ffn),
                          np.asarray(conv_w), np.asarray(conv_b))
    gtab = _bias_tables(np.asarray(rel_pos_bias, np.float32))
    return _run(NL, x, w_in, w_branch, w_out, w_up, w_down, params, gtab).astype(np.float32)
```
